# Optimizing a Trainium2 kernel written in Bass

```python
import jax, jax.numpy as jnp
from jax import lax
import numpy as np

D_MODEL = 1024
BATCH = 8
SEQ = 2048
DEPTH = 2
DEC_BATCH = 128
DEC_SEQ = 8
PAST_LEN = 16384
PAGE_SIZE = 128

N_EVEN = (DEPTH + 1) // 2
N_ODD = DEPTH // 2
MIX_HALF = D_MODEL // 2
A_WIDTH = MIX_HALF
CONV_W = 3
B_KDIM = 128
B_VDIM = 128
B_HEADS = MIX_HALF // B_VDIM
C_KDIM = 128
C_VDIM = 128
C_HEADS = MIX_HALF // C_VDIM
D_HDIM = 64
D_HEADS = MIX_HALF // D_HDIM
D_WIDTH = D_HEADS * D_HDIM
D_W_RANK = 64
D_A_RANK = 64
D_G_RANK = 128
D_FF = ((8 * D_MODEL // 3 + 127) // 128) * 128
N_MEM = 256
X_HEADS = 4
X_HDIM = D_MODEL // X_HEADS

CHUNK = 64
NORM_EPS = 1e-6
RWKV_GN_EPS = 64e-5
ROPE_BASE = 10000.0

EVEN_SIZES = (A_WIDTH, A_WIDTH, A_WIDTH, B_HEADS * B_KDIM, B_HEADS * B_KDIM, B_HEADS * B_VDIM, B_HEADS * B_VDIM)
C_SIZES = (C_HEADS * C_KDIM, C_HEADS * C_KDIM, C_HEADS * C_VDIM, C_HEADS * C_VDIM)
D_SIZES = (D_WIDTH, D_WIDTH, D_WIDTH, D_W_RANK, D_A_RANK, D_G_RANK)
EVEN_IN = sum(EVEN_SIZES)
C_IN = sum(C_SIZES)
D_IN = sum(D_SIZES)
ODD_IN = C_IN + D_IN
EVEN_OUT = A_WIDTH + B_HEADS * B_VDIM
ODD_OUT = C_HEADS * C_VDIM + D_WIDTH

kernel_name = "hybrid_conv_hgrn2_retnet_rwkv7_macaron_step"


def _split(t, sizes):
    idx = np.cumsum(np.array(sizes))[:-1].tolist()
    return jnp.split(t, idx, axis=-1)


def _heads(t, n):
    return t.reshape(t.shape[:-1] + (n, t.shape[-1] // n))


def _merge(t):
    return t.reshape(t.shape[:-2] + (t.shape[-2] * t.shape[-1],))


def rmsnorm(x, g):
    xf = x.astype(jnp.float32)
    y = xf * lax.rsqrt(jnp.mean(xf * xf, axis=-1, keepdims=True) + NORM_EPS)
    return (y * g.astype(jnp.float32)).astype(x.dtype)


def _head_rmsnorm(o, g):
    o = o * lax.rsqrt(jnp.mean(o * o, axis=-1, keepdims=True) + NORM_EPS)
    return _merge(o) * g.astype(jnp.float32)


def swiglu(x, w_gu, w_down):
    gate, up = jnp.split(x @ w_gu, 2, axis=-1)
    return (jax.nn.silu(gate) * up) @ w_down


def _rope(t, pos):
    half = t.shape[-1] // 2
    inv = ROPE_BASE ** (-jnp.arange(half, dtype=jnp.float32) / half)
    ang = pos.astype(jnp.float32)[:, None] * inv[None, :]
    cos = jnp.cos(ang)[None, :, None, :]
    sin = jnp.sin(ang)[None, :, None, :]
    t1, t2 = t[..., :half], t[..., half:]
    return jnp.concatenate([t1 * cos - t2 * sin, t1 * sin + t2 * cos], axis=-1)


def _chunk_len(T):
    return CHUNK if T % CHUNK == 0 else T


def _to_chunks(t, C):
    Bn, T, H, d = t.shape
    return t.reshape(Bn, T // C, C, H, d).transpose(1, 0, 3, 2, 4)


def _from_chunks(o):
    n, Bn, H, C, d = o.shape
    return o.transpose(1, 0, 3, 2, 4).reshape(Bn, n * C, H, d)


def gla_chunked(q, k, v, logf, S0):
    C = _chunk_len(q.shape[1])
    mask = jnp.tril(jnp.ones((C, C), dtype=bool))[:, :, None]

    def step(S, inp):
        qc, kc, vc, gc = inp
        b = jnp.cumsum(gc, axis=2)
        inter = jnp.einsum('bhck,bhkv->bhcv', qc * jnp.exp(b), S)
        diff = b[:, :, :, None, :] - b[:, :, None, :, :]
        dec = jnp.exp(jnp.where(mask, diff, -jnp.inf))
        att = jnp.einsum('bhik,bhjk,bhijk->bhij', qc, kc, dec)
        intra = jnp.einsum('bhij,bhjv->bhiv', att, vc)
        b_last = b[:, :, -1:, :]
        S = jnp.exp(b_last[:, :, 0, :])[..., None] * S + jnp.einsum(
            'bhck,bhcv->bhkv', kc * jnp.exp(b_last - b), vc)
        return S, inter + intra

    S, o = lax.scan(step, S0, (_to_chunks(q, C), _to_chunks(k, C), _to_chunks(v, C), _to_chunks(logf, C)))
    return _from_chunks(o), S


def retention_chunked(q, k, v, S0):
    C = _chunk_len(q.shape[1])
    lg = jnp.log1p(-jnp.exp2(-5.0 - jnp.arange(C_HEADS, dtype=jnp.float32)))[:, None]
    idx = jnp.arange(C, dtype=jnp.float32)
    q_dec = jnp.exp(lg * (idx + 1.0))[None, :, :, None]
    k_dec = jnp.exp(lg * (C - 1.0 - idx))[None, :, :, None]
    rel = idx[:, None] - idx[None, :]
    dmask = jnp.where(rel >= 0, jnp.exp(lg[:, :, None] * jnp.maximum(rel, 0.0)), 0.0)
    chunk_dec = jnp.exp(lg * C)[None, :, :, None]

    def step(S, inp):
        qc, kc, vc = inp
        inter = jnp.einsum('bhck,bhkv->bhcv', qc, S) * q_dec
        att = jnp.einsum('bhik,bhjk->bhij', qc, kc) * dmask
        intra = jnp.einsum('bhij,bhjv->bhiv', att, vc)
        S = chunk_dec * S + jnp.einsum('bhck,bhcv->bhkv', kc * k_dec, vc)
        return S, inter + intra

    S, o = lax.scan(step, S0, (_to_chunks(q, C), _to_chunks(k, C), _to_chunks(v, C)))
    return _from_chunks(o), S


def rwkv7_scan(r, w, k, v, a_vec, b_vec, S0):
    def step(S, inp):
        rt, wt, kt, vt, at, bt = inp
        sa = jnp.einsum('bhvk,bhk->bhv', S, at)
        S = S * wt[:, :, None, :] + sa[..., None] * bt[:, :, None, :] + vt[..., None] * kt[:, :, None, :]
        return S, jnp.einsum('bhvk,bhk->bhv', S, rt)

    xs = tuple(t.transpose(1, 0, 2, 3) for t in (r, w, k, v, a_vec, b_vec))
    S, o = lax.scan(step, S0, xs)
    return o.transpose(1, 0, 2, 3), S


def even_mixer(h, conv_buf, s_hgrn, w_in, w_out, conv_w, lb, gnorm):
    f32 = jnp.float32
    p = (h @ w_in).astype(f32)
    v_a, b_a, c_a, q_b, f_b, i_b, g_b = _split(p, EVEN_SIZES)
    u = c_a * v_a
    T = u.shape[1]
    ext = jnp.concatenate([conv_buf.astype(f32), u], axis=1)
    cw = conv_w.astype(f32)
    conv = sum(ext[:, j:j + T] * cw[j] for j in range(CONV_W))
    y_a = b_a * conv
    new_buf = ext[:, T:]
    lbf = lb.astype(f32)
    f = lbf + (1.0 - lbf) * jax.nn.sigmoid(f_b)
    q = _heads(jax.nn.silu(q_b), B_HEADS)
    k = _heads(1.0 - f, B_HEADS)
    logf = _heads(jnp.log(f), B_HEADS)
    v = _heads(i_b, B_HEADS)
    o, s_new = gla_chunked(q, k, v, logf, s_hgrn.astype(f32))
    y_b = _head_rmsnorm(o, gnorm) * jax.nn.silu(g_b)
    y = jnp.concatenate([y_a, y_b], axis=-1).astype(h.dtype) @ w_out
    return y, new_buf, s_new


def odd_mixer(h, pos, s_ret, s_rwkv, shift_prev, w_in, w_out, ret_gnorm, mu, w0, w2, a0, a2, g2,
              k_k, k_a, r_k, lnx_g, lnx_b):
    f32 = jnp.float32
    p = (h @ w_in).astype(f32)
    pc, pd = p[..., :C_IN], p[..., C_IN:]
    q, k, v, g = _split(pc, C_SIZES)
    q = _rope(_heads(q, C_HEADS), pos)
    k = _rope(_heads(k, C_HEADS), pos) * (C_KDIM ** -0.5)
    o_c, s_ret_new = retention_chunked(q, k, _heads(v, C_HEADS), s_ret.astype(f32))
    y_c = _head_rmsnorm(o_c, ret_gnorm) * jax.nn.silu(g)
    prev = jnp.concatenate([shift_prev.astype(f32)[:, None], pd[:, :-1]], axis=1)
    pm = pd + mu.astype(f32) * (prev - pd)
    r, kd, vd, w_dn, a_dn, g_dn = _split(pm, D_SIZES)
    w_log = -jax.nn.softplus(-(w0.astype(f32) + jnp.tanh(w_dn) @ w2.astype(f32))) - 0.5
    decay = jnp.exp(-jnp.exp(w_log))
    a = jax.nn.sigmoid(a0.astype(f32) + a_dn @ a2.astype(f32))
    gate = jax.nn.sigmoid(g_dn) @ g2.astype(f32)
    kk = _heads(kd * k_k.astype(f32), D_HEADS)
    kk = kk / jnp.maximum(jnp.linalg.norm(kk, axis=-1, keepdims=True), 1e-12)
    kd = kd * (1.0 + (a - 1.0) * k_a.astype(f32))
    r_h, k_h, v_h, a_h = (_heads(t, D_HEADS) for t in (r, kd, vd, a))
    o_d, s_rwkv_new = rwkv7_scan(r_h, _heads(decay, D_HEADS), k_h, v_h, -kk, kk * a_h, s_rwkv.astype(f32))
    mean = jnp.mean(o_d, axis=-1, keepdims=True)
    var = jnp.mean(jnp.square(o_d - mean), axis=-1, keepdims=True)
    on = _merge((o_d - mean) * lax.rsqrt(var + RWKV_GN_EPS)) * lnx_g.astype(f32) + lnx_b.astype(f32)
    bonus = _merge(jnp.sum(r_h * k_h * r_k.astype(f32), axis=-1, keepdims=True) * v_h)
    y_d = (on + bonus) * gate
    y = jnp.concatenate([y_c, y_d], axis=-1).astype(h.dtype) @ w_out
    return y, s_ret_new, s_rwkv_new, pd[:, -1]


def mem_kv(mem, mem_g, w_kv):
    k, v = jnp.split(rmsnorm(mem, mem_g) @ w_kv, 2, axis=-1)
    return _heads(k, X_HEADS), _heads(v, X_HEADS)


def cross_attend(h, mk, mv, wq, wo):
    q = _heads(h @ wq, X_HEADS)
    s = jnp.einsum('bthd,bmhd->bhtm', q, mk.astype(q.dtype)).astype(jnp.float32) * (X_HDIM ** -0.5)
    pr = jax.nn.softmax(s, axis=-1).astype(h.dtype)
    o = jnp.einsum('bhtm,bmhd->bthd', pr, mv.astype(h.dtype))
    return _merge(o) @ wo


def trunk(x, pos0, conv_buf, s_hgrn, s_ret, s_rwkv, s_shift, mem_k, mem_v, prm):
    T = x.shape[1]
    pos = pos0 + jnp.arange(T, dtype=jnp.int32)
    lb_all = jnp.cumsum(jax.nn.softmax(prm['hgrn_lb'].astype(jnp.float32), axis=0), axis=0)
    n_conv, n_hgrn, n_ret, n_rwkv, n_shift = [], [], [], [], []
    for l in range(DEPTH):
        x = x + 0.5 * swiglu(rmsnorm(x, prm['ffn1_norm'][l]), prm['ffn1_w_gu'][l], prm['ffn1_w_down'][l])
        h = rmsnorm(x, prm['mix_norm'][l])
        j = l // 2
        if l % 2 == 0:
            y, cb, sh = even_mixer(h, conv_buf[j], s_hgrn[j], prm['even_w_in'][j], prm['even_w_out'][j],
                                   prm['conv_w'][j], lb_all[j], prm['hgrn_gnorm'][j])
            n_conv.append(cb.astype(x.dtype))
            n_hgrn.append(sh.astype(x.dtype))
        else:
            y, sr, sw, ss = odd_mixer(h, pos, s_ret[j], s_rwkv[j], s_shift[j], prm['odd_w_in'][j],
                                      prm['odd_w_out'][j], prm['ret_gnorm'][j], prm['rwkv_mu'][j],
                                      prm['rwkv_w0'][j], prm['rwkv_w2'][j], prm['rwkv_a0'][j],
                                      prm['rwkv_a2'][j], prm['rwkv_g2'][j], prm['rwkv_k_k'][j],
                                      prm['rwkv_k_a'][j], prm['rwkv_r_k'][j], prm['rwkv_lnx_g'][j],
                                      prm['rwkv_lnx_b'][j])
            n_ret.append(sr.astype(x.dtype))
            n_rwkv.append(sw.astype(x.dtype))
            n_shift.append(ss.astype(x.dtype))
        x = x + y.astype(x.dtype)
        x = x + cross_attend(rmsnorm(x, prm['xattn_norm'][l]), mem_k[l], mem_v[l],
                             prm['xattn_wq'][l], prm['xattn_wo'][l])
        x = x + 0.5 * swiglu(rmsnorm(x, prm['ffn2_norm'][l]), prm['ffn2_w_gu'][l], prm['ffn2_w_down'][l])
    y = rmsnorm(x, prm['final_norm'])
    return (y, jnp.stack(n_conv), jnp.stack(n_hgrn), jnp.stack(n_ret), jnp.stack(n_rwkv), jnp.stack(n_shift))


def setup_inputs(seed: int = 0) -> dict:
    key = jax.random.key(seed)
    keys = jax.random.split(key, 64)
    counter = iter(range(64))
    f32 = jnp.float32

    def nrm(shape, scale=1.0):
        return jax.random.normal(keys[next(counter)], shape, f32) * scale

    def gain(shape):
        return 1.0 + 0.05 * jax.random.normal(keys[next(counter)], shape, f32)

    def unif(shape, lo, hi):
        return jax.random.uniform(keys[next(counter)], shape, f32, lo, hi)

    d = D_MODEL
    return {
        'x_prompt': nrm((BATCH, SEQ, d)),
        'x_sample': nrm((DEC_BATCH, DEC_SEQ, d)),
        'state_conv': nrm((N_EVEN, DEC_BATCH, CONV_W - 1, A_WIDTH)),
        'state_hgrn': nrm((N_EVEN, DEC_BATCH, B_HEADS, B_KDIM, B_VDIM), 0.5),
        'state_ret': nrm((N_ODD, DEC_BATCH, C_HEADS, C_KDIM, C_VDIM)),
        'state_rwkv': nrm((N_ODD, DEC_BATCH, D_HEADS, D_HDIM, D_HDIM), 0.5),
        'state_shift': nrm((N_ODD, DEC_BATCH, D_IN)),
        'cache_mem_k': nrm((DEPTH, DEC_BATCH, N_MEM, X_HEADS, X_HDIM)),
        'cache_mem_v': nrm((DEPTH, DEC_BATCH, N_MEM, X_HEADS, X_HDIM)),
        'mem_prompt': nrm((BATCH, N_MEM, d)),
        'ffn1_norm': gain((DEPTH, d)),
        'ffn1_w_gu': nrm((DEPTH, d, 2 * D_FF), d ** -0.5),
        'ffn1_w_down': nrm((DEPTH, D_FF, d), D_FF ** -0.5),
        'mix_norm': gain((DEPTH, d)),
        'even_w_in': nrm((N_EVEN, d, EVEN_IN), d ** -0.5),
        'even_w_out': nrm((N_EVEN, EVEN_OUT, d), EVEN_OUT ** -0.5),
        'conv_w': nrm((N_EVEN, CONV_W, A_WIDTH), CONV_W ** -0.5),
        'hgrn_lb': nrm((N_EVEN + 1, B_HEADS * B_KDIM), 0.5),
        'hgrn_gnorm': gain((N_EVEN, B_HEADS * B_VDIM)),
        'odd_w_in': nrm((N_ODD, d, ODD_IN), d ** -0.5),
        'odd_w_out': nrm((N_ODD, ODD_OUT, d), ODD_OUT ** -0.5),
        'ret_gnorm': gain((N_ODD, C_HEADS * C_VDIM)),
        'rwkv_mu': unif((N_ODD, D_IN), 0.0, 1.0),
        'rwkv_w0': unif((N_ODD, D_WIDTH), -5.0, 0.5),
        'rwkv_w2': nrm((N_ODD, D_W_RANK, D_WIDTH), D_W_RANK ** -0.5),
        'rwkv_a0': nrm((N_ODD, D_WIDTH), 0.1),
        'rwkv_a2': nrm((N_ODD, D_A_RANK, D_WIDTH), D_A_RANK ** -0.5),
        'rwkv_g2': nrm((N_ODD, D_G_RANK, D_WIDTH), D_G_RANK ** -0.5),
        'rwkv_k_k': 0.85 + nrm((N_ODD, D_WIDTH), 0.05),
        'rwkv_k_a': gain((N_ODD, D_WIDTH)),
        'rwkv_r_k': nrm((N_ODD, D_HEADS, D_HDIM), 0.1),
        'rwkv_lnx_g': gain((N_ODD, D_WIDTH)),
        'rwkv_lnx_b': nrm((N_ODD, D_WIDTH), 0.01),
        'xattn_norm': gain((DEPTH, d)),
        'mem_norm': gain((DEPTH, d)),
        'xattn_wq': nrm((DEPTH, d, d), d ** -0.5),
        'xattn_wkv': nrm((DEPTH, d, 2 * d), d ** -0.5),
        'xattn_wo': nrm((DEPTH, d, d), d ** -0.5),
        'ffn2_norm': gain((DEPTH, d)),
        'ffn2_w_gu': nrm((DEPTH, d, 2 * D_FF), d ** -0.5),
        'ffn2_w_down': nrm((DEPTH, D_FF, d), D_FF ** -0.5),
        'final_norm': gain((d,)),
    }


def reference(x_prompt, x_sample, state_conv, state_hgrn, state_ret, state_rwkv, state_shift,
              cache_mem_k, cache_mem_v, mem_prompt, ffn1_norm, ffn1_w_gu, ffn1_w_down, mix_norm,
              even_w_in, even_w_out, conv_w, hgrn_lb, hgrn_gnorm, odd_w_in, odd_w_out, ret_gnorm,
              rwkv_mu, rwkv_w0, rwkv_w2, rwkv_a0, rwkv_a2, rwkv_g2, rwkv_k_k, rwkv_k_a, rwkv_r_k,
              rwkv_lnx_g, rwkv_lnx_b, xattn_norm, mem_norm, xattn_wq, xattn_wkv, xattn_wo,
              ffn2_norm, ffn2_w_gu, ffn2_w_down, final_norm):
    prm = {
        'ffn1_norm': ffn1_norm, 'ffn1_w_gu': ffn1_w_gu, 'ffn1_w_down': ffn1_w_down,
        'mix_norm': mix_norm, 'even_w_in': even_w_in, 'even_w_out': even_w_out, 'conv_w': conv_w,
        'hgrn_lb': hgrn_lb, 'hgrn_gnorm': hgrn_gnorm, 'odd_w_in': odd_w_in, 'odd_w_out': odd_w_out,
        'ret_gnorm': ret_gnorm, 'rwkv_mu': rwkv_mu, 'rwkv_w0': rwkv_w0, 'rwkv_w2': rwkv_w2,
        'rwkv_a0': rwkv_a0, 'rwkv_a2': rwkv_a2, 'rwkv_g2': rwkv_g2, 'rwkv_k_k': rwkv_k_k,
        'rwkv_k_a': rwkv_k_a, 'rwkv_r_k': rwkv_r_k, 'rwkv_lnx_g': rwkv_lnx_g, 'rwkv_lnx_b': rwkv_lnx_b,
        'xattn_norm': xattn_norm, 'xattn_wq': xattn_wq, 'xattn_wo': xattn_wo,
        'ffn2_norm': ffn2_norm, 'ffn2_w_gu': ffn2_w_gu, 'ffn2_w_down': ffn2_w_down,
        'final_norm': final_norm,
    }
    mks, mvs = [], []
    for l in range(DEPTH):
        mk, mv = mem_kv(mem_prompt, mem_norm[l], xattn_wkv[l])
        mks.append(mk)
        mvs.append(mv)
    mem_k_p = jnp.stack(mks)
    mem_v_p = jnp.stack(mvs)
    dt = x_prompt.dtype
    zc = jnp.zeros((N_EVEN, BATCH, CONV_W - 1, A_WIDTH), dt)
    zh = jnp.zeros((N_EVEN, BATCH, B_HEADS, B_KDIM, B_VDIM), dt)
    zr = jnp.zeros((N_ODD, BATCH, C_HEADS, C_KDIM, C_VDIM), dt)
    zw = jnp.zeros((N_ODD, BATCH, D_HEADS, D_HDIM, D_HDIM), dt)
    zs = jnp.zeros((N_ODD, BATCH, D_IN), dt)
    y_prompt, conv_p, hgrn_p, ret_p, rwkv_p, shift_p = trunk(
        x_prompt, 0, zc, zh, zr, zw, zs, mem_k_p, mem_v_p, prm)
    y_sample, conv_s, hgrn_s, ret_s, rwkv_s, shift_s = trunk(
        x_sample, PAST_LEN, state_conv, state_hgrn, state_ret, state_rwkv, state_shift,
        cache_mem_k, cache_mem_v, prm)
    return (y_prompt, y_sample, conv_p, hgrn_p, ret_p, rwkv_p, shift_p, mem_k_p, mem_v_p,
            conv_s, hgrn_s, ret_s, rwkv_s, shift_s)
```

```python
import numpy as np
from contextlib import ExitStack
import concourse.bass as bass
import concourse.mybir as mybir
from concourse.bass_utils import run_bass_kernel_spmd

F32 = mybir.dt.float32
BF16 = mybir.dt.bfloat16
AF = mybir.ActivationFunctionType
ALU = mybir.AluOpType

NCORES = 8
D = 1024
KC = 8
SEQ = 2048
NS = 16
TS = 8
NTOK = SEQ + NS * TS
DFF = 2816
JC = DFF // 128
NMEM = 256
EPS = 1e-6
PAST = 16384
TILES = [(0, 512), (512, 1024), (1024, 1536), (1536, 2048), (2048, 2176)]
NDS = 48
WBUF = 2816
NWB = 4


class Trk:
    __slots__ = ("w", "r", "excl")

    def __init__(self, excl=False):
        self.w = None
        self.r = {}
        self.excl = excl


class Eng:
    def __init__(self, name, e, sem, idx):
        self.name, self.e, self.sem, self.idx = name, e, sem, idx
        self.cnt = 0
        self.waited = {}


class KB:
    def __init__(self, nc, st):
        self.nc = nc
        self.sems = []
        self.E = {}
        for n, e in (("pe", nc.tensor), ("act", nc.scalar), ("dve", nc.vector), ("pool", nc.gpsimd), ("sp", nc.sync)):
            sem = st.enter_context(nc.semaphore("sem_" + n))
            self.E[n] = Eng(n, e, sem, len(self.sems))
            self.sems.append(sem)
        self.dbase = len(self.sems)
        for i in range(NDS):
            self.sems.append(st.enter_context(nc.semaphore("dsem%d" % i)))
        self.dval = [0] * NDS
        self.task = None
        self.hook = None
        self.opcount = 0
        self.dnext = {"hw": 0, "sw": NDS // 2}
        self.nins = 0

    def _deps(self, E, reads, writes):
        evs = []
        for t in reads:
            if t.w is not None:
                evs.append(t.w)
            if t.excl:
                for s, v in t.r.items():
                    if s != E.idx:
                        evs.append((s, v))
        own_ok = (E.name == "pe")
        for t in writes:
            if t.w is not None and not (own_ok and t.w[0] == E.idx):
                evs.append(t.w)
            for s, v in t.r.items():
                if not (own_ok and s == E.idx):
                    evs.append((s, v))
        return evs

    def _wait(self, E, evs):
        need = {}
        for s, v in evs:
            if s == E.idx and v > E.cnt:
                continue
            if need.get(s, 0) < v:
                need[s] = v
        for s, v in need.items():
            if E.waited.get(s, 0) < v:
                E.e.wait_ge(self.sems[s], v)
                E.waited[s] = v

    def _record(self, ev, reads, writes):
        for t in reads:
            if t.r.get(ev[0], 0) < ev[1]:
                t.r[ev[0]] = ev[1]
        for t in writes:
            t.w = ev
            t.r = {}

    def op(self, en, fn, reads=(), writes=(), inc=True):
        E = self.E[en]
        self._wait(E, self._deps(E, reads, writes))
        ins = fn(E.e)
        ev = (E.idx, E.cnt + 1)
        if inc:
            ins.then_inc(E.sem, 1)
            E.cnt += 1
        self._record(ev, reads, writes)
        self.nins += 1
        if self.hook is not None:
            self.opcount += 1
            if self.opcount >= self.quantum:
                self.opcount = 0
                self.hook()
        return ins

    def interleave(self, tasks, quantum=4):
        import threading
        n = len(tasks)
        sems = [threading.Semaphore(0) for _ in range(n)]
        main = threading.Semaphore(0)
        alive = [True] * n
        errs = []

        def nxt(i):
            for d in range(1, n + 1):
                j = (i + d) % n
                if alive[j]:
                    return j
            return None

        def hook():
            i = self.task
            j = nxt(i)
            if j is None or j == i:
                return
            self.task = j
            sems[j].release()
            sems[i].acquire()

        def runner(i):
            sems[i].acquire()
            try:
                tasks[i]()
            except BaseException as ex:
                errs.append(ex)
            alive[i] = False
            j = nxt(i)
            if j is None:
                main.release()
            else:
                self.task = j
                sems[j].release()

        ths = [threading.Thread(target=runner, args=(i,)) for i in range(n)]
        for t in ths:
            t.start()
        self.quantum = quantum
        self.opcount = 0
        self.hook = hook
        self.task = 0
        sems[0].release()
        main.acquire()
        for t in ths:
            t.join()
        self.hook = None
        self.task = None
        if errs:
            raise errs[0]

    def dma(self, qn, out, in_, reads=(), writes=(), **kw):
        E = self.E[qn]
        evs = self._deps(E, reads, writes)
        half = NDS // 2
        if qn == "pool":
            i = self.dnext["sw"]
            self.dnext["sw"] = half + (i + 1 - half) % half
        else:
            i = self.dnext["hw"]
            self.dnext["hw"] = (i + 1) % half
        sidx = self.dbase + i
        if self.dval[i] > 0:
            evs.append((sidx, self.dval[i]))
        self._wait(E, evs)
        ins = E.e.dma_start(out=out, in_=in_, **kw)
        self.dval[i] += 16
        ins.then_inc(self.sems[sidx], 16)
        self._record((sidx, self.dval[i]), reads, writes)
        self.nins += 1
        return ins

    def barrier(self):
        for E in self.E.values():
            evs = [(F.idx, F.cnt) for F in self.E.values() if F is not E and F.cnt > 0]
            evs += [(self.dbase + i, self.dval[i]) for i in range(NDS) if self.dval[i] > 0]
            self._wait(E, evs)

    def finish(self):
        E = self.E["sp"]
        evs = [(F.idx, F.cnt) for F in self.E.values() if F is not E and F.cnt > 0]
        evs += [(self.dbase + i, self.dval[i]) for i in range(NDS) if self.dval[i] > 0]
        self._wait(E, evs)


class Buf:
    def __init__(self, t, excl=False):
        self.t = t
        self.k = Trk(excl)


class Stream:
    def __init__(self, kb, pool, loads, queue="pool"):
        self.kb, self.pool, self.loads, self.queue = kb, pool, loads, queue
        self.issued = 0
        self.slot = {}

    def get(self, i):
        kb = self.kb
        nb = len(self.pool.bufs)
        while self.issued < min(len(self.loads), i + nb - 1):
            k = self.issued
            b = self.pool.next()
            view, src = self.loads[k]
            kb.dma(self.queue, out=view(b.t), in_=src, writes=[b.k])
            self.slot[k] = b
            self.issued += 1
        return self.slot.pop(i)


class Ring:
    def __init__(self, bufs, kb=None):
        self.bufs = bufs
        self.i = 0
        self.kb = kb
        self.ti = [0, 0]

    def next(self):
        if self.kb is not None and self.kb.task is not None:
            t = self.kb.task % 2
            lo, cnt = (0, 2) if t == 0 else (2, len(self.bufs) - 2)
            b = self.bufs[lo + self.ti[t]]
            self.ti[t] = (self.ti[t] + 1) % cnt
            return b
        b = self.bufs[self.i]
        self.i = (self.i + 1) % len(self.bufs)
        return b


ALL_STAGES = ("ffn1_0", "mix_0", "xattn_0", "ffn2_0", "ffn1_1", "mix_1", "xattn_1", "ffn2_1", "final")
GI = {"ffn1": 0, "mix": 2, "xattn": 4, "ffn2": 6, "final": 8, "mem": 9}
NG = 11


CO = {"ident": 0, "causalT": 128, "blkmask": 256, "seqmask": 384, "rmask_p": 400, "rmask_s": 912, "rmask_p32": 1040}
NCONST = 1552
GCH = 32


CO2 = {}
_o = 0
for _nm, _w in (("perm", 128), ("ret_dm_p", 512), ("ret_dm_s", 512), ("ret_qd_p", 512), ("ret_qd_s", 512), ("ret_kd", 8), ("blk1", 128),
                ("strictT", 128), ("strictL", 128), ("bstrictT", 128), ("bstrictL", 128), ("hmask", 2)):
    CO2[_nm] = _o
    _o += _w
NCO = _o


def host_consts_odd():
    m = np.zeros((128, NCO), np.float64)
    i = np.arange(128)
    m[:, CO2["perm"]:CO2["perm"] + 128] = (np.abs(i[:, None] - i[None, :]) == 64)
    same = (i[:, None] // TS == i[None, :] // TS)
    for h in range(4):
        lg = np.log1p(-2.0 ** (-5.0 - h))
        d = (i[None, :] - i[:, None]).astype(np.float64)
        m[:, CO2["ret_dm_p"] + h * 128:CO2["ret_dm_p"] + (h + 1) * 128] = np.where(d >= 0, np.exp(lg * np.maximum(d, 0)), 0.0)
        m[:, CO2["ret_dm_s"] + h * 128:CO2["ret_dm_s"] + (h + 1) * 128] = np.where((d >= 0) & same, np.exp(lg * np.maximum(d, 0)), 0.0)
        m[:, CO2["ret_qd_p"] + h * 128:CO2["ret_qd_p"] + (h + 1) * 128] = np.exp(lg * (i + 1.0))[None, :]
        m[:, CO2["ret_qd_s"] + h * 128:CO2["ret_qd_s"] + (h + 1) * 128] = np.exp(lg * (i % TS + 1.0))[None, :]
        m[:, CO2["ret_kd"] + h] = np.exp(lg * (127.0 - i))
        m[:, CO2["ret_kd"] + 4 + h] = np.exp(lg * (TS - 1.0 - i % TS))
    m[:, CO2["blk1"]:CO2["blk1"] + 128] = (i[:, None] // 64 == i[None, :] // 64)
    m[:, CO2["strictT"]:CO2["strictT"] + 128] = (i[:, None] < i[None, :])
    m[:, CO2["strictL"]:CO2["strictL"] + 128] = (i[:, None] > i[None, :])
    m[:, CO2["bstrictT"]:CO2["bstrictT"] + 128] = (i[:, None] < i[None, :]) & same
    m[:, CO2["bstrictL"]:CO2["bstrictL"] + 128] = (i[:, None] > i[None, :]) & same
    m[:, CO2["hmask"]] = (i < 64)
    m[:, CO2["hmask"] + 1] = (i >= 64)
    pos = np.concatenate([np.arange(SEQ), np.tile(PAST + np.arange(TS), NS)]).astype(np.float32)
    inv = (np.float32(10000.0) ** (-np.arange(64, dtype=np.float32) / np.float32(64))).astype(np.float32)
    ang = (pos[None, :] * inv[:, None]).astype(np.float32).astype(np.float64)
    rope = np.zeros((2, 128, NTOK), np.float32)
    rope[0, :64] = np.cos(ang)
    rope[0, 64:] = np.cos(ang)
    rope[1, :64] = -np.sin(ang)
    rope[1, 64:] = np.sin(ang)
    return {"consts_odd": m.astype(np.float32), "rope": rope}


def host_consts():
    m = np.zeros((128, NCONST), np.float32)
    i = np.arange(128)
    m[:, 0:128] = np.eye(128)
    m[:, 128:256] = (i[:, None] <= i[None, :])
    m[:, 256:384] = (i[:, None] <= i[None, :]) & (i[:, None] // TS == i[None, :] // TS)
    m[:, 384:400] = (i[:, None] // TS == np.arange(NS)[None, :])
    m[:, 400:912] = (np.arange(512) % 128 != 0)[None, :]
    m[:, 912:1040] = (np.arange(128) % TS != 0)[None, :]
    m[:, 1040:1552] = (np.arange(512) % GCH != 0)[None, :]
    return {"consts": m}


def build(stages=ALL_STAGES):
    nc = bass.Bass("TRN2", target_bir_lowering=False)

    declared = []
    need = set(sg.rsplit("_", 1)[0] for sg in stages)
    import os
    kdbg = os.environ.get("KDBG", "")
    if "kv" in kdbg:
        need.add("xattn")

    def din(name, shape, dt=F32, when=None):
        if when is not None and when not in need:
            return None
        declared.append(name)
        return nc.dram_tensor(name, list(shape), dt, kind="ExternalInput").ap()

    def dout(name, shape):
        return nc.dram_tensor(name, list(shape), F32, kind="ExternalOutput").ap()

    x_prompt = din("x_prompt", [SEQ, D])
    x_sample = din("x_sample", [NS * TS, D])
    mem_prompt = din("mem_prompt", [NMEM, D])
    cache_k = din("cache_k", [2, NS, NMEM, D], when="xattn")
    cache_v = din("cache_v", [2, NS, NMEM, D], when="xattn")
    consts_d = din("consts", [128, NCONST])
    gains_d = din("gains", [NG, D])
    w_gu = [[din("ffn%d_w_gu_%d" % (f, l), [D, 2 * DFF], when="ffn%d" % f) for l in range(2)] for f in (1, 2)]
    w_dn = [[din("ffn%d_w_down_%d" % (f, l), [DFF, D], when="ffn%d" % f) for l in range(2)] for f in (1, 2)]
    w_q = [din("wq_%d" % l, [D, D], when="xattn") for l in range(2)]
    w_kv = [din("wkv_%d" % l, [D, 2 * D], when="xattn") for l in range(2)]
    w_o = [din("wo_%d" % l, [D, D], when="xattn") for l in range(2)]
    even_w_in = din("even_w_in", [D, 3584], when="mix")
    even_w_out = din("even_w_out", [D, D], when="mix")
    conv_w = din("conv_w", [3, 512], when="mix")
    hgrn_lb = din("hgrn_lb", [2, 512], when="mix")
    hgrn_gnorm = din("hgrn_gnorm", [512], when="mix")
    state_conv = din("state_conv", [NS, 2, 512], when="mix")
    state_hgrn = din("state_hgrn", [NS, 4, 128, 128], when="mix")
    odd_w_in = din("odd_w_in", [D, 3840], when="mix")
    odd_w_out = din("odd_w_out", [D, D], when="mix")
    consts_odd_d = din("consts_odd", [128, NCO], when="mix")
    rope_d = din("rope", [2, 128, NTOK], when="mix")
    ret_gnorm = din("ret_gnorm", [512], when="mix")
    rwkv_mu = din("rwkv_mu", [1792], when="mix")
    rwkv_w0 = din("rwkv_w0", [512], when="mix")
    rwkv_w2 = din("rwkv_w2", [64, 512], when="mix")
    rwkv_a0 = din("rwkv_a0", [512], when="mix")
    rwkv_a2 = din("rwkv_a2", [64, 512], when="mix")
    rwkv_g2 = din("rwkv_g2", [128, 512], when="mix")
    rwkv_k_k = din("rwkv_k_k", [512], when="mix")
    rwkv_k_a = din("rwkv_k_a", [512], when="mix")
    rwkv_r_k = din("rwkv_r_k", [512], when="mix")
    rwkv_lnx_g = din("rwkv_lnx_g", [512], when="mix")
    rwkv_lnx_b = din("rwkv_lnx_b", [512], when="mix")
    state_ret = din("state_ret", [NS, 4, 128, 128], when="mix")
    state_rwkv = din("state_rwkv", [NS, 8, 64, 64], when="mix")
    state_shift = din("state_shift", [NS, 1792], when="mix")
    ret_p = dout("ret_p", [4, 128, 128])
    rwkv_p = dout("rwkv_p", [8, 64, 64])
    shift_p = dout("shift_p", [1792])
    ret_s = dout("ret_s", [NS, 4, 128, 128])
    rwkv_s = dout("rwkv_s", [NS, 8, 64, 64])
    shift_s = dout("shift_s", [NS, 1792])
    conv_p = dout("conv_p", [2, 512])
    hgrn_p = dout("hgrn_p", [4, 128, 128])
    conv_s = dout("conv_s", [NS, 2, 512])
    hgrn_s = dout("hgrn_s", [NS, 4, 128, 128])
    y_prompt = dout("y_prompt", [SEQ, D])
    y_sample = dout("y_sample", [NS * TS, D])
    mem_k_p = dout("mem_k_p", [2, NMEM, D])
    mem_v_p = dout("mem_v_p", [2, NMEM, D])

    with ExitStack() as st:
        kb = KB(nc, st)

        _uid = [0]

        def sb(stk, name, shape, dt=F32):
            _uid[0] += 1
            return Buf(stk.enter_context(nc.sbuf_tensor("%s_%d" % (name, _uid[0]), list(shape), dt)))

        xT = st.enter_context(nc.sbuf_tensor("xT", [128, KC, NTOK], F32))
        xk = [[Trk() for _ in TILES] for _ in range(KC)]
        cst = sb(st, "consts_sb", [128, NCONST])

        class _V:
            pass
        ident = _V()
        ident.t = cst.t[:, 0:128]
        ident.k = cst.k
        ones_bf = sb(st, "ones_bf", [128, 128], BF16)
        one1_bf = sb(st, "one1_bf", [128, 128], BF16)
        gains = sb(st, "gains_sb", [128, NG, KC])
        epsc = sb(st, "epsc", [128, 4])
        wpool = Ring([sb(st, "wb%d" % i, [128, WBUF], BF16) for i in range(NWB)])
        psum = Ring([Buf(st.enter_context(nc.psum_tensor("ps%d" % i, [128, 512], F32)), excl=True) for i in range(6)], kb=kb)
        psacc = Ring([Buf(st.enter_context(nc.psum_tensor("psa%d" % i, [128, 512], F32)), excl=True) for i in range(2)])
        sqr = Ring([sb(st, "sq%d" % i, [128, 512], BF16) for i in range(2)])
        rstd = sb(st, "rstd", [128, 512])
        kT_p = [None, None]
        v_p = [None, None]
        ones128_bf = sb(st, "ones128_bf", [128, 128], BF16)

        kb.dma("sp", out=cst.t[:], in_=consts_d[:, :], writes=[cst.k])
        for g_ in range(NG):
            kb.dma("sp", out=gains.t[:, g_, :], in_=gains_d[g_].rearrange("(k p) -> p k", p=128), writes=[gains.k],
                   allow_slow_non_contiguous=True)
        kb.op("dve", lambda e: e.memset(ones_bf.t[:], 1.0 / D), writes=[ones_bf.k])
        kb.op("dve", lambda e: e.memset(one1_bf.t[:], 1.0), writes=[one1_bf.k])
        kb.op("dve", lambda e: e.memset(ones128_bf.t[:], 1.0 / 128), writes=[ones128_bf.k])
        kb.op("dve", lambda e: e.memset(epsc.t[:, 0:1], EPS), writes=[epsc.k])

        def tile_of(col):
            for i, (a, b) in enumerate(TILES):
                if a <= col < b:
                    return i
            raise ValueError

        evac_i = [0]

        def evac_copy(out, in_, reads, writes):
            evac_i[0] ^= 1
            if evac_i[0]:
                kb.op("act", lambda e: e.copy(out=out, in_=in_), reads=reads, writes=writes)
            else:
                kb.op("dve", lambda e: e.tensor_copy(out=out, in_=in_), reads=reads, writes=writes)

        def wview(t, k, n):
            return t[:, 0:k * n].rearrange("p (k n) -> p k n", k=k)

        def rmsnorm_g(src, gi, out, out_trk, n):
            ps = psum.next()
            for kc in range(KC):
                a, t = src(kc)
                q = sqr.next()
                kb.op("act", lambda e: e.activation(out=q.t[:, :n], in_=a, func=AF.Square), reads=[t], writes=[q.k])
                kb.op("pe", lambda e: e.matmul(ps.t[:, :n], ones_bf.t[:], q.t[:, :n], start=(kc == 0), stop=(kc == KC - 1)),
                      reads=[q.k, ones_bf.k], writes=[ps.k], inc=True)
            kb.op("act", lambda e: e.activation(out=rstd.t[:, :n], in_=ps.t[:, :n], func=AF.Ln, bias=epsc.t[:, 0:1], scale=1.0),
                  reads=[ps.k, epsc.k], writes=[rstd.k])
            kb.op("act", lambda e: e.activation(out=rstd.t[:, :n], in_=rstd.t[:, :n], func=AF.Exp, scale=-0.5), reads=[rstd.k], writes=[rstd.k])
            for kc in range(KC):
                a, t = src(kc)
                kb.op("dve", lambda e: e.scalar_tensor_tensor(out=out(kc), in0=a, scalar=gains.t[:, gi, kc:kc + 1], in1=rstd.t[:, :n],
                                                              op0=ALU.mult, op1=ALU.mult),
                      reads=[t, gains.k, rstd.k], writes=[out_trk])

        def rmsnorm_x(ti, gi, out_buf):
            c0, c1 = TILES[ti]
            n = c1 - c0
            rmsnorm_g(lambda kc: (xT[:, kc, c0:c1], xk[kc][ti]), gi, lambda kc: out_buf.t[:, kc, :n], out_buf.k, n)

        def proj_fm(s, li, nblk, bw, rhs, rhs_trk, n, consume, kch=KC):
            for b in range(nblk):
                w = s.get(li + b)
                wv = wview(w.t, kch, bw)
                for jj in range(bw // 128):
                    oc = b * (bw // 128) + jj
                    ps = psum.next()
                    for kc in range(kch):
                        kb.op("pe", lambda e: e.matmul(ps.t[:, :n], wv[:, kc, jj * 128:(jj + 1) * 128], rhs(kc),
                                                       start=(kc == 0), stop=(kc == kch - 1)),
                              reads=[w.k, rhs_trk], writes=[ps.k], inc=(kc == kch - 1))
                    consume(oc, ps)
            return li + nblk

        def resid_add(ti, scale):
            c0, c1 = TILES[ti]
            n = c1 - c0

            def f(oc, ps):
                kb.op("dve", lambda e: e.scalar_tensor_tensor(out=xT[:, oc, c0:c1], in0=ps.t[:, :n], scalar=scale,
                                                              in1=xT[:, oc, c0:c1], op0=ALU.mult, op1=ALU.add),
                      reads=[ps.k, xk[oc][ti]], writes=[xk[oc][ti]])
            return f

        def phase_load():
            with ExitStack() as ph:
                xin = Ring([sb(ph, "xin%d" % i, [128, D]) for i in range(2)])
                for blk in range(NTOK // 128):
                    xi = xin.next()
                    src = x_prompt[blk * 128:(blk + 1) * 128, :] if blk < 16 else x_sample[:, :]
                    kb.dma("sp", out=xi.t[:], in_=src, writes=[xi.k])
                    ti = tile_of(blk * 128)
                    for g in range(2):
                        ps = psum.next()
                        for j in range(4):
                            kc = 4 * g + j
                            kb.op("pe", lambda e: e.transpose(ps.t[:, j * 128:(j + 1) * 128], xi.t[:, kc * 128:(kc + 1) * 128], ident.t),
                                  reads=[xi.k, ident.k], writes=[ps.k], inc=(j == 3))
                        evac_copy(xT[:, 4 * g:4 * g + 4, blk * 128:(blk + 1) * 128],
                                  ps.t[:].rearrange("p (j c) -> p j c", j=4), [ps.k], [xk[kc][ti] for kc in range(4 * g, 4 * g + 4)])
                kb.barrier()

        def phase_ffn(f, l):
            gi = GI["ffn1" if f == 0 else "ffn2"] + l
            SUPER = [[0, 1], [2, 3, 4]]
            WMAX = 1152
            with ExitStack() as ph:
                hTs = sb(ph, "hTs", [128, KC, NTOK], BF16)
                actT = sb(ph, "actT", [128, JC, WMAX], BF16)
                stmp = Ring([sb(ph, "stmp%d" % i, [128, 512]) for i in range(2)])
                hk = [Trk() for _ in TILES]
                ak = [Trk() for _ in TILES]
                gu_v = w_gu[f][l].rearrange("(k p) n -> p k n", p=128)
                dn_v = w_dn[f][l].rearrange("(j p) n -> p j n", p=128)
                loads = []
                for st_ in SUPER:
                    for b in range(JC // 2):
                        loads.append((lambda t: wview(t, KC, 256), gu_v[:, :, b * 256:(b + 1) * 256]))
                        loads.append((lambda t: wview(t, KC, 256), gu_v[:, :, DFF + b * 256:DFF + (b + 1) * 256]))
                    for nch in range(KC):
                        loads.append((lambda t: wview(t, JC, 128), dn_v[:, :, nch * 128:(nch + 1) * 128]))
                s = Stream(kb, wpool, loads)
                lic = [0]

                def norms(st_):
                    for ti in st_:
                        c0, c1 = TILES[ti]
                        rmsnorm_g(lambda kc: (xT[:, kc, c0:c1], xk[kc][ti]), gi, lambda kc: hTs.t[:, kc, c0:c1], hk[ti], c1 - c0)

                def main(st_):
                    li = lic[0]
                    base = TILES[st_[0]][0]
                    for b in range(JC // 2):
                        wg = s.get(li)
                        wu = s.get(li + 1)
                        li += 2
                        wgv = wview(wg.t, KC, 256)
                        wuv = wview(wu.t, KC, 256)
                        for jj in range(2):
                            j = 2 * b + jj
                            for ti in st_:
                                c0, c1 = TILES[ti]
                                n = c1 - c0
                                o0 = c0 - base
                                pg = psum.next()
                                pu = psum.next()
                                for kc in range(KC):
                                    kb.op("pe", lambda e: e.matmul(pg.t[:, :n], wgv[:, kc, jj * 128:(jj + 1) * 128], hTs.t[:, kc, c0:c1],
                                                                   start=(kc == 0), stop=(kc == KC - 1)),
                                          reads=[wg.k, hk[ti]], writes=[pg.k], inc=(kc == KC - 1))
                                for kc in range(KC):
                                    kb.op("pe", lambda e: e.matmul(pu.t[:, :n], wuv[:, kc, jj * 128:(jj + 1) * 128], hTs.t[:, kc, c0:c1],
                                                                   start=(kc == 0), stop=(kc == KC - 1)),
                                          reads=[wu.k, hk[ti]], writes=[pu.k], inc=(kc == KC - 1))
                                sm = stmp.next()
                                kb.op("act", lambda e: e.activation(out=sm.t[:, :n], in_=pg.t[:, :n], func=AF.Silu),
                                      reads=[pg.k], writes=[sm.k])
                                kb.op("dve", lambda e: e.tensor_tensor(out=actT.t[:, j, o0:o0 + n], in0=sm.t[:, :n], in1=pu.t[:, :n], op=ALU.mult),
                                      reads=[sm.k, pu.k], writes=[ak[ti]])
                    for nch in range(KC):
                        wd = s.get(li)
                        li += 1
                        wdv = wview(wd.t, JC, 128)
                        for ti in st_:
                            c0, c1 = TILES[ti]
                            n = c1 - c0
                            o0 = c0 - base
                            po = psum.next()
                            for j in range(JC):
                                kb.op("pe", lambda e: e.matmul(po.t[:, :n], wdv[:, j, :], actT.t[:, j, o0:o0 + n], start=(j == 0), stop=(j == JC - 1)),
                                      reads=[wd.k, ak[ti]], writes=[po.k], inc=(j == JC - 1))
                            resid_add(ti, 0.5)(nch, po)
                    lic[0] = li

                norms(SUPER[0])
                for si, st_ in enumerate(SUPER):
                    if si + 1 < len(SUPER):
                        kb.interleave([(lambda nx=SUPER[si + 1]: norms(nx)), (lambda cur=st_: main(cur))], quantum=4)
                    else:
                        main(st_)
                kb.barrier()

        def phase_kvprep(layers=(0, 1)):
            with ExitStack() as ph:
                memin = sb(ph, "memin", [128, 2, D])
                memT = sb(ph, "memT", [128, KC, NMEM])
                memh = sb(ph, "memh", [128, KC, NMEM], BF16)
                kvst = [sb(ph, "kvst%d" % i, [128, 2 * D]) for i in range(2)]
                kb.dma("sp", out=memin.t[:], in_=mem_prompt.rearrange("(c p) d -> p c d", p=128), writes=[memin.k])
                for mc in range(2):
                    for g in range(2):
                        ps = psum.next()
                        for j in range(4):
                            kc = 4 * g + j
                            kb.op("pe", lambda e: e.transpose(ps.t[:, j * 128:(j + 1) * 128], memin.t[:, mc, kc * 128:(kc + 1) * 128], ident.t),
                                  reads=[memin.k, ident.k], writes=[ps.k], inc=(j == 3))
                        evac_copy(memT.t[:, 4 * g:4 * g + 4, mc * 128:(mc + 1) * 128], ps.t[:].rearrange("p (j c) -> p j c", j=4),
                                  [ps.k], [memT.k])
                for l in layers:
                    if "kv1" in kdbg:
                        break
                    rmsnorm_g(lambda kc: (memT.t[:, kc, :], memT.k), GI["mem"] + l, lambda kc: memh.t[:, kc, :], memh.k, NMEM)
                    if "kv2" in kdbg:
                        continue
                    kv_v = w_kv[l].rearrange("(k p) n -> p k n", p=128)
                    loads = [(lambda t: wview(t, KC, 256), kv_v[:, :, b * 256:(b + 1) * 256]) for b in range(8)]
                    s = Stream(kb, wpool, loads)
                    for b in range(8):
                        w = s.get(b)
                        wv = wview(w.t, KC, 256)
                        for mc in range(2):
                            ps = psum.next()
                            for kc in range(KC):
                                kb.op("pe", lambda e: e.matmul(ps.t[:, :256], memh.t[:, kc, mc * 128:(mc + 1) * 128], wv[:, kc, :],
                                                               start=(kc == 0), stop=(kc == KC - 1)),
                                      reads=[w.k, memh.k], writes=[ps.k], inc=(kc == KC - 1))
                            kb.op("act", lambda e: e.copy(out=kvst[mc].t[:, b * 256:(b + 1) * 256], in_=ps.t[:, :256]),
                                  reads=[ps.k], writes=[kvst[mc].k])
                            if b >= 4 and "kvA" not in kdbg:
                                kb.op("dve", lambda e: e.tensor_copy(out=v_p[l].t[:, mc, (b - 4) * 256:(b - 3) * 256],
                                                                     in_=kvst[mc].t[:, b * 256:(b + 1) * 256]),
                                      reads=[kvst[mc].k], writes=[v_p[l].k])
                        if b < 4 and "kvB" not in kdbg:
                            for jj in range(2):
                                nch = 2 * b + jj
                                ps = psum.next()
                                for kc in range(KC):
                                    kb.op("pe", lambda e: e.matmul(ps.t[:, :256], wv[:, kc, jj * 128:(jj + 1) * 128], memh.t[:, kc, :],
                                                                   start=(kc == 0), stop=(kc == KC - 1)),
                                          reads=[w.k, memh.k], writes=[ps.k], inc=(kc == KC - 1))
                                kb.op("dve", lambda e: e.tensor_copy(out=kT_p[l].t[:, nch, :], in_=ps.t[:, :256]),
                                      reads=[ps.k], writes=[kT_p[l].k])
                    if "kv3" in kdbg:
                        continue
                    for mc in range(2):
                        kb.dma("sp", out=mem_k_p[l, mc * 128:(mc + 1) * 128, :], in_=kvst[mc].t[:, 0:D], reads=[kvst[mc].k])
                        kb.dma("sp", out=mem_v_p[l, mc * 128:(mc + 1) * 128, :], in_=kvst[mc].t[:, D:2 * D], reads=[kvst[mc].k])
                kb.barrier()

        def phase_xattn(l):
            gi = GI["xattn"] + l
            with ExitStack() as ph:
                hT = sb(ph, "hT", [128, KC, 512], BF16)
                kT_p[l] = sb(ph, "kT_p", [128, KC, NMEM], BF16)
                v_p[l] = sb(ph, "v_p", [128, 2, D], BF16)
                phase_kvprep((l,))
                qT = sb(ph, "qT", [128, KC, 512], BF16)
                oT = sb(ph, "oT", [128, KC, 512], BF16)
                pT = Ring([sb(ph, "pT%d" % i, [128, 2, 512], BF16) for i in range(2)])
                rinv = Ring([sb(ph, "rinv%d" % i, [128, 512]) for i in range(2)])
                kin = Ring([sb(ph, "kin%d" % i, [128, 2, D]) for i in range(2)])
                vs = Ring([sb(ph, "vs%d" % i, [128, 2, D], BF16) for i in range(2)])
                kTs = Ring([sb(ph, "kTs%d" % i, [128, KC, NMEM], BF16) for i in range(2)])
                pTs = Ring([sb(ph, "pTs%d" % i, [128, 64], BF16) for i in range(2)])
                rinvs = Ring([sb(ph, "rinvs%d" % i, [128, 32]) for i in range(2)])
                q_v = w_q[l].rearrange("(k p) n -> p k n", p=128)
                o_v = w_o[l].rearrange("(k p) n -> p k n", p=128)
                loads = []
                for ti in range(len(TILES)):
                    for b in range(4):
                        loads.append((lambda t: wview(t, KC, 256), q_v[:, :, b * 256:(b + 1) * 256]))
                    for b in range(4):
                        loads.append((lambda t: wview(t, KC, 256), o_v[:, :, b * 256:(b + 1) * 256]))
                s = Stream(kb, wpool, loads)
                li = 0
                sc = float(256 ** -0.5)
                for ti, (c0, c1) in enumerate(TILES):
                    n = c1 - c0
                    rmsnorm_x(ti, gi, hT)

                    def q_evac(oc, ps):
                        evac_copy(qT.t[:, oc, :n], ps.t[:, :n], [ps.k], [qT.k])
                    li = proj_fm(s, li, 4, 256, lambda kc: hT.t[:, kc, :n], hT.k, n, q_evac)
                    import os
                    if c0 >= SEQ and os.environ.get('XSKIP'):
                        kb.op('dve', lambda e: e.memset(oT.t[:, :, :n], 0.0), writes=[oT.k])
                    elif c0 < SEQ:
                        for hd in range(4):
                            p = pT.next()
                            for mc in range(2):
                                ps = psum.next()
                                for dc in range(2):
                                    kb.op("pe", lambda e: e.matmul(ps.t[:, :n], kT_p[l].t[:, 2 * hd + dc, mc * 128:(mc + 1) * 128],
                                                                   qT.t[:, 2 * hd + dc, :n], start=(dc == 0), stop=(dc == 1)),
                                          reads=[kT_p[l].k, qT.k], writes=[ps.k], inc=(dc == 1))
                                kb.op("act", lambda e: e.activation(out=p.t[:, mc, :n], in_=ps.t[:, :n], func=AF.Exp, scale=sc),
                                      reads=[ps.k], writes=[p.k])
                            pss = psum.next()
                            for mc in range(2):
                                kb.op("pe", lambda e: e.matmul(pss.t[:, :n], one1_bf.t[:], p.t[:, mc, :n], start=(mc == 0), stop=(mc == 1)),
                                      reads=[one1_bf.k, p.k], writes=[pss.k], inc=(mc == 1))
                            ri = rinv.next()
                            kb.op("act", lambda e: e.activation(out=ri.t[:, :n], in_=pss.t[:, :n], func=AF.Ln), reads=[pss.k], writes=[ri.k])
                            kb.op("act", lambda e: e.activation(out=ri.t[:, :n], in_=ri.t[:, :n], func=AF.Exp, scale=-1.0), reads=[ri.k], writes=[ri.k])
                            for dc in range(2):
                                pso = psum.next()
                                for mc in range(2):
                                    kb.op("pe", lambda e: e.matmul(pso.t[:, :n], v_p[l].t[:, mc, hd * 256 + dc * 128:hd * 256 + (dc + 1) * 128],
                                                                   p.t[:, mc, :n], start=(mc == 0), stop=(mc == 1)),
                                          reads=[v_p[l].k, p.k], writes=[pso.k], inc=(mc == 1))
                                kb.op("dve", lambda e: e.tensor_tensor(out=oT.t[:, 2 * hd + dc, :n], in0=pso.t[:, :n], in1=ri.t[:, :n], op=ALU.mult),
                                      reads=[pso.k, ri.k], writes=[oT.k])
                    else:
                        for sq_ in range(NS):
                            ki = kin.next()
                            vv = vs.next()
                            kt = kTs.next()
                            kb.dma("sp", out=ki.t[:], in_=cache_k[l, sq_].rearrange("(c p) d -> p c d", p=128), writes=[ki.k])
                            kb.dma("pool", out=vv.t[:], in_=cache_v[l, sq_].rearrange("(c p) d -> p c d", p=128), writes=[vv.k])
                            for mc in range(2):
                                for g in range(2):
                                    ps = psum.next()
                                    for j in range(4):
                                        kc = 4 * g + j
                                        kb.op("pe", lambda e: e.transpose(ps.t[:, j * 128:(j + 1) * 128], ki.t[:, mc, kc * 128:(kc + 1) * 128], ident.t),
                                              reads=[ki.k, ident.k], writes=[ps.k], inc=(j == 3))
                                    evac_copy(kt.t[:, 4 * g:4 * g + 4, mc * 128:(mc + 1) * 128], ps.t[:].rearrange("p (j c) -> p j c", j=4),
                                              [ps.k], [kt.k])
                            cs = slice(sq_ * TS, (sq_ + 1) * TS)
                            ps = psum.next()
                            for hd in range(4):
                                for mc in range(2):
                                    col = (hd * 2 + mc) * TS
                                    for dc in range(2):
                                        last = (hd == 3 and mc == 1 and dc == 1)
                                        kb.op("pe", lambda e: e.matmul(ps.t[:, col:col + TS], kt.t[:, 2 * hd + dc, mc * 128:(mc + 1) * 128],
                                                                       qT.t[:, 2 * hd + dc, cs], start=(dc == 0), stop=(dc == 1)),
                                              reads=[kt.k, qT.k], writes=[ps.k], inc=last)
                            p = pTs.next()
                            kb.op("act", lambda e: e.activation(out=p.t[:, :], in_=ps.t[:, 0:64], func=AF.Exp, scale=sc),
                                  reads=[ps.k], writes=[p.k])
                            pv4 = p.t[:, :].rearrange("p (h m t) -> p h m t", h=4, m=2)
                            pss = psum.next()
                            for mc in range(2):
                                kb.op("pe", lambda e: e.matmul(pss.t[:, 0:32].rearrange("p (h t) -> p h t", h=4), one1_bf.t[:], pv4[:, :, mc, :],
                                                               start=(mc == 0), stop=(mc == 1)),
                                      reads=[one1_bf.k, p.k], writes=[pss.k], inc=(mc == 1))
                            ri = rinvs.next()
                            kb.op("dve", lambda e: e.reciprocal(out=ri.t[:, :], in_=pss.t[:, 0:32]), reads=[pss.k], writes=[ri.k])
                            pso = psum.next()
                            for hd in range(4):
                                for dc in range(2):
                                    col = (hd * 2 + dc) * TS
                                    for mc in range(2):
                                        last = (hd == 3 and dc == 1 and mc == 1)
                                        kb.op("pe", lambda e: e.matmul(pso.t[:, col:col + TS], vv.t[:, mc, hd * 256 + dc * 128:hd * 256 + (dc + 1) * 128],
                                                                       pv4[:, hd, mc, :], start=(mc == 0), stop=(mc == 1)),
                                              reads=[vv.k, p.k], writes=[pso.k], inc=last)
                            for hd in range(4):
                                kb.op("dve", lambda e: e.tensor_tensor(out=oT.t[:, 2 * hd:2 * hd + 2, cs],
                                                                       in0=pso.t[:, hd * 16:(hd + 1) * 16].rearrange("p (c t) -> p c t", c=2),
                                                                       in1=ri.t[:, hd * TS:(hd + 1) * TS].unsqueeze(1).to_broadcast([128, 2, TS]),
                                                                       op=ALU.mult),
                                      reads=[pso.k, ri.k], writes=[oT.k])
                    li = proj_fm(s, li, 4, 256, lambda kc: oT.t[:, kc, :n], oT.k, n, resid_add(ti, 1.0))
                kb.barrier()

        def phase_final(do_norm):
            with ExitStack() as ph:
                yo = Ring([sb(ph, "yo%d" % i, [128, D]) for i in range(2)])
                hF = sb(ph, "hF", [128, KC, 512])
                for ti, (c0, c1) in enumerate(TILES):
                    n = c1 - c0
                    if do_norm:
                        rmsnorm_x(ti, GI["final"], hF)
                    for sbk in range(n // 128):
                        yb = yo.next()
                        for g in range(2):
                            ps = psum.next()
                            for j in range(4):
                                kc = 4 * g + j
                                if do_norm:
                                    src_ap, rd = hF.t[:, kc, sbk * 128:(sbk + 1) * 128], [hF.k, ident.k]
                                else:
                                    src_ap, rd = xT[:, kc, c0 + sbk * 128:c0 + (sbk + 1) * 128], [xk[kc][ti], ident.k]
                                kb.op("pe", lambda e: e.transpose(ps.t[:, j * 128:(j + 1) * 128], src_ap, ident.t),
                                      reads=rd, writes=[ps.k], inc=(j == 3))
                            evac_copy(yb.t[:, g * 512:(g + 1) * 512], ps.t[:], [ps.k], [yb.k])
                        col = c0 + sbk * 128
                        dst = y_prompt[col:col + 128, :] if col < SEQ else y_sample[:, :]
                        kb.dma("sp", out=dst, in_=yb.t[:], reads=[yb.k])
                kb.barrier()

        class HSet:
            def __init__(self, stk, tag, w=512):
                self.qs = sb(stk, "qs" + tag, [128, w])
                self.fk = sb(stk, "fk" + tag, [128, w])
                self.lf = sb(stk, "lf" + tag, [128, w])
                self.vb = sb(stk, "vb" + tag, [128, w])
                self.sg = sb(stk, "sg" + tag, [128, w])

        def head_norm_out(o_ap, o_trk, from_psum, osb, tmpb, gcol, gcol_trk, sgb, y_ap, y_trk, n, ones_t=None):
            ones_t = ones_t or ones128_bf
            q = sqr.next()
            kb.op("act", lambda e: e.activation(out=q.t[:, :n], in_=o_ap, func=AF.Square), reads=[o_trk], writes=[q.k])
            if from_psum:
                kb.op("act", lambda e: e.copy(out=osb.t[:, :n], in_=o_ap), reads=[o_trk], writes=[osb.k])
            psn = psum.next()
            kb.op("pe", lambda e: e.matmul(psn.t[:, :n], ones_t.t[:], q.t[:, :n], start=True, stop=True),
                  reads=[q.k, ones_t.k], writes=[psn.k])
            kb.op("act", lambda e: e.activation(out=rstd.t[:, :n], in_=psn.t[:, :n], func=AF.Ln, bias=epsc.t[:, 0:1], scale=1.0),
                  reads=[psn.k, epsc.k], writes=[rstd.k])
            kb.op("act", lambda e: e.activation(out=rstd.t[:, :n], in_=rstd.t[:, :n], func=AF.Exp, scale=-0.5), reads=[rstd.k], writes=[rstd.k])
            kb.op("dve", lambda e: e.tensor_tensor(out=tmpb.t[:, :n], in0=osb.t[:, :n], in1=rstd.t[:, :n], op=ALU.mult),
                  reads=[osb.k, rstd.k], writes=[tmpb.k])
            kb.op("dve", lambda e: e.scalar_tensor_tensor(out=y_ap, in0=tmpb.t[:, :n], scalar=gcol, in1=sgb.t[:, :n],
                                                          op0=ALU.mult, op1=ALU.mult),
                  reads=[tmpb.k, gcol_trk, sgb.k], writes=[y_trk])

        def phase_mix_even():
            gi = GI["mix"] + 0
            with ExitStack() as ph:
                hT = sb(ph, "hT", [128, KC, 512], BF16)
                cw = sb(ph, "cw", [128, 3, 4])
                lbr = sb(ph, "lbr", [128, 2, 4])
                gn = sb(ph, "gn", [128, 4])
                lb = sb(ph, "lb", [128, 4])
                oml = sb(ph, "oml", [128, 4])
                ubuf = sb(ph, "ubuf", [128, 4, 514])
                ubs = sb(ph, "ubs", [128, 4, NS, TS + 2])
                yT = sb(ph, "yT", [128, KC, 512], BF16)
                sets = [HSet(ph, "0"), HSet(ph, "1")]
                for S_ in sets:
                    S_.qb = sb(ph, "qb", [128, 512], BF16)
                    S_.kb = sb(ph, "kb", [128, 512], BF16)
                    S_.vh = sb(ph, "vh", [128, 512], BF16)
                identb = sb(ph, "identb", [128, 128], BF16)
                kb.op("dve", lambda e: e.tensor_copy(out=identb.t[:], in_=ident.t), reads=[ident.k], writes=[identb.k])
                bb = sb(ph, "bb", [128, 512])
                e1 = sb(ph, "e1", [128, 512])
                e2 = sb(ph, "e2", [128, 512])
                osb = sb(ph, "osb", [128, 512])
                tmpb = sb(ph, "tmpb", [128, 512])
                bm = sb(ph, "bm", [128, 16])
                ebm = sb(ph, "ebm", [128, 16])
                kvtm = Ring([sb(ph, "kvtm%d" % i, [128, 256]) for i in range(2)])
                kvb = Ring([sb(ph, "kvb%d" % i, [128, 256], BF16) for i in range(3)])
                Am = Ring([sb(ph, "Am%d" % i, [128, 128]) for i in range(2)])
                Am32 = Ring([sb(ph, "Am32_%d" % i, [32, 32], BF16) for i in range(4)])
                ebmU = [sb(ph, "ebmU%d" % i, [128, 16]) for i in range(2)]
                elastU = [sb(ph, "elastU%d" % i, [128, 16]) for i in range(2)]
                for _b in Am32.bufs:
                    kb.op("dve", lambda e: e.memset(_b.t[:], 0.0), writes=[_b.k])
                Hh = Ring([sb(ph, "Hh%d" % i, [128, 128], BF16) for i in range(4)])
                Hs = Ring([sb(ph, "Hs%d" % i, [128, 128]) for i in range(2)])
                H = [sb(ph, "Hst%d" % i, [128, 128]) for i in range(4)]
                h0 = sb(ph, "h0", [128, NS, 128])
                vexp = sb(ph, "vexp", [128, NS, 128])
                hn = Ring([sb(ph, "hn%d" % i, [128, 4, 128]) for i in range(2)])
                scin = sb(ph, "scin", [32, 512])
                cst32 = sb(ph, "cst32", [128, 4, 32])
                scout = sb(ph, "scout", [32, 512])

                kb.dma("sp", out=cw.t[:], in_=conv_w.rearrange("j (c p) -> p j c", p=128), writes=[cw.k], allow_slow_non_contiguous=True)
                kb.dma("sp", out=lbr.t[:], in_=hgrn_lb.rearrange("r (c p) -> p r c", p=128), writes=[lbr.k], allow_slow_non_contiguous=True)
                kb.dma("sp", out=gn.t[:], in_=hgrn_gnorm.rearrange("(c p) -> p c", p=128), writes=[gn.k], allow_slow_non_contiguous=True)
                kb.op("dve", lambda e: e.tensor_tensor(out=lb.t[:], in0=lbr.t[:, 0, :], in1=lbr.t[:, 1, :], op=ALU.subtract),
                      reads=[lbr.k], writes=[lb.k])
                kb.op("act", lambda e: e.activation(out=lb.t[:], in_=lb.t[:], func=AF.Sigmoid), reads=[lb.k], writes=[lb.k])
                kb.op("dve", lambda e: e.tensor_scalar(out=oml.t[:], in0=lb.t[:], scalar1=-1.0, scalar2=1.0, op0=ALU.mult, op1=ALU.add),
                      reads=[lb.k], writes=[oml.k])
                kb.op("dve", lambda e: e.memset(ubuf.t[:, :, 0:2], 0.0), writes=[ubuf.k])
                for hd in range(4):
                    kb.op("dve", lambda e: e.memset(H[hd].t[:], 0.0), writes=[H[hd].k])
                kb.dma("sp", out=scin.t[:], in_=state_conv.rearrange("s r c -> (s r) c"), writes=[scin.k])
                for c in range(4):
                    ps = psum.next()
                    kb.op("pe", lambda e: e.transpose(ps.t[:, 0:32], scin.t[0:32, c * 128:(c + 1) * 128], ident.t[0:32, 0:32]),
                          reads=[scin.k, ident.k], writes=[ps.k])
                    kb.op("dve", lambda e: e.tensor_copy(out=ubs.t[:, c, :, 0:2], in_=ps.t[:, 0:32].rearrange("p (s r) -> p s r", r=2)),
                          reads=[ps.k], writes=[ubs.k])

                win_v = even_w_in.rearrange("(k p) n -> p k n", p=128)
                wout_v = even_w_out.rearrange("(k p) n -> p k n", p=128)
                order = [0, 4, 2, 1, 5, 3, 6, 8, 10, 12, 7, 9, 11, 13]
                loads = []
                for ti in range(len(TILES)):
                    for b in order:
                        loads.append((lambda t: wview(t, KC, 256), win_v[:, :, b * 256:(b + 1) * 256]))
                    for b in range(4):
                        loads.append((lambda t: wview(t, KC, 256), wout_v[:, :, b * 256:(b + 1) * 256]))
                s = Stream(kb, wpool, loads)
                li = 0

                def gla_common(hd, S, n, rm_off, mask_off, use_mid):
                    kb.op("dve", lambda e: e.tensor_tensor_scan(out=bb.t[:, :n], data0=cst.t[:, rm_off:rm_off + n], data1=S.lf.t[:, :n],
                                                                initial=0.0, op0=ALU.mult, op1=ALU.add),
                          reads=[cst.k, S.lf.k], writes=[bb.k])
                    if use_mid:
                        nch = n // GCH
                        b3 = bb.t[:, :n].rearrange("p (c t) -> p c t", t=GCH)
                        kb.op("dve", lambda e: e.tensor_copy(out=bm.t[:, 0:nch], in_=b3[:, :, GCH // 2 - 1]), reads=[bb.k], writes=[bm.k])
                        kb.op("dve", lambda e: e.tensor_tensor(out=b3, in0=b3, in1=bm.t[:, 0:nch].unsqueeze(2).to_broadcast([128, nch, GCH]),
                                                               op=ALU.subtract), reads=[bb.k, bm.k], writes=[bb.k])
                        kb.op("act", lambda e: e.activation(out=ebm.t[:, 0:nch], in_=bm.t[:, 0:nch], func=AF.Exp), reads=[bm.k], writes=[ebm.k])
                    kb.op("act", lambda e: e.activation(out=e1.t[:, :n], in_=bb.t[:, :n], func=AF.Exp), reads=[bb.k], writes=[e1.k])
                    kb.op("act", lambda e: e.activation(out=e2.t[:, :n], in_=bb.t[:, :n], func=AF.Exp, scale=-1.0), reads=[bb.k], writes=[e2.k])
                    qo, ko = (S.qb, S.kb) if use_mid else (S.qs, S.fk)
                    kb.op("dve", lambda e: e.tensor_tensor(out=qo.t[:, :n], in0=S.qs.t[:, :n], in1=e1.t[:, :n], op=ALU.mult),
                          reads=[S.qs.k, e1.k], writes=[qo.k])
                    kb.op("dve", lambda e: e.tensor_tensor(out=ko.t[:, :n], in0=S.fk.t[:, :n], in1=e2.t[:, :n], op=ALU.mult),
                          reads=[S.fk.k, e2.k], writes=[ko.k])

                def chunk_tm(S, a):
                    pst = psum.next()
                    kb.op("pe", lambda e: e.transpose(pst.t[:, 0:128], S.fk.t[:, a:a + 128], ident.t), reads=[S.fk.k, ident.k], writes=[pst.k], inc=False)
                    kb.op("pe", lambda e: e.transpose(pst.t[:, 128:256], S.vb.t[:, a:a + 128], ident.t), reads=[S.vb.k, ident.k], writes=[pst.k])
                    kv = kvtm.next()
                    evac_copy(kv.t[:, :], pst.t[:, 0:256], [pst.k], [kv.k])
                    return kv

                def chunk_att(S, a, kv, mask_off):
                    psA = psum.next()
                    kb.op("pe", lambda e: e.matmul(psA.t[:, 0:128], S.fk.t[:, a:a + 128], S.qs.t[:, a:a + 128], start=True, stop=True),
                          reads=[S.fk.k, S.qs.k], writes=[psA.k])
                    am = Am.next()
                    kb.op("dve", lambda e: e.tensor_tensor(out=am.t[:], in0=psA.t[:, 0:128], in1=cst.t[:, mask_off:mask_off + 128], op=ALU.mult),
                          reads=[psA.k, cst.k], writes=[am.k])
                    return am

                def gla_prompt_pair(hds, Ss, n):
                    nch = n // GCH
                    psOs = []
                    for u in range(2):
                        gla_common(hds[u], Ss[u], n, CO["rmask_p32"], CO["causalT"], True)
                        kb.op("dve", lambda e: e.tensor_copy(out=ebmU[u].t[:, 0:nch], in_=ebm.t[:, 0:nch]), reads=[ebm.k], writes=[ebmU[u].k])
                        kb.op("dve", lambda e: e.tensor_copy(out=elastU[u].t[:, 0:nch], in_=e1.t[:, :n].rearrange("p (c t) -> p c t", t=GCH)[:, :, GCH - 1]),
                              reads=[e1.k], writes=[elastU[u].k])
                        psOs.append(psacc.next())
                    cmask = cst.t[0:GCH, CO["causalT"]:CO["causalT"] + GCH].bitcast(mybir.dt.uint32)
                    items = [(ch, u) for ch in range(nch) for u in range(2)]
                    st = {}

                    def stage_a(it):
                        ch, u = it
                        S = Ss[u]
                        a = ch * GCH
                        pst = psum.next()
                        pstb = pst.t[:, :].bitcast(BF16)
                        kb.op("pe", lambda e: e.transpose(pstb[0:GCH, 0:128], S.kb.t[:, a:a + GCH], identb.t[:]), reads=[S.kb.k, identb.k], writes=[pst.k], inc=False)
                        kb.op("pe", lambda e: e.transpose(pstb[0:GCH, 128:256], S.vh.t[:, a:a + GCH], identb.t[:]), reads=[S.vh.k, identb.k], writes=[pst.k])
                        psA = psum.next()
                        kb.op("pe", lambda e: e.matmul(psA.t[0:GCH, 0:GCH], S.kb.t[:, a:a + GCH], S.qb.t[:, a:a + GCH], start=True, stop=True),
                              reads=[S.kb.k, S.qb.k], writes=[psA.k])
                        st[it] = [pst, psA]

                    def stage_b(it):
                        ch, u = it
                        pst, psA = st[it]
                        kv = kvb.next()
                        kb.op("act", lambda e: e.copy(out=kv.t[0:GCH, :], in_=pst.t[:, :].bitcast(BF16)[0:GCH, 0:256]), reads=[pst.k], writes=[kv.k])
                        am = Am32.next()
                        kb.op("dve", lambda e: e.copy_predicated(out=am.t[:, :], mask=cmask, data=psA.t[0:GCH, 0:GCH]),
                              reads=[psA.k, cst.k], writes=[am.k])
                        hh = Hh.next()
                        kb.op("dve", lambda e: e.tensor_scalar(out=hh.t[:], in0=H[hds[u]].t[:], scalar1=ebmU[u].t[:, ch:ch + 1], scalar2=None, op0=ALU.mult),
                              reads=[H[hds[u]].k, ebmU[u].k], writes=[hh.k])
                        st[it] = [kv, am, hh]

                    def stage_c(it):
                        ch, u = it
                        S = Ss[u]
                        a = ch * GCH
                        kv, am, hh = st[it]
                        psO = psOs[u]
                        kb.op("pe", lambda e: e.matmul(psO.t[:, a:a + GCH], hh.t[:], S.qb.t[:, a:a + GCH], start=True, stop=False),
                              reads=[hh.k, S.qb.k], writes=[psO.k], inc=False)
                        kb.op("pe", lambda e: e.matmul(psO.t[:, a:a + GCH], kv.t[0:GCH, 128:256], am.t[:, :], start=False, stop=True),
                              reads=[kv.k, am.k], writes=[psO.k])
                        psH = psum.next()
                        kb.op("pe", lambda e: e.matmul(psH.t[:, 0:128], kv.t[0:GCH, 0:128], kv.t[0:GCH, 128:256], start=True, stop=True),
                              reads=[kv.k], writes=[psH.k])
                        hs = Hs.next()
                        kb.op("dve", lambda e: e.scalar_tensor_tensor(out=hs.t[:], in0=H[hds[u]].t[:], scalar=ebmU[u].t[:, ch:ch + 1], in1=psH.t[:, 0:128],
                                                                      op0=ALU.mult, op1=ALU.add),
                              reads=[H[hds[u]].k, ebmU[u].k, psH.k], writes=[hs.k])
                        kb.op("dve", lambda e: e.tensor_scalar(out=H[hds[u]].t[:], in0=hs.t[:], scalar1=elastU[u].t[:, ch:ch + 1], scalar2=None, op0=ALU.mult),
                              reads=[hs.k, elastU[u].k], writes=[H[hds[u]].k])
                        del st[it]

                    stage_a(items[0])
                    for i, it in enumerate(items):
                        stage_b(it)
                        if i + 1 < len(items):
                            stage_a(items[i + 1])
                        stage_c(it)
                    for u in range(2):
                        head_norm_out(psOs[u].t[:, :n], psOs[u].k, True, osb, tmpb, gn.t[:, hds[u]:hds[u] + 1], gn.k, Ss[u].sg,
                                      yT.t[:, 4 + hds[u], :n], yT.k, n)

                def gla_sample(hd, S):
                    n = 128
                    gla_common(hd, S, n, CO["rmask_s"], CO["blkmask"], False)
                    kv = chunk_tm(S, 0)
                    am = chunk_att(S, 0, kv, CO["blkmask"])
                    psO = psacc.next()
                    kb.op("pe", lambda e: e.matmul(psO.t[:, 0:128], kv.t[:, 128:256], am.t[:], start=True, stop=True),
                          reads=[kv.k, am.k], writes=[psO.k])
                    kb.dma("sp", out=h0.t[:], in_=state_hgrn[:, hd].rearrange("s k v -> k s v"), writes=[h0.k])
                    psI = psum.next()
                    for sq_ in range(NS):
                        kb.op("pe", lambda e: e.matmul(psI.t[:, sq_ * TS:(sq_ + 1) * TS], h0.t[:, sq_, :], S.qs.t[:, sq_ * TS:(sq_ + 1) * TS],
                                                       start=True, stop=True),
                              reads=[h0.k, S.qs.k], writes=[psI.k], inc=(sq_ == NS - 1))
                    kb.op("act", lambda e: e.copy(out=tmpb.t[:, :n], in_=psI.t[:, :n]), reads=[psI.k], writes=[tmpb.k])
                    kb.op("dve", lambda e: e.tensor_tensor(out=osb.t[:, :n], in0=psO.t[:, :n], in1=tmpb.t[:, :n], op=ALU.add),
                          reads=[psO.k, tmpb.k], writes=[osb.k])
                    head_norm_out(osb.t[:, :n], osb.k, False, osb, tmpb, gn.t[:, hd:hd + 1], gn.k, S.sg, yT.t[:, 4 + hd, :n], yT.k, n)
                    kb.op("dve", lambda e: e.tensor_tensor(out=vexp.t[:], in0=kv.t[:, 128:256].unsqueeze(1).to_broadcast([128, NS, 128]),
                                                           in1=cst.t[:, CO["seqmask"]:CO["seqmask"] + NS].unsqueeze(2).to_broadcast([128, NS, 128]),
                                                           op=ALU.mult), reads=[kv.k, cst.k], writes=[vexp.k])
                    e1l = e1.t[:, :n].rearrange("p (s t) -> p s t", t=TS)
                    for j in range(4):
                        psD = psum.next()
                        kb.op("pe", lambda e: e.matmul(psD.t[:, :], kv.t[:, 0:128], vexp.t[:, 4 * j:4 * j + 4, :].rearrange("p s v -> p (s v)"),
                                                       start=True, stop=True), reads=[kv.k, vexp.k], writes=[psD.k])
                        hb = hn.next()
                        kb.op("dve", lambda e: e.tensor_tensor(out=hb.t[:], in0=psD.t[:, :].rearrange("p (s v) -> p s v", s=4),
                                                               in1=h0.t[:, 4 * j:4 * j + 4, :], op=ALU.add), reads=[psD.k, h0.k], writes=[hb.k])
                        kb.op("dve", lambda e: e.tensor_tensor(out=hb.t[:], in0=hb.t[:],
                                                               in1=e1l[:, 4 * j:4 * j + 4, TS - 1].unsqueeze(2).to_broadcast([128, 4, 128]),
                                                               op=ALU.mult), reads=[hb.k, e1.k], writes=[hb.k])
                        kb.dma("sp", out=hgrn_s[4 * j:4 * j + 4, hd].rearrange("s k v -> k s v"), in_=hb.t[:], reads=[hb.k])

                for ti, (c0, c1) in enumerate(TILES):
                    n = c1 - c0
                    is_s = c0 >= SEQ
                    rmsnorm_x(ti, gi, hT)
                    for oi, b in enumerate(order):
                        w = s.get(li)
                        li += 1
                        wv = wview(w.t, KC, 256)
                        role = b // 2
                        for jj in range(2):
                            c = (2 * b + jj) % 4
                            S = sets[jj]
                            ps = psum.next()
                            for kc in range(KC):
                                kb.op("pe", lambda e: e.matmul(ps.t[:, :n], wv[:, kc, jj * 128:(jj + 1) * 128], hT.t[:, kc, :n],
                                                               start=(kc == 0), stop=(kc == KC - 1)),
                                      reads=[w.k, hT.k], writes=[ps.k], inc=(kc == KC - 1))
                            if is_s:
                                u_now = ubs.t[:, c, :, 2:TS + 2]
                                u_m1 = ubs.t[:, c, :, 1:TS + 1]
                                u_m2 = ubs.t[:, c, :, 0:TS]
                                utrk = ubs.k
                                v3 = lambda ap: ap.rearrange("p (s t) -> p s t", t=TS)
                            else:
                                u_now = ubuf.t[:, c, 2:2 + n]
                                u_m1 = ubuf.t[:, c, 1:1 + n]
                                u_m2 = ubuf.t[:, c, 0:n]
                                utrk = ubuf.k
                                v3 = lambda ap: ap
                            if role == 0:
                                kb.op("act", lambda e: e.copy(out=S.qs.t[:, :n], in_=ps.t[:, :n]), reads=[ps.k], writes=[S.qs.k])
                            elif role == 2:
                                kb.op("dve", lambda e: e.tensor_tensor(out=u_now, in0=v3(S.qs.t[:, :n]), in1=v3(ps.t[:, :n]), op=ALU.mult),
                                      reads=[S.qs.k, ps.k], writes=[utrk])
                            elif role == 1:
                                ct = S.sg
                                kb.op("dve", lambda e: e.tensor_scalar(out=v3(ct.t[:, :n]), in0=u_now, scalar1=cw.t[:, 2, c:c + 1], scalar2=None, op0=ALU.mult),
                                      reads=[utrk, cw.k], writes=[ct.k])
                                kb.op("dve", lambda e: e.scalar_tensor_tensor(out=v3(ct.t[:, :n]), in0=u_m1, scalar=cw.t[:, 1, c:c + 1], in1=v3(ct.t[:, :n]),
                                                                              op0=ALU.mult, op1=ALU.add), reads=[utrk, cw.k, ct.k], writes=[ct.k])
                                kb.op("dve", lambda e: e.scalar_tensor_tensor(out=v3(ct.t[:, :n]), in0=u_m2, scalar=cw.t[:, 0, c:c + 1], in1=v3(ct.t[:, :n]),
                                                                              op0=ALU.mult, op1=ALU.add), reads=[utrk, cw.k, ct.k], writes=[ct.k])
                                kb.op("dve", lambda e: e.tensor_tensor(out=yT.t[:, c, :n], in0=ct.t[:, :n], in1=ps.t[:, :n], op=ALU.mult),
                                      reads=[ct.k, ps.k], writes=[yT.k])
                            elif role == 3:
                                kb.op("act", lambda e: e.activation(out=S.qs.t[:, :n], in_=ps.t[:, :n], func=AF.Silu), reads=[ps.k], writes=[S.qs.k])
                            elif role == 4:
                                kb.op("act", lambda e: e.activation(out=S.fk.t[:, :n], in_=ps.t[:, :n], func=AF.Sigmoid), reads=[ps.k], writes=[S.fk.k])
                                kb.op("dve", lambda e: e.tensor_scalar(out=S.fk.t[:, :n], in0=S.fk.t[:, :n], scalar1=oml.t[:, c:c + 1], scalar2=lb.t[:, c:c + 1],
                                                                       op0=ALU.mult, op1=ALU.add), reads=[S.fk.k, oml.k, lb.k], writes=[S.fk.k])
                                kb.op("act", lambda e: e.activation(out=S.lf.t[:, :n], in_=S.fk.t[:, :n], func=AF.Ln), reads=[S.fk.k], writes=[S.lf.k])
                                kb.op("dve", lambda e: e.tensor_scalar(out=S.fk.t[:, :n], in0=S.fk.t[:, :n], scalar1=-1.0, scalar2=1.0,
                                                                       op0=ALU.mult, op1=ALU.add), reads=[S.fk.k], writes=[S.fk.k])
                            elif role == 5:
                                vo = S.vb if is_s else S.vh
                                kb.op("act", lambda e: e.copy(out=vo.t[:, :n], in_=ps.t[:, :n]), reads=[ps.k], writes=[vo.k])
                            elif role == 6:
                                kb.op("act", lambda e: e.activation(out=S.sg.t[:, :n], in_=ps.t[:, :n], func=AF.Silu), reads=[ps.k], writes=[S.sg.k])
                        if role == 6:
                            if is_s:
                                for jj in range(2):
                                    gla_sample((2 * b + jj) % 4, sets[jj])
                            else:
                                gla_prompt_pair([(2 * b) % 4, (2 * b + 1) % 4], sets, n)
                    if not is_s:
                        if c1 == SEQ:
                            for c in range(4):
                                kb.dma("sp", out=conv_p[:, c * 128:(c + 1) * 128].rearrange("r p -> p r"), in_=ubuf.t[:, c, n:n + 2], reads=[ubuf.k],
                                       allow_slow_non_contiguous=True)
                            for hd in range(4):
                                kb.dma("sp", out=hgrn_p[hd], in_=H[hd].t[:], reads=[H[hd].k])
                        else:
                            kb.op("dve", lambda e: e.tensor_copy(out=ubuf.t[:, :, 0:2], in_=ubuf.t[:, :, n:n + 2]), reads=[ubuf.k], writes=[ubuf.k])
                    else:
                        kb.op("dve", lambda e: e.tensor_copy(out=cst32.t[:].rearrange("p c (s r) -> p c s r", r=2), in_=ubs.t[:, :, :, TS:TS + 2]),
                              reads=[ubs.k], writes=[cst32.k])
                        ps = psum.next()
                        for c in range(4):
                            kb.op("pe", lambda e: e.transpose(ps.t[0:32, c * 128:(c + 1) * 128], cst32.t[:, c, :], ident.t),
                                  reads=[cst32.k, ident.k], writes=[ps.k], inc=(c == 3))
                        kb.op("act", lambda e: e.copy(out=scout.t[:], in_=ps.t[0:32, :]), reads=[ps.k], writes=[scout.k])
                        kb.dma("sp", out=conv_s.rearrange("s r c -> (s r) c"), in_=scout.t[:], reads=[scout.k])
                    li = proj_fm(s, li, 4, 256, lambda kc: yT.t[:, kc, :n], yT.k, n, resid_add(ti, 1.0))
                kb.barrier()

        def phase_mix_odd():
            gi = GI["mix"] + 1
            WDEC = 0.6065306597126334
            with ExitStack() as ph:
                hT = sb(ph, "hT", [128, KC, 256], BF16)
                co = sb(ph, "co_sb", [128, NCO])
                kb.dma("sp", out=co.t[:], in_=consts_odd_d[:, :], writes=[co.k])
                prm = sb(ph, "oprm", [128, 64])
                PRM = {}
                pcol = [0]

                def pload(name, src_ap, ncol):
                    PRM[name] = pcol[0]
                    kb.dma("sp", out=prm.t[:, pcol[0]:pcol[0] + ncol], in_=src_ap, writes=[prm.k], allow_slow_non_contiguous=True)
                    pcol[0] += ncol
                pload("rgn", ret_gnorm.rearrange("(c p) -> p c", p=128), 4)
                pload("mu", rwkv_mu.rearrange("(c p) -> p c", p=128), 14)
                pload("w0", rwkv_w0.rearrange("(c p) -> p c", p=128), 4)
                pload("a0", rwkv_a0.rearrange("(c p) -> p c", p=128), 4)
                pload("kk", rwkv_k_k.rearrange("(c p) -> p c", p=128), 4)
                pload("ka", rwkv_k_a.rearrange("(c p) -> p c", p=128), 4)
                pload("rk", rwkv_r_k.rearrange("(c p) -> p c", p=128), 4)
                pload("lg", rwkv_lnx_g.rearrange("(c p) -> p c", p=128), 4)
                pload("lbb", rwkv_lnx_b.rearrange("(c p) -> p c", p=128), 4)
                omu = sb(ph, "omu", [128, 14])
                kb.op("dve", lambda e: e.tensor_scalar(out=omu.t[:], in0=prm.t[:, PRM["mu"]:PRM["mu"] + 14], scalar1=-1.0, scalar2=1.0,
                                                       op0=ALU.mult, op1=ALU.add), reads=[prm.k], writes=[omu.k])
                gneps = sb(ph, "gneps", [128, 1])
                kb.op("dve", lambda e: e.memset(gneps.t[:], 64e-5), writes=[gneps.k])
                w2a2 = sb(ph, "w2a2", [128, 512])
                g2sb = sb(ph, "g2sb", [128, 512])
                kb.dma("sp", out=w2a2.t[0:64, :], in_=rwkv_w2[:, :], writes=[w2a2.k])
                kb.dma("sp", out=w2a2.t[64:128, :], in_=rwkv_a2[:, :], writes=[w2a2.k])
                kb.dma("sp", out=g2sb.t[:], in_=rwkv_g2[:, :], writes=[g2sb.k])

                W = 256
                yT = sb(ph, "yT", [128, KC, W], BF16)
                sets = [HSet(ph, "0", W), HSet(ph, "1", W)]
                osb = sb(ph, "osb", [128, W])
                tmpb = sb(ph, "tmpb", [128, W])
                osbR = sb(ph, "osbR", [128, W])
                tmpbR = sb(ph, "tmpbR", [128, W])
                cosb = sb(ph, "cosb", [128, W])
                sinb = sb(ph, "sinb", [128, W])
                kvtm = Ring([sb(ph, "kvtm%d" % i, [128, 256]) for i in range(2)])
                Am = Ring([sb(ph, "Am%d" % i, [128, 128]) for i in range(2)])
                Hret = [sb(ph, "Hret%d" % i, [128, 128]) for i in range(4)]
                h0r = Ring([sb(ph, "h0r%d" % i, [128, 4, 128]) for i in range(1)])
                h0Tr = Ring([sb(ph, "h0Tr%d" % i, [128, 4, 128]) for i in range(4)])
                hn = Ring([sb(ph, "hn%d" % i, [128, 4, 128]) for i in range(1)])
                vexp = Ring([sb(ph, "vexp%d" % i, [128, 4, 128]) for i in range(2)])
                pdraw = sb(ph, "pdraw", [128, W + 4])
                carry = sb(ph, "carry", [128, 14])
                shs = sb(ph, "shs", [128, 14, NS])
                p28 = sb(ph, "p28", [128, W])
                p29 = sb(ph, "p29", [128, W])
                R = {nm: sb(ph, "rw_" + nm, [128, W]) for nm in ("r", "k", "v", "lw", "a", "kk", "bt", "at", "atA", "atB", "rA", "rB", "bon", "e1")}
                Hblk = [sb(ph, "Hblk%d" % i, [128, 128]) for i in range(4)]
                Hh = Ring([sb(ph, "Hh%d" % i, [128, 128]) for i in range(2)])
                Hs = Ring([sb(ph, "Hs%d" % i, [128, 128]) for i in range(2)])
                bm = sb(ph, "bm", [128, 4])
                ebm = sb(ph, "ebm", [128, 4])
                class RWUnit:
                    def __init__(self, u):
                        self.ext = {nm: sb(ph, "ext%d_%s" % (u, nm), [128, 128]) for nm in ("KA", "KB", "BA", "BB", "VA", "VB")}
                        self.M = Ring([sb(ph, "u%d_M%d" % (u, i), [128, 256], BF16) for i in range(2)])
                        self.MT = Ring([sb(ph, "u%d_MT%d" % (u, i), [128, 256], BF16) for i in range(2)])
                        self.TTb = Ring([sb(ph, "u%d_TTb%d" % (u, i), [128, 256], BF16) for i in range(2)])
                        self.TT = sb(ph, "u%d_TT" % u, [128, 256])
                        self.LakT = sb(ph, "u%d_LakT" % u, [128, 256])
                        self.ArbT = sb(ph, "u%d_ArbT" % u, [128, 256])
                        self.ArkT = sb(ph, "u%d_ArkT" % u, [128, 256])
                UX = [RWUnit(0), RWUnit(1)]
                extU = {nm: sb(ph, "ext_" + nm, [128, 128]) for nm in ("UA", "UB")}
                identb = sb(ph, "identb", [128, 128], BF16)
                kb.op("dve", lambda e: e.tensor_copy(out=identb.t[:], in_=ident.t), reads=[ident.k], writes=[identb.k])
                Rsb = sb(ph, "Rsb", [128, 128])
                xTs = sb(ph, "xTs", [128, 128])

                for X_ in UX:
                    for nm in X_.ext:
                        kb.op("dve", lambda e: e.memset(X_.ext[nm].t[:], 0.0), writes=[X_.ext[nm].k])
                for nm in extU:
                    kb.op("dve", lambda e: e.memset(extU[nm].t[:], 0.0), writes=[extU[nm].k])
                for i in range(4):
                    kb.op("dve", lambda e: e.memset(Hret[i].t[:], 0.0), writes=[Hret[i].k])
                    kb.op("dve", lambda e: e.memset(Hblk[i].t[:], 0.0), writes=[Hblk[i].k])
                kb.op("dve", lambda e: e.memset(carry.t[:], 0.0), writes=[carry.k])
                for c in range(14):
                    kb.dma("sp", out=shs.t[:, c, :], in_=state_shift[:, c * 128:(c + 1) * 128].rearrange("s p -> p s"), writes=[shs.k],
                           allow_slow_non_contiguous=True)

                win_v = odd_w_in.rearrange("(k p) n -> p k n", p=128)
                wout_v = odd_w_out.rearrange("(k p) n -> p k n", p=128)
                def pair_blocks(pr_):
                    return [((16 + pr_) * 128, 128), ((20 + pr_) * 128, 128), ((24 + pr_) * 128, 128)]
                blocks = ([(b * 256, 256) for b in (0, 2, 4, 6)] + [(28 * 128, 256)] + pair_blocks(0) + pair_blocks(1)
                          + [(b * 256, 256) for b in (1, 3, 5, 7)] + pair_blocks(2) + pair_blocks(3))
                NSUB = SEQ // 256 + 1
                loads = []
                for ti in range(NSUB):
                    for (c0_, wd_) in blocks:
                        loads.append(((lambda wd__: (lambda t: wview(t, KC, wd__)))(wd_), win_v[:, :, c0_:c0_ + wd_]))
                    for b in range(4):
                        loads.append((lambda t: wview(t, KC, 256), wout_v[:, :, b * 256:(b + 1) * 256]))
                s = Stream(kb, wpool, loads)
                li = 0
                LG = [float(np.log1p(-2.0 ** (-5.0 - h))) for h in range(4)]

                def ret_chunk_tm(S, a, hd, is_s):
                    pst = psum.next()
                    kb.op("pe", lambda e: e.transpose(pst.t[:, 0:128], S.fk.t[:, a:a + 128], ident.t), reads=[S.fk.k, ident.k], writes=[pst.k], inc=False)
                    kb.op("pe", lambda e: e.transpose(pst.t[:, 128:256], S.vb.t[:, a:a + 128], ident.t), reads=[S.vb.k, ident.k], writes=[pst.k])
                    kv = kvtm.next()
                    kdc = CO2["ret_kd"] + (4 if is_s else 0) + hd
                    kb.op("dve", lambda e: e.tensor_scalar(out=kv.t[:, 0:128], in0=pst.t[:, 0:128], scalar1=co.t[:, kdc:kdc + 1], scalar2=None, op0=ALU.mult),
                          reads=[pst.k, co.k], writes=[kv.k])
                    kb.op("dve", lambda e: e.tensor_copy(out=kv.t[:, 128:256], in_=pst.t[:, 128:256]), reads=[pst.k], writes=[kv.k])
                    return kv

                def ret_att(S, a, hd, is_s):
                    psA = psum.next()
                    kb.op("pe", lambda e: e.matmul(psA.t[:, 0:128], S.fk.t[:, a:a + 128], S.qs.t[:, a:a + 128], start=True, stop=True),
                          reads=[S.fk.k, S.qs.k], writes=[psA.k])
                    am = Am.next()
                    mo = CO2["ret_dm_s" if is_s else "ret_dm_p"] + hd * 128
                    kb.op("dve", lambda e: e.tensor_tensor(out=am.t[:], in0=psA.t[:, 0:128], in1=co.t[:, mo:mo + 128], op=ALU.mult),
                          reads=[psA.k, co.k], writes=[am.k])
                    return am

                def ret_head(hd, S, n, is_s):
                    qo = CO2["ret_qd_s" if is_s else "ret_qd_p"] + hd * 128
                    nch = n // 128
                    kb.op("dve", lambda e: e.tensor_tensor(out=S.lf.t[:, :n].rearrange("p (c t) -> p c t", t=128),
                                                           in0=S.qs.t[:, :n].rearrange("p (c t) -> p c t", t=128),
                                                           in1=co.t[:, qo:qo + 128].unsqueeze(1).to_broadcast([128, nch, 128]), op=ALU.mult),
                          reads=[S.qs.k, co.k], writes=[S.lf.k])
                    psO = psacc.next()
                    if not is_s:
                        g128 = float(np.exp(LG[hd] * 128))
                        for ch in range(nch):
                            a = ch * 128
                            kv = ret_chunk_tm(S, a, hd, False)
                            am = ret_att(S, a, hd, False)
                            kb.op("pe", lambda e: e.matmul(psO.t[:, a:a + 128], Hret[hd].t[:], S.lf.t[:, a:a + 128], start=True, stop=False),
                                  reads=[Hret[hd].k, S.lf.k], writes=[psO.k], inc=False)
                            kb.op("pe", lambda e: e.matmul(psO.t[:, a:a + 128], kv.t[:, 128:256], am.t[:], start=False, stop=True),
                                  reads=[kv.k, am.k], writes=[psO.k])
                            psH = psum.next()
                            kb.op("pe", lambda e: e.matmul(psH.t[:, 0:128], kv.t[:, 0:128], kv.t[:, 128:256], start=True, stop=True),
                                  reads=[kv.k], writes=[psH.k])
                            kb.op("dve", lambda e: e.scalar_tensor_tensor(out=Hret[hd].t[:], in0=Hret[hd].t[:], scalar=g128, in1=psH.t[:, 0:128],
                                                                          op0=ALU.mult, op1=ALU.add), reads=[Hret[hd].k, psH.k], writes=[Hret[hd].k])
                        head_norm_out(psO.t[:, :n], psO.k, True, osbR, tmpbR, prm.t[:, PRM["rgn"] + hd:PRM["rgn"] + hd + 1], prm.k, S.sg,
                                      yT.t[:, hd, :n], yT.k, n)
                    else:
                        g8 = float(np.exp(LG[hd] * TS))
                        kv = ret_chunk_tm(S, 0, hd, True)
                        am = ret_att(S, 0, hd, True)
                        kb.op("pe", lambda e: e.matmul(psO.t[:, 0:128], kv.t[:, 128:256], am.t[:], start=True, stop=True),
                              reads=[kv.k, am.k], writes=[psO.k])
                        psI = psum.next()
                        for j in range(4):
                            hg_ = h0Tr.next()
                            kb.dma("sp", out=hg_.t[:], in_=state_ret[4 * j:4 * j + 4, hd].rearrange("s k v -> k s v"), writes=[hg_.k])
                            for q in range(4):
                                sq_ = 4 * j + q
                                kb.op("pe", lambda e: e.matmul(psI.t[:, sq_ * TS:(sq_ + 1) * TS], hg_.t[:, q, :], S.lf.t[:, sq_ * TS:(sq_ + 1) * TS],
                                                               start=True, stop=True), reads=[hg_.k, S.lf.k], writes=[psI.k], inc=(q == 3))
                            vx = vexp.next()
                            kb.op("dve", lambda e: e.tensor_tensor(out=vx.t[:], in0=kv.t[:, 128:256].unsqueeze(1).to_broadcast([128, 4, 128]),
                                                                   in1=cst.t[:, CO["seqmask"] + 4 * j:CO["seqmask"] + 4 * j + 4].unsqueeze(2).to_broadcast([128, 4, 128]),
                                                                   op=ALU.mult), reads=[kv.k, cst.k], writes=[vx.k])
                            psD = psum.next()
                            kb.op("pe", lambda e: e.matmul(psD.t[:, :], kv.t[:, 0:128], vx.t[:].rearrange("p s v -> p (s v)"), start=True, stop=True),
                                  reads=[kv.k, vx.k], writes=[psD.k])
                            hb = hn.next()
                            kb.op("dve", lambda e: e.scalar_tensor_tensor(out=hb.t[:], in0=hg_.t[:], scalar=g8,
                                                                          in1=psD.t[:, :].rearrange("p (s v) -> p s v", s=4), op0=ALU.mult, op1=ALU.add),
                                  reads=[hg_.k, psD.k], writes=[hb.k])
                            kb.dma("sp", out=ret_s[4 * j:4 * j + 4, hd].rearrange("s k v -> k s v"), in_=hb.t[:], reads=[hb.k])
                        kb.op("act", lambda e: e.copy(out=tmpb.t[:, :n], in_=psI.t[:, :n]), reads=[psI.k], writes=[tmpb.k])
                        kb.op("dve", lambda e: e.tensor_tensor(out=osb.t[:, :n], in0=psO.t[:, :n], in1=tmpb.t[:, :n], op=ALU.add),
                              reads=[psO.k, tmpb.k], writes=[osb.k])
                        head_norm_out(osb.t[:, :n], osb.k, False, osb, tmpb, prm.t[:, PRM["rgn"] + hd:PRM["rgn"] + hd + 1], prm.k, S.sg,
                                      yT.t[:, hd, :n], yT.k, n)

                def rope(S_buf, ps, n, scale):
                    kb.op("act", lambda e: e.activation(out=tmpb.t[:, :n], in_=ps.t[:, :n], func=AF.Copy, scale=scale), reads=[ps.k], writes=[tmpb.k])
                    pr_ = psum.next()
                    kb.op("pe", lambda e: e.matmul(pr_.t[:, :n], co.t[:, CO2["perm"]:CO2["perm"] + 128], tmpb.t[:, :n], start=True, stop=True),
                          reads=[co.k, tmpb.k], writes=[pr_.k])
                    kb.op("dve", lambda e: e.tensor_tensor(out=S_buf.t[:, :n], in0=pr_.t[:, :n], in1=sinb.t[:, :n], op=ALU.mult),
                          reads=[pr_.k, sinb.k], writes=[S_buf.k])
                    kb.op("dve", lambda e: e.tensor_tensor(out=tmpb.t[:, :n], in0=tmpb.t[:, :n], in1=cosb.t[:, :n], op=ALU.mult),
                          reads=[tmpb.k, cosb.k], writes=[tmpb.k])
                    kb.op("dve", lambda e: e.tensor_tensor(out=S_buf.t[:, :n], in0=S_buf.t[:, :n], in1=tmpb.t[:, :n], op=ALU.add),
                          reads=[S_buf.k, tmpb.k], writes=[S_buf.k])

                def m2(ap):
                    return ap.rearrange("p (h t) -> p h t", h=2)

                def bc2(ap):
                    return ap.unsqueeze(1).to_broadcast([128, 2, 128])

                def rw_gram(dst, lhs_full, rhsA, rhsB, a, mask_ap, swap=False):
                    ps = psum.next()
                    for h, rh in enumerate((rhsA, rhsB)):
                        if swap:
                            l_, r_ = rh, lhs_full
                        else:
                            l_, r_ = lhs_full, rh
                        kb.op("pe", lambda e: e.matmul(ps.t[:, h * 128:(h + 1) * 128], l_.t[:, a:a + 128], r_.t[:, a:a + 128], start=True, stop=True),
                              reads=[l_.k, r_.k], writes=[ps.k], inc=(h == 1))
                    kb.op("dve", lambda e: e.tensor_tensor(out=m2(dst.t[:, :]), in0=m2(ps.t[:, 0:256]), in1=bc2(mask_ap), op=ALU.mult),
                          reads=[ps.k, co.k, cst.k], writes=[dst.k])

                def rw_tm(a, X):
                    ext = X.ext
                    pst = psum.next()
                    kb.op("pe", lambda e: e.transpose(pst.t[:, 0:128], R["k"].t[:, a:a + 128], ident.t), reads=[R["k"].k, ident.k], writes=[pst.k], inc=False)
                    kb.op("pe", lambda e: e.transpose(pst.t[:, 128:256], R["bt"].t[:, a:a + 128], ident.t), reads=[R["bt"].k, ident.k], writes=[pst.k], inc=False)
                    kb.op("pe", lambda e: e.transpose(pst.t[:, 256:384], R["v"].t[:, a:a + 128], ident.t), reads=[R["v"].k, ident.k], writes=[pst.k])
                    for i, nm in enumerate(("K", "B", "V")):
                        kb.op("act", lambda e: e.copy(out=ext[nm + "A"].t[:, 0:64], in_=pst.t[:, i * 128:i * 128 + 64]), reads=[pst.k], writes=[ext[nm + "A"].k])
                        kb.op("act", lambda e: e.copy(out=ext[nm + "B"].t[:, 64:128], in_=pst.t[:, i * 128 + 64:i * 128 + 128]), reads=[pst.k], writes=[ext[nm + "B"].k])

                def rw_pre(As, is_s):
                    mT = co.t[:, CO2["bstrictT" if is_s else "strictT"]:][:, 0:128]
                    mL = co.t[:, CO2["bstrictL" if is_s else "strictL"]:][:, 0:128]
                    mC = cst.t[:, CO["blkmask" if is_s else "causalT"]:][:, 0:128]
                    cur = []
                    for u, a in enumerate(As):
                        X = UX[u]
                        rw_tm(a, X)
                        MT = X.MT.next()
                        M = X.M.next()
                        rw_gram(MT, R["bt"], R["atA"], R["atB"], a, mT)
                        rw_gram(M, R["bt"], R["atA"], R["atB"], a, mL, swap=True)
                        rw_gram(X.LakT, R["k"], R["atA"], R["atB"], a, mT)
                        rw_gram(X.ArbT, R["bt"], R["rA"], R["rB"], a, mC)
                        rw_gram(X.ArkT, R["k"], R["rA"], R["rB"], a, mC)
                        cur.append([M, MT, None])
                    nlev = 2 if is_s else 6
                    for u in range(len(As)):
                        X = UX[u]
                        M, MT, _ = cur[u]
                        TT = X.TTb.next() if nlev > 0 else X.TT
                        kb.op("dve", lambda e: e.tensor_tensor(out=m2(TT.t[:, :]), in0=m2(MT.t[:, :]), in1=bc2(identb.t[:]), op=ALU.add),
                              reads=[MT.k, identb.k], writes=[TT.k])
                        cur[u][2] = TT
                    for lev in range(nlev):
                        pp = []
                        for u in range(len(As)):
                            M, MT, TT = cur[u]
                            psa = psum.next()
                            psb = psum.next()
                            for h in range(2):
                                hs_ = slice(h * 128, (h + 1) * 128)
                                kb.op("pe", lambda e: e.matmul(psa.t[:, hs_], MT.t[:, hs_], M.t[:, hs_], start=True, stop=True),
                                      reads=[MT.k, M.k], writes=[psa.k], inc=(h == 1))
                            for h in range(2):
                                hs_ = slice(h * 128, (h + 1) * 128)
                                kb.op("pe", lambda e: e.matmul(psb.t[:, hs_], M.t[:, hs_], MT.t[:, hs_], start=True, stop=True),
                                      reads=[MT.k, M.k], writes=[psb.k], inc=(h == 1))
                            pp.append((psa, psb))
                        for u in range(len(As)):
                            X = UX[u]
                            psa, psb = pp[u]
                            M2 = X.M.next()
                            MT2 = X.MT.next()
                            kb.op("act", lambda e: e.copy(out=M2.t[:, :], in_=psa.t[:, 0:256]), reads=[psa.k], writes=[M2.k])
                            kb.op("dve", lambda e: e.tensor_copy(out=MT2.t[:, :], in_=psb.t[:, 0:256]), reads=[psb.k], writes=[MT2.k])
                            cur[u][0], cur[u][1] = M2, MT2
                        pq = []
                        for u in range(len(As)):
                            M, MT, TT = cur[u]
                            pst_ = psum.next()
                            for h in range(2):
                                hs_ = slice(h * 128, (h + 1) * 128)
                                kb.op("pe", lambda e: e.matmul(pst_.t[:, hs_], M.t[:, hs_], TT.t[:, hs_], start=True, stop=True),
                                      reads=[M.k, TT.k], writes=[pst_.k], inc=(h == 1))
                            pq.append(pst_)
                        for u in range(len(As)):
                            X = UX[u]
                            M, MT, TT = cur[u]
                            TT2 = X.TT if lev == nlev - 1 else X.TTb.next()
                            kb.op("dve", lambda e: e.tensor_tensor(out=TT2.t[:, :], in0=TT.t[:, :], in1=pq[u].t[:, 0:256], op=ALU.add),
                                  reads=[TT.k, pq[u].k], writes=[TT2.k])
                            cur[u][2] = TT2
                    return [UX[u] for u in range(len(As))]

                def rw_solve_u(X, psR):
                    kb.op("act", lambda e: e.copy(out=Rsb.t[:], in_=psR.t[:, 0:128]), reads=[psR.k], writes=[Rsb.k])
                    psU = psum.next()
                    for h in range(2):
                        kb.op("pe", lambda e: e.matmul(psU.t[:, h * 64:(h + 1) * 64], X.TT.t[:, h * 128:(h + 1) * 128], Rsb.t[:, h * 64:(h + 1) * 64],
                                                       start=True, stop=True), reads=[X.TT.k, Rsb.k], writes=[psU.k], inc=(h == 1))
                    kb.op("dve", lambda e: e.tensor_copy(out=extU["UA"].t[:, 0:64], in_=psU.t[:, 0:64]), reads=[psU.k], writes=[extU["UA"].k])
                    kb.op("dve", lambda e: e.tensor_copy(out=extU["UB"].t[:, 64:128], in_=psU.t[:, 64:128]), reads=[psU.k], writes=[extU["UB"].k])

                def rw_o_intra(psO, a, first, X):
                    seq_ = [(extU["UA"], X.ArbT, 0), (extU["UB"], X.ArbT, 1), (X.ext["VA"], X.ArkT, 0), (X.ext["VB"], X.ArkT, 1)]
                    for i, (eb, mat, h) in enumerate(seq_):
                        kb.op("pe", lambda e: e.matmul(psO.t[:, a:a + 128], eb.t[:], mat.t[:, h * 128:(h + 1) * 128],
                                                       start=(first and i == 0), stop=(i == 3)),
                              reads=[eb.k, mat.k], writes=[psO.k], inc=(i == 3))

                def rw_prompt(pr, n, psO):
                    nch = n // 128
                    units = rw_pre([ch * 128 for ch in range(nch)], False)
                    for ch in range(nch):
                        a = ch * 128
                        X = units[ch]
                        ext = X.ext
                        hh = Hh.next()
                        kb.op("dve", lambda e: e.tensor_scalar(out=hh.t[:], in0=Hblk[pr].t[:], scalar1=ebm.t[:, ch:ch + 1], scalar2=None, op0=ALU.mult),
                              reads=[Hblk[pr].k, ebm.k], writes=[hh.k])
                        psR = psum.next()
                        for h, at_ in enumerate((R["atA"], R["atB"])):
                            kb.op("pe", lambda e: e.matmul(psR.t[:, h * 64:(h + 1) * 64], at_.t[:, a:a + 128], hh.t[:, h * 64:(h + 1) * 64], start=True, stop=False),
                                  reads=[at_.k, hh.k], writes=[psR.k], inc=False)
                            vx = ext["VA" if h == 0 else "VB"]
                            kb.op("pe", lambda e: e.matmul(psR.t[:, h * 64:(h + 1) * 64], X.LakT.t[:, h * 128:(h + 1) * 128], vx.t[:, h * 64:(h + 1) * 64],
                                                           start=False, stop=True), reads=[X.LakT.k, vx.k], writes=[psR.k], inc=(h == 1))
                        rw_solve_u(X, psR)
                        kb.op("pe", lambda e: e.matmul(psO.t[:, a:a + 128], hh.t[:], R["r"].t[:, a:a + 128], start=True, stop=False),
                              reads=[hh.k, R["r"].k], writes=[psO.k], inc=False)
                        rw_o_intra(psO, a, False, X)
                        psH = psum.next()
                        seq_ = [(ext["BA"], extU["UA"]), (ext["BB"], extU["UB"]), (ext["KA"], ext["VA"]), (ext["KB"], ext["VB"])]
                        for i, (l_, r_) in enumerate(seq_):
                            kb.op("pe", lambda e: e.matmul(psH.t[:, 0:128], l_.t[:], r_.t[:], start=(i == 0), stop=(i == 3)),
                                  reads=[l_.k, r_.k], writes=[psH.k], inc=(i == 3))
                        hs = Hs.next()
                        kb.op("dve", lambda e: e.tensor_tensor(out=hs.t[:], in0=psH.t[:, 0:128], in1=hh.t[:], op=ALU.add),
                              reads=[psH.k, hh.k], writes=[hs.k])
                        kb.op("dve", lambda e: e.tensor_scalar(out=Hblk[pr].t[:], in0=hs.t[:], scalar1=R["e1"].t[:, a + 127:a + 128], scalar2=None, op0=ALU.mult),
                              reads=[hs.k, R["e1"].k], writes=[Hblk[pr].k])

                def rw_sample(pr, psO):
                    n = 128
                    X = rw_pre([0], True)[0]
                    ext = dict(X.ext)
                    ext.update(extU)
                    LakT = X.LakT
                    psX = psum.next()
                    psI = psum.next()
                    h0gs = []
                    for j in range(4):
                        sg_ = h0r.next()
                        hg_ = h0Tr.next()
                        kb.op("dve", lambda e: e.memset(sg_.t[:], 0.0), writes=[sg_.k])
                        for h in range(2):
                            kb.dma("sp", out=sg_.t[h * 64:(h + 1) * 64, :, h * 64:(h + 1) * 64],
                                   in_=state_rwkv[4 * j:4 * j + 4, 2 * pr + h].rearrange("s v k -> v s k"), writes=[sg_.k])
                        pst = psum.next()
                        for q in range(4):
                            kb.op("pe", lambda e: e.transpose(pst.t[:, q * 128:(q + 1) * 128], sg_.t[:, q, :], ident.t), reads=[sg_.k, ident.k], writes=[pst.k], inc=(q == 3))
                        evac_copy(hg_.t[:].rearrange("p s v -> p (s v)"), pst.t[:, :], [pst.k], [hg_.k])
                        h0gs.append(hg_)
                        for q in range(4):
                            sq_ = 4 * j + q
                            cs = slice(sq_ * TS, (sq_ + 1) * TS)
                            kb.op("pe", lambda e: e.matmul(psX.t[:, cs], hg_.t[:, q, :], R["at"].t[:, cs], start=True, stop=True),
                                  reads=[hg_.k, R["at"].k], writes=[psX.k], inc=False)
                            kb.op("pe", lambda e: e.matmul(psI.t[:, cs], hg_.t[:, q, :], R["r"].t[:, cs], start=True, stop=True),
                                  reads=[hg_.k, R["r"].k], writes=[psI.k], inc=(q == 3))
                    kb.op("act", lambda e: e.copy(out=xTs.t[:], in_=psX.t[:, 0:128]), reads=[psX.k], writes=[xTs.k])
                    kb.op("act", lambda e: e.copy(out=tmpb.t[:, :n], in_=psI.t[:, :n]), reads=[psI.k], writes=[tmpb.k])
                    psR = psum.next()
                    kb.op("pe", lambda e: e.matmul(psR.t[:, 0:128], xTs.t[:], ident.t, start=True, stop=False), reads=[xTs.k, ident.k], writes=[psR.k], inc=False)
                    for h in range(2):
                        vx = ext["VA" if h == 0 else "VB"]
                        kb.op("pe", lambda e: e.matmul(psR.t[:, h * 64:(h + 1) * 64], LakT.t[:, h * 128:(h + 1) * 128], vx.t[:, h * 64:(h + 1) * 64],
                                                       start=False, stop=(h == 1)), reads=[LakT.k, vx.k], writes=[psR.k], inc=(h == 1))
                    rw_solve_u(X, psR)
                    rw_o_intra(psO, 0, True, X)
                    e1l = R["e1"].t[:, :n].rearrange("p (s t) -> p s t", t=TS)
                    for j in range(4):
                        sm = cst.t[:, CO["seqmask"] + 4 * j:CO["seqmask"] + 4 * j + 4].unsqueeze(2).to_broadcast([128, 4, 128])
                        psD = psum.next()
                        seq_ = [("BA", "UA"), ("BB", "UB"), ("KA", "VA"), ("KB", "VB")]
                        for i, (l_, r_) in enumerate(seq_):
                            vx = vexp.next()
                            kb.op("dve", lambda e: e.tensor_tensor(out=vx.t[:], in0=ext[r_].t[:].unsqueeze(1).to_broadcast([128, 4, 128]), in1=sm, op=ALU.mult),
                                  reads=[ext[r_].k, cst.k], writes=[vx.k])
                            kb.op("pe", lambda e: e.matmul(psD.t[:, :], ext[l_].t[:], vx.t[:].rearrange("p s v -> p (s v)"), start=(i == 0), stop=(i == 3)),
                                  reads=[ext[l_].k, vx.k], writes=[psD.k], inc=True)
                        hb = hn.next()
                        kb.op("dve", lambda e: e.tensor_tensor(out=hb.t[:], in0=psD.t[:, :].rearrange("p (s v) -> p s v", s=4), in1=h0gs[j].t[:], op=ALU.add),
                              reads=[psD.k, h0gs[j].k], writes=[hb.k])
                        kb.op("dve", lambda e: e.tensor_tensor(out=hb.t[:], in0=hb.t[:], in1=e1l[:, 4 * j:4 * j + 4, TS - 1].unsqueeze(2).to_broadcast([128, 4, 128]),
                                                               op=ALU.mult), reads=[hb.k, R["e1"].k], writes=[hb.k])
                        pst = psum.next()
                        for q in range(4):
                            kb.op("pe", lambda e: e.transpose(pst.t[:, q * 128:(q + 1) * 128], hb.t[:, q, :], ident.t), reads=[hb.k, ident.k], writes=[pst.k], inc=(q == 3))
                        so = h0r.next()
                        evac_copy(so.t[:].rearrange("p s v -> p (s v)"), pst.t[:, :], [pst.k], [so.k])
                        for h in range(2):
                            kb.dma("sp", out=rwkv_s[4 * j:4 * j + 4, 2 * pr + h].rearrange("s v k -> v s k"),
                                   in_=so.t[h * 64:(h + 1) * 64, :, h * 64:(h + 1) * 64], reads=[so.k])

                SUB = [(c0, c0 + 256) for c0 in range(0, SEQ, 256)] + [(SEQ, NTOK)]
                for c0, c1 in SUB:
                    n = c1 - c0
                    is_s = c0 >= SEQ
                    ti = tile_of(c0)
                    rmsnorm_g(lambda kc: (xT[:, kc, c0:c1], xk[kc][ti]), gi, lambda kc: hT.t[:, kc, :n], hT.k, n)
                    kb.dma("sp", out=cosb.t[:, :n], in_=rope_d[0, :, c0:c1], writes=[cosb.k])
                    kb.dma("sp", out=sinb.t[:, :n], in_=rope_d[1, :, c0:c1], writes=[sinb.k])
                    psO_rw = None
                    for (bc0, wd) in blocks:
                        w = s.get(li)
                        li += 1
                        wv = wview(w.t, KC, wd)
                        for jj in range(wd // 128):
                            chunk = bc0 // 128 + jj
                            ps = psum.next()
                            for kc in range(KC):
                                kb.op("pe", lambda e: e.matmul(ps.t[:, :n], wv[:, kc, jj * 128:(jj + 1) * 128], hT.t[:, kc, :n],
                                                               start=(kc == 0), stop=(kc == KC - 1)),
                                      reads=[w.k, hT.k], writes=[ps.k], inc=(kc == KC - 1))
                            if chunk < 16:
                                S = sets[jj]
                                role = chunk // 4
                                if role == 0:
                                    rope(S.qs, ps, n, 1.0)
                                elif role == 1:
                                    rope(S.fk, ps, n, float(128 ** -0.5))
                                elif role == 2:
                                    kb.op("act", lambda e: e.copy(out=S.vb.t[:, :n], in_=ps.t[:, :n]), reads=[ps.k], writes=[S.vb.k])
                                else:
                                    kb.op("act", lambda e: e.activation(out=S.sg.t[:, :n], in_=ps.t[:, :n], func=AF.Silu), reads=[ps.k], writes=[S.sg.k])
                                continue
                            pc = chunk - 16
                            if is_s:
                                raw = pdraw.t[:, 0:NS * (TS + 1)].rearrange("p (s t) -> p s t", t=TS + 1)
                                kb.op("dve", lambda e: e.tensor_copy(out=raw[:, :, 0], in_=shs.t[:, pc, :]), reads=[shs.k], writes=[pdraw.k])
                                kb.op("act", lambda e: e.copy(out=raw[:, :, 1:TS + 1], in_=ps.t[:, :n].rearrange("p (s t) -> p s t", t=TS)),
                                      reads=[ps.k], writes=[pdraw.k])
                                cur, prv = raw[:, :, 1:TS + 1], raw[:, :, 0:TS]
                                v3 = lambda ap: ap.rearrange("p (s t) -> p s t", t=TS)
                                kb.op("dve", lambda e: e.tensor_copy(out=shs.t[:, pc, :], in_=raw[:, :, TS]), reads=[pdraw.k], writes=[shs.k])
                            else:
                                kb.op("dve", lambda e: e.tensor_copy(out=pdraw.t[:, 0:1], in_=carry.t[:, pc:pc + 1]), reads=[carry.k], writes=[pdraw.k])
                                kb.op("act", lambda e: e.copy(out=pdraw.t[:, 1:n + 1], in_=ps.t[:, :n]), reads=[ps.k], writes=[pdraw.k])
                                cur, prv = pdraw.t[:, 1:n + 1], pdraw.t[:, 0:n]
                                v3 = lambda ap: ap
                                kb.op("dve", lambda e: e.tensor_copy(out=carry.t[:, pc:pc + 1], in_=pdraw.t[:, n:n + 1]), reads=[pdraw.k], writes=[carry.k])
                            mu_c = prm.t[:, PRM["mu"] + pc:PRM["mu"] + pc + 1]
                            dstb = {0: "r", 1: "k", 2: "v"}.get(pc // 4) if pc < 12 else None
                            dst = R[dstb] if dstb else (p28 if pc == 12 else p29)
                            kb.op("dve", lambda e: e.tensor_scalar(out=v3(dst.t[:, :n]), in0=cur, scalar1=omu.t[:, pc:pc + 1], scalar2=None, op0=ALU.mult),
                                  reads=[pdraw.k, omu.k], writes=[dst.k])
                            kb.op("dve", lambda e: e.scalar_tensor_tensor(out=v3(dst.t[:, :n]), in0=prv, scalar=mu_c, in1=v3(dst.t[:, :n]), op0=ALU.mult, op1=ALU.add),
                                  reads=[pdraw.k, prm.k, dst.k], writes=[dst.k])
                            if pc == 12:
                                kb.op("act", lambda e: e.activation(out=p28.t[0:64, :n], in_=p28.t[0:64, :n], func=AF.Tanh), reads=[p28.k], writes=[p28.k])
                            elif pc == 13:
                                kb.op("act", lambda e: e.activation(out=p29.t[:, :n], in_=p29.t[:, :n], func=AF.Sigmoid), reads=[p29.k], writes=[p29.k])
                            if pc >= 12 or pc // 4 != 2:
                                continue
                            pr = pc % 4

                            def do_pair(pr=pr, n=n, is_s=is_s):
                                csl = slice(pr * 128, (pr + 1) * 128)
                                psw = psum.next()
                                kb.op("pe", lambda e: e.matmul(psw.t[:, :n], w2a2.t[0:64, csl], p28.t[0:64, :n], start=True, stop=True),
                                      reads=[w2a2.k, p28.k], writes=[psw.k])
                                kb.op("act", lambda e: e.activation(out=R["lw"].t[:, :n], in_=psw.t[:, :n], func=AF.Sigmoid,
                                                                    bias=prm.t[:, PRM["w0"] + pr:PRM["w0"] + pr + 1], scale=1.0),
                                      reads=[psw.k, prm.k], writes=[R["lw"].k])
                                psa_ = psum.next()
                                kb.op("pe", lambda e: e.matmul(psa_.t[:, :n], w2a2.t[64:128, csl], p28.t[64:128, :n], start=True, stop=True),
                                      reads=[w2a2.k, p28.k], writes=[psa_.k])
                                kb.op("act", lambda e: e.activation(out=R["a"].t[:, :n], in_=psa_.t[:, :n], func=AF.Sigmoid,
                                                                    bias=prm.t[:, PRM["a0"] + pr:PRM["a0"] + pr + 1], scale=1.0),
                                      reads=[psa_.k, prm.k], writes=[R["a"].k])
                                kb.op("dve", lambda e: e.tensor_scalar(out=R["kk"].t[:, :n], in0=R["k"].t[:, :n], scalar1=prm.t[:, PRM["kk"] + pr:PRM["kk"] + pr + 1],
                                                                       scalar2=None, op0=ALU.mult), reads=[R["k"].k, prm.k], writes=[R["kk"].k])
                                kb.op("act", lambda e: e.activation(out=tmpb.t[:, :n], in_=R["kk"].t[:, :n], func=AF.Square), reads=[R["kk"].k], writes=[tmpb.k])
                                psn_ = psum.next()
                                kb.op("pe", lambda e: e.matmul(psn_.t[:, :n], co.t[:, CO2["blk1"]:CO2["blk1"] + 128], tmpb.t[:, :n], start=True, stop=True),
                                      reads=[co.k, tmpb.k], writes=[psn_.k])
                                kb.op("act", lambda e: e.activation(out=tmpb.t[:, :n], in_=psn_.t[:, :n], func=AF.Sqrt), reads=[psn_.k], writes=[tmpb.k])
                                kb.op("dve", lambda e: e.tensor_scalar(out=tmpb.t[:, :n], in0=tmpb.t[:, :n], scalar1=1e-12, scalar2=None, op0=ALU.max),
                                      reads=[tmpb.k], writes=[tmpb.k])
                                kb.op("dve", lambda e: e.reciprocal(out=tmpb.t[:, :n], in_=tmpb.t[:, :n]), reads=[tmpb.k], writes=[tmpb.k])
                                kb.op("dve", lambda e: e.tensor_tensor(out=R["kk"].t[:, :n], in0=R["kk"].t[:, :n], in1=tmpb.t[:, :n], op=ALU.mult),
                                      reads=[R["kk"].k, tmpb.k], writes=[R["kk"].k])
                                kb.op("dve", lambda e: e.tensor_scalar(out=tmpb.t[:, :n], in0=R["a"].t[:, :n], scalar1=-1.0, scalar2=prm.t[:, PRM["ka"] + pr:PRM["ka"] + pr + 1],
                                                                       op0=ALU.add, op1=ALU.mult), reads=[R["a"].k, prm.k], writes=[tmpb.k])
                                kb.op("dve", lambda e: e.scalar_tensor_tensor(out=R["k"].t[:, :n], in0=tmpb.t[:, :n], scalar=1.0, in1=R["k"].t[:, :n], op0=ALU.add, op1=ALU.mult),
                                      reads=[tmpb.k, R["k"].k], writes=[R["k"].k])
                                kb.op("dve", lambda e: e.scalar_tensor_tensor(out=tmpb.t[:, :n], in0=R["r"].t[:, :n], scalar=prm.t[:, PRM["rk"] + pr:PRM["rk"] + pr + 1],
                                                                              in1=R["k"].t[:, :n], op0=ALU.mult, op1=ALU.mult), reads=[R["r"].k, R["k"].k, prm.k], writes=[tmpb.k])
                                psb_ = psum.next()
                                kb.op("pe", lambda e: e.matmul(psb_.t[:, :n], co.t[:, CO2["blk1"]:CO2["blk1"] + 128], tmpb.t[:, :n], start=True, stop=True),
                                      reads=[co.k, tmpb.k], writes=[psb_.k])
                                kb.op("dve", lambda e: e.tensor_tensor(out=R["bon"].t[:, :n], in0=psb_.t[:, :n], in1=R["v"].t[:, :n], op=ALU.mult),
                                      reads=[psb_.k, R["v"].k], writes=[R["bon"].k])
                                kb.op("dve", lambda e: e.tensor_scalar(out=R["lw"].t[:, :n], in0=R["lw"].t[:, :n], scalar1=-WDEC, scalar2=None, op0=ALU.mult),
                                      reads=[R["lw"].k], writes=[R["lw"].k])
                                rm_off = CO["rmask_s"] if is_s else CO["rmask_p"]
                                bbuf = R["e1"]
                                kb.op("dve", lambda e: e.tensor_tensor_scan(out=osb.t[:, :n], data0=cst.t[:, rm_off:rm_off + n], data1=R["lw"].t[:, :n],
                                                                            initial=0.0, op0=ALU.mult, op1=ALU.add), reads=[cst.k, R["lw"].k], writes=[osb.k])
                                if not is_s:
                                    nch = n // 128
                                    b3 = osb.t[:, :n].rearrange("p (c t) -> p c t", t=128)
                                    kb.op("dve", lambda e: e.tensor_copy(out=bm.t[:, 0:nch], in_=b3[:, :, 63]), reads=[osb.k], writes=[bm.k])
                                    kb.op("dve", lambda e: e.tensor_tensor(out=b3, in0=b3, in1=bm.t[:, 0:nch].unsqueeze(2).to_broadcast([128, nch, 128]), op=ALU.subtract),
                                          reads=[osb.k, bm.k], writes=[osb.k])
                                    kb.op("act", lambda e: e.activation(out=ebm.t[:, 0:nch], in_=bm.t[:, 0:nch], func=AF.Exp), reads=[bm.k], writes=[ebm.k])
                                kb.op("act", lambda e: e.activation(out=R["e1"].t[:, :n], in_=osb.t[:, :n], func=AF.Exp), reads=[osb.k], writes=[R["e1"].k])
                                kb.op("act", lambda e: e.activation(out=tmpb.t[:, :n], in_=osb.t[:, :n], func=AF.Exp, scale=-1.0), reads=[osb.k], writes=[tmpb.k])
                                kb.op("dve", lambda e: e.tensor_tensor(out=R["lw"].t[:, :n], in0=osb.t[:, :n], in1=R["lw"].t[:, :n], op=ALU.subtract),
                                      reads=[osb.k, R["lw"].k], writes=[R["lw"].k])
                                kb.op("act", lambda e: e.activation(out=R["lw"].t[:, :n], in_=R["lw"].t[:, :n], func=AF.Exp), reads=[R["lw"].k], writes=[R["lw"].k])
                                kb.op("dve", lambda e: e.tensor_tensor(out=R["r"].t[:, :n], in0=R["r"].t[:, :n], in1=R["e1"].t[:, :n], op=ALU.mult),
                                      reads=[R["r"].k, R["e1"].k], writes=[R["r"].k])
                                kb.op("dve", lambda e: e.tensor_tensor(out=R["k"].t[:, :n], in0=R["k"].t[:, :n], in1=tmpb.t[:, :n], op=ALU.mult),
                                      reads=[R["k"].k, tmpb.k], writes=[R["k"].k])
                                kb.op("dve", lambda e: e.tensor_tensor(out=R["bt"].t[:, :n], in0=R["kk"].t[:, :n], in1=R["a"].t[:, :n], op=ALU.mult),
                                      reads=[R["kk"].k, R["a"].k], writes=[R["bt"].k])
                                kb.op("dve", lambda e: e.tensor_tensor(out=R["bt"].t[:, :n], in0=R["bt"].t[:, :n], in1=tmpb.t[:, :n], op=ALU.mult),
                                      reads=[R["bt"].k, tmpb.k], writes=[R["bt"].k])
                                kb.op("dve", lambda e: e.scalar_tensor_tensor(out=R["at"].t[:, :n], in0=R["kk"].t[:, :n], scalar=-1.0, in1=R["lw"].t[:, :n], op0=ALU.mult, op1=ALU.mult),
                                      reads=[R["kk"].k, R["lw"].k], writes=[R["at"].k])
                                hmo = CO2["hmask"]
                                for nm, src, col in (("atA", "at", 0), ("atB", "at", 1), ("rA", "r", 0), ("rB", "r", 1)):
                                    kb.op("dve", lambda e: e.tensor_scalar(out=R[nm].t[:, :n], in0=R[src].t[:, :n], scalar1=co.t[:, hmo + col:hmo + col + 1], scalar2=None, op0=ALU.mult),
                                          reads=[R[src].k, co.k], writes=[R[nm].k])
                                psO = psacc.next()
                                if is_s:
                                    rw_sample(pr, psO)
                                    kb.op("dve", lambda e: e.tensor_tensor(out=osb.t[:, :n], in0=psO.t[:, :n], in1=tmpb.t[:, :n], op=ALU.add),
                                          reads=[psO.k, tmpb.k], writes=[osb.k])
                                else:
                                    rw_prompt(pr, n, psO)
                                    kb.op("act", lambda e: e.copy(out=osb.t[:, :n], in_=psO.t[:, :n]), reads=[psO.k], writes=[osb.k])
                                blk = co.t[:, CO2["blk1"]:CO2["blk1"] + 128]
                                psm = psum.next()
                                kb.op("pe", lambda e: e.matmul(psm.t[:, :n], blk, osb.t[:, :n], start=True, stop=True), reads=[co.k, osb.k], writes=[psm.k])
                                kb.op("dve", lambda e: e.scalar_tensor_tensor(out=osb.t[:, :n], in0=psm.t[:, :n], scalar=-1.0 / 64, in1=osb.t[:, :n], op0=ALU.mult, op1=ALU.add),
                                      reads=[psm.k, osb.k], writes=[osb.k])
                                kb.op("act", lambda e: e.activation(out=tmpb.t[:, :n], in_=osb.t[:, :n], func=AF.Square), reads=[osb.k], writes=[tmpb.k])
                                psv = psum.next()
                                kb.op("pe", lambda e: e.matmul(psv.t[:, :n], blk, tmpb.t[:, :n], start=True, stop=True), reads=[co.k, tmpb.k], writes=[psv.k])
                                kb.op("act", lambda e: e.activation(out=tmpb.t[:, :n], in_=psv.t[:, :n], func=AF.Ln, bias=gneps.t[:, 0:1], scale=1.0 / 64),
                                      reads=[psv.k, gneps.k], writes=[tmpb.k])
                                kb.op("act", lambda e: e.activation(out=tmpb.t[:, :n], in_=tmpb.t[:, :n], func=AF.Exp, scale=-0.5), reads=[tmpb.k], writes=[tmpb.k])
                                kb.op("dve", lambda e: e.tensor_tensor(out=osb.t[:, :n], in0=osb.t[:, :n], in1=tmpb.t[:, :n], op=ALU.mult),
                                      reads=[osb.k, tmpb.k], writes=[osb.k])
                                kb.op("dve", lambda e: e.tensor_scalar(out=osb.t[:, :n], in0=osb.t[:, :n], scalar1=prm.t[:, PRM["lg"] + pr:PRM["lg"] + pr + 1],
                                                                       scalar2=prm.t[:, PRM["lbb"] + pr:PRM["lbb"] + pr + 1], op0=ALU.mult, op1=ALU.add),
                                      reads=[osb.k, prm.k], writes=[osb.k])
                                kb.op("dve", lambda e: e.tensor_tensor(out=osb.t[:, :n], in0=osb.t[:, :n], in1=R["bon"].t[:, :n], op=ALU.add),
                                      reads=[osb.k, R["bon"].k], writes=[osb.k])
                                psg = psum.next()
                                kb.op("pe", lambda e: e.matmul(psg.t[:, :n], g2sb.t[:, csl], p29.t[:, :n], start=True, stop=True), reads=[g2sb.k, p29.k], writes=[psg.k])
                                kb.op("dve", lambda e: e.tensor_tensor(out=yT.t[:, 4 + pr, :n], in0=osb.t[:, :n], in1=psg.t[:, :n], op=ALU.mult),
                                      reads=[osb.k, psg.k], writes=[yT.k])

                            if is_s:
                                ret_head(pr, sets[pr % 2], n, True)
                                do_pair()
                            else:
                                kb.interleave([(lambda hd_=pr: ret_head(hd_, sets[hd_ % 2], n, False)), do_pair], quantum=3)
                    def radd(oc, ps):
                        kb.op("dve", lambda e: e.scalar_tensor_tensor(out=xT[:, oc, c0:c1], in0=ps.t[:, :n], scalar=1.0, in1=xT[:, oc, c0:c1], op0=ALU.mult, op1=ALU.add),
                              reads=[ps.k, xk[oc][ti]], writes=[xk[oc][ti]])
                    li = proj_fm(s, li, 4, 256, lambda kc: yT.t[:, kc, :n], yT.k, n, radd)
                    if c1 == SEQ:
                        for hd in range(4):
                            kb.dma("sp", out=ret_p[hd], in_=Hret[hd].t[:], reads=[Hret[hd].k])
                        for pr in range(4):
                            pst = psum.next()
                            kb.op("pe", lambda e: e.transpose(pst.t[:, 0:128], Hblk[pr].t[:], ident.t), reads=[Hblk[pr].k, ident.k], writes=[pst.k])
                            so = Hs.next()
                            kb.op("act", lambda e: e.copy(out=so.t[:], in_=pst.t[:, 0:128]), reads=[pst.k], writes=[so.k])
                            for h in range(2):
                                kb.dma("sp", out=rwkv_p[2 * pr + h], in_=so.t[h * 64:(h + 1) * 64, h * 64:(h + 1) * 64], reads=[so.k])
                        kb.dma("sp", out=shift_p.rearrange("(c p) -> p c", p=128), in_=carry.t[:, :], reads=[carry.k], allow_slow_non_contiguous=True)
                for c in range(14):
                    kb.dma("sp", out=shift_s[:, c * 128:(c + 1) * 128].rearrange("s p -> p s"), in_=shs.t[:, c, :], reads=[shs.k],
                           allow_slow_non_contiguous=True)
                kb.barrier()

        def phase_mix(l):
            if l == 0:
                phase_mix_even()
            else:
                phase_mix_odd()

        phase_load()
        for sg in stages:
            if sg == "final":
                continue
            name, l = sg.rsplit("_", 1)
            l = int(l)
            if name == "ffn1":
                phase_ffn(0, l)
            elif name == "ffn2":
                phase_ffn(1, l)
            elif name == "xattn":
                phase_xattn(l)
            elif name == "mix":
                phase_mix(l)
        phase_final("final" in stages)
        kb.finish()
        print("instructions:", kb.nins)
    return nc, declared


_CACHE = {}


def _get_nc(stages):
    if stages not in _CACHE:
        _CACHE[stages] = build(stages)
    return _CACHE[stages]


def make_in_maps(inp):
    hc = host_consts()
    f32 = lambda a: np.ascontiguousarray(a, dtype=np.float32)
    gains = f32(np.concatenate([inp["ffn1_norm"], inp["mix_norm"], inp["xattn_norm"], inp["ffn2_norm"],
                                inp["final_norm"][None, :], inp["mem_norm"]], axis=0))
    shared = {"consts": hc["consts"], "gains": gains}
    for f in (1, 2):
        for l in range(2):
            shared["ffn%d_w_gu_%d" % (f, l)] = f32(inp["ffn%d_w_gu" % f][l])
            shared["ffn%d_w_down_%d" % (f, l)] = f32(inp["ffn%d_w_down" % f][l])
    for l in range(2):
        shared["wq_%d" % l] = f32(inp["xattn_wq"][l])
        shared["wkv_%d" % l] = f32(inp["xattn_wkv"][l])
        shared["wo_%d" % l] = f32(inp["xattn_wo"][l])
    shared.update(host_consts_odd())
    shared["odd_w_in"] = f32(inp["odd_w_in"][0])
    shared["odd_w_out"] = f32(inp["odd_w_out"][0])
    for nm in ("ret_gnorm", "rwkv_mu", "rwkv_w0", "rwkv_w2", "rwkv_a0", "rwkv_a2", "rwkv_g2", "rwkv_k_k", "rwkv_k_a", "rwkv_lnx_g", "rwkv_lnx_b"):
        shared[nm] = f32(inp[nm][0])
    shared["rwkv_r_k"] = f32(inp["rwkv_r_k"][0].reshape(512))
    shared["even_w_in"] = f32(inp["even_w_in"][0])
    shared["even_w_out"] = f32(inp["even_w_out"][0])
    shared["conv_w"] = f32(inp["conv_w"][0])
    shared["hgrn_lb"] = f32(inp["hgrn_lb"])
    shared["hgrn_gnorm"] = f32(inp["hgrn_gnorm"][0])
    maps = []
    for c in range(NCORES):
        sl = slice(NS * c, NS * (c + 1))
        m = dict(shared)
        m["state_ret"] = f32(inp["state_ret"][0, sl])
        m["state_rwkv"] = f32(inp["state_rwkv"][0, sl])
        m["state_shift"] = f32(inp["state_shift"][0, sl])
        m["state_conv"] = f32(inp["state_conv"][0, sl])
        m["state_hgrn"] = f32(inp["state_hgrn"][0, sl])
        m["x_prompt"] = f32(inp["x_prompt"][c])
        m["x_sample"] = f32(inp["x_sample"][sl].reshape(NS * TS, D))
        m["mem_prompt"] = f32(inp["mem_prompt"][c])
        m["cache_k"] = f32(inp["cache_mem_k"][:, sl].reshape(2, NS, NMEM, D))
        m["cache_v"] = f32(inp["cache_mem_v"][:, sl].reshape(2, NS, NMEM, D))
        maps.append(m)
    return maps


def run(inp, stages=ALL_STAGES):
    nc, declared = _get_nc(tuple(stages))
    maps = [{k: m[k] for k in declared} for m in make_in_maps(inp)]
    res = run_bass_kernel_spmd(nc, maps, core_ids=list(range(NCORES)))
    return res.results


def kernel(**inp):
    inp = {k: np.asarray(v) for k, v in inp.items()}
    r = run(inp)
    cat = lambda name, shp: np.concatenate([np.asarray(r[c][name], np.float32).reshape(shp) for c in range(NCORES)], axis=0)
    y_prompt = cat("y_prompt", (1, SEQ, D))
    y_sample = cat("y_sample", (NS, TS, D))
    conv_p = cat("conv_p", (1, 2, 512))[None]
    hgrn_p = cat("hgrn_p", (1, 4, 128, 128))[None]
    ret_p = cat("ret_p", (1, 4, 128, 128))[None]
    rwkv_p = cat("rwkv_p", (1, 8, 64, 64))[None]
    shift_p = cat("shift_p", (1, 1792))[None]
    mem_k_p = np.stack([np.asarray(r[c]["mem_k_p"], np.float32).reshape(2, NMEM, 4, 256) for c in range(NCORES)], axis=1)
    mem_v_p = np.stack([np.asarray(r[c]["mem_v_p"], np.float32).reshape(2, NMEM, 4, 256) for c in range(NCORES)], axis=1)
    conv_s = cat("conv_s", (NS, 2, 512))[None]
    hgrn_s = cat("hgrn_s", (NS, 4, 128, 128))[None]
    ret_s = cat("ret_s", (NS, 4, 128, 128))[None]
    rwkv_s = cat("rwkv_s", (NS, 8, 64, 64))[None]
    shift_s = cat("shift_s", (NS, 1792))[None]
    return (y_prompt, y_sample, conv_p, hgrn_p, ret_p, rwkv_p, shift_p, mem_k_p, mem_v_p,
            conv_s, hgrn_s, ret_s, rwkv_s, shift_s)
```

```python
import numpy as np
from contextlib import ExitStack
import concourse.bass as bass
import concourse.mybir as mybir
from concourse.bass_utils import run_bass_kernel_spmd

F32 = mybir.dt.float32
BF16 = mybir.dt.bfloat16
AF = mybir.ActivationFunctionType
ALU = mybir.AluOpType

NCORES = 8
D = 1024
KC = 8
SEQ = 2048
NS = 16
TS = 8
NTOK = SEQ + NS * TS
DFF = 2816
JC = DFF // 128
NMEM = 256
EPS = 1e-6
PAST = 16384
TILES = [(0, 512), (512, 1024), (1024, 1536), (1536, 2048), (2048, 2176)]
NDS = 48
WBUF = 2816
NWB = 4


class Trk:
    __slots__ = ("w", "r", "excl")

    def __init__(self, excl=False):
        self.w = None
        self.r = {}
        self.excl = excl


class Eng:
    def __init__(self, name, e, sem, idx):
        self.name, self.e, self.sem, self.idx = name, e, sem, idx
        self.cnt = 0
        self.waited = {}


class KB:
    def __init__(self, nc, st):
        self.nc = nc
        self.sems = []
        self.E = {}
        for n, e in (("pe", nc.tensor), ("act", nc.scalar), ("dve", nc.vector), ("pool", nc.gpsimd), ("sp", nc.sync)):
            sem = st.enter_context(nc.semaphore("sem_" + n))
            self.E[n] = Eng(n, e, sem, len(self.sems))
            self.sems.append(sem)
        self.dbase = len(self.sems)
        for i in range(NDS):
            self.sems.append(st.enter_context(nc.semaphore("dsem%d" % i)))
        self.dval = [0] * NDS
        self.task = None
        self.hook = None
        self.opcount = 0
        self.dnext = {"hw": 0, "sw": NDS // 2}
        self.nins = 0

    def _deps(self, E, reads, writes):
        evs = []
        for t in reads:
            if t.w is not None:
                evs.append(t.w)
            if t.excl:
                for s, v in t.r.items():
                    if s != E.idx:
                        evs.append((s, v))
        own_ok = (E.name == "pe")
        for t in writes:
            if t.w is not None and not (own_ok and t.w[0] == E.idx):
                evs.append(t.w)
            for s, v in t.r.items():
                if not (own_ok and s == E.idx):
                    evs.append((s, v))
        return evs

    def _wait(self, E, evs):
        need = {}
        for s, v in evs:
            if s == E.idx and v > E.cnt:
                continue
            if need.get(s, 0) < v:
                need[s] = v
        for s, v in need.items():
            if E.waited.get(s, 0) < v:
                E.e.wait_ge(self.sems[s], v)
                E.waited[s] = v

    def _record(self, ev, reads, writes):
        for t in reads:
            if t.r.get(ev[0], 0) < ev[1]:
                t.r[ev[0]] = ev[1]
        for t in writes:
            t.w = ev
            t.r = {}

    def op(self, en, fn, reads=(), writes=(), inc=True):
        E = self.E[en]
        self._wait(E, self._deps(E, reads, writes))
        ins = fn(E.e)
        ev = (E.idx, E.cnt + 1)
        if inc:
            ins.then_inc(E.sem, 1)
            E.cnt += 1
        self._record(ev, reads, writes)
        self.nins += 1
        if self.hook is not None:
            self.opcount += 1
            if self.opcount >= self.quantum:
                self.opcount = 0
                self.hook()
        return ins

    def interleave(self, tasks, quantum=4):
        import threading
        n = len(tasks)
        sems = [threading.Semaphore(0) for _ in range(n)]
        main = threading.Semaphore(0)
        alive = [True] * n
        errs = []

        def nxt(i):
            for d in range(1, n + 1):
                j = (i + d) % n
                if alive[j]:
                    return j
            return None

        def hook():
            i = self.task
            j = nxt(i)
            if j is None or j == i:
                return
            self.task = j
            sems[j].release()
            sems[i].acquire()

        def runner(i):
            sems[i].acquire()
            try:
                tasks[i]()
            except BaseException as ex:
                errs.append(ex)
            alive[i] = False
            j = nxt(i)
            if j is None:
                main.release()
            else:
                self.task = j
                sems[j].release()

        ths = [threading.Thread(target=runner, args=(i,)) for i in range(n)]
        for t in ths:
            t.start()
        self.quantum = quantum
        self.opcount = 0
        self.hook = hook
        self.task = 0
        sems[0].release()
        main.acquire()
        for t in ths:
            t.join()
        self.hook = None
        self.task = None
        if errs:
            raise errs[0]

    def dma(self, qn, out, in_, reads=(), writes=(), **kw):
        E = self.E[qn]
        evs = self._deps(E, reads, writes)
        half = NDS // 2
        if qn == "pool":
            i = self.dnext["sw"]
            self.dnext["sw"] = half + (i + 1 - half) % half
        else:
            i = self.dnext["hw"]
            self.dnext["hw"] = (i + 1) % half
        sidx = self.dbase + i
        if self.dval[i] > 0:
            evs.append((sidx, self.dval[i]))
        self._wait(E, evs)
        ins = E.e.dma_start(out=out, in_=in_, **kw)
        self.dval[i] += 16
        ins.then_inc(self.sems[sidx], 16)
        self._record((sidx, self.dval[i]), reads, writes)
        self.nins += 1
        return ins

    def barrier(self):
        for E in self.E.values():
            evs = [(F.idx, F.cnt) for F in self.E.values() if F is not E and F.cnt > 0]
            evs += [(self.dbase + i, self.dval[i]) for i in range(NDS) if self.dval[i] > 0]
            self._wait(E, evs)

    def finish(self):
        E = self.E["sp"]
        evs = [(F.idx, F.cnt) for F in self.E.values() if F is not E and F.cnt > 0]
        evs += [(self.dbase + i, self.dval[i]) for i in range(NDS) if self.dval[i] > 0]
        self._wait(E, evs)


class Buf:
    def __init__(self, t, excl=False):
        self.t = t
        self.k = Trk(excl)


class Stream:
    def __init__(self, kb, pool, loads, queue="pool"):
        self.kb, self.pool, self.loads, self.queue = kb, pool, loads, queue
        self.issued = 0
        self.slot = {}

    def get(self, i):
        kb = self.kb
        nb = len(self.pool.bufs)
        while self.issued < min(len(self.loads), i + nb - 1):
            k = self.issued
            b = self.pool.next()
            view, src = self.loads[k]
            kb.dma(self.queue, out=view(b.t), in_=src, writes=[b.k])
            self.slot[k] = b
            self.issued += 1
        return self.slot.pop(i)


class Ring:
    def __init__(self, bufs, kb=None):
        self.bufs = bufs
        self.i = 0
        self.kb = kb
        self.ti = [0, 0]

    def next(self):
        if self.kb is not None and self.kb.task is not None:
            t = self.kb.task % 2
            lo, cnt = (0, 2) if t == 0 else (2, len(self.bufs) - 2)
            b = self.bufs[lo + self.ti[t]]
            self.ti[t] = (self.ti[t] + 1) % cnt
            return b
        b = self.bufs[self.i]
        self.i = (self.i + 1) % len(self.bufs)
        return b


ALL_STAGES = ("ffn1_0", "mix_0", "xattn_0", "ffn2_0", "ffn1_1", "mix_1", "xattn_1", "ffn2_1", "final")
GI = {"ffn1": 0, "mix": 2, "xattn": 4, "ffn2": 6, "final": 8, "mem": 9}
NG = 11


CO = {"ident": 0, "causalT": 128, "blkmask": 256, "seqmask": 384, "rmask_p": 400, "rmask_s": 912, "rmask_p32": 1040}
NCONST = 1552
GCH = 64


CO2 = {}
_o = 0
for _nm, _w in (("perm", 128), ("ret_dm_p", 512), ("ret_dm_s", 512), ("ret_qd_p", 512), ("ret_qd_s", 512), ("ret_kd", 8), ("blk1", 128),
                ("strictT", 128), ("strictL", 128), ("bstrictT", 128), ("bstrictL", 128), ("hmask", 2)):
    CO2[_nm] = _o
    _o += _w
NCO = _o


def host_consts_odd():
    m = np.zeros((128, NCO), np.float64)
    i = np.arange(128)
    m[:, CO2["perm"]:CO2["perm"] + 128] = (np.abs(i[:, None] - i[None, :]) == 64)
    same = (i[:, None] // TS == i[None, :] // TS)
    for h in range(4):
        lg = np.log1p(-2.0 ** (-5.0 - h))
        d = (i[None, :] - i[:, None]).astype(np.float64)
        m[:, CO2["ret_dm_p"] + h * 128:CO2["ret_dm_p"] + (h + 1) * 128] = np.where(d >= 0, np.exp(lg * np.maximum(d, 0)), 0.0)
        m[:, CO2["ret_dm_s"] + h * 128:CO2["ret_dm_s"] + (h + 1) * 128] = np.where((d >= 0) & same, np.exp(lg * np.maximum(d, 0)), 0.0)
        m[:, CO2["ret_qd_p"] + h * 128:CO2["ret_qd_p"] + (h + 1) * 128] = np.exp(lg * (i + 1.0))[None, :]
        m[:, CO2["ret_qd_s"] + h * 128:CO2["ret_qd_s"] + (h + 1) * 128] = np.exp(lg * (i % TS + 1.0))[None, :]
        m[:, CO2["ret_kd"] + h] = np.exp(lg * (127.0 - i))
        m[:, CO2["ret_kd"] + 4 + h] = np.exp(lg * (TS - 1.0 - i % TS))
    m[:, CO2["blk1"]:CO2["blk1"] + 128] = (i[:, None] // 64 == i[None, :] // 64)
    m[:, CO2["strictT"]:CO2["strictT"] + 128] = (i[:, None] < i[None, :])
    m[:, CO2["strictL"]:CO2["strictL"] + 128] = (i[:, None] > i[None, :])
    m[:, CO2["bstrictT"]:CO2["bstrictT"] + 128] = (i[:, None] < i[None, :]) & same
    m[:, CO2["bstrictL"]:CO2["bstrictL"] + 128] = (i[:, None] > i[None, :]) & same
    m[:, CO2["hmask"]] = (i < 64)
    m[:, CO2["hmask"] + 1] = (i >= 64)
    pos = np.concatenate([np.arange(SEQ), np.tile(PAST + np.arange(TS), NS)]).astype(np.float32)
    inv = (np.float32(10000.0) ** (-np.arange(64, dtype=np.float32) / np.float32(64))).astype(np.float32)
    ang = (pos[None, :] * inv[:, None]).astype(np.float32).astype(np.float64)
    rope = np.zeros((2, 128, NTOK), np.float32)
    rope[0, :64] = np.cos(ang)
    rope[0, 64:] = np.cos(ang)
    rope[1, :64] = -np.sin(ang)
    rope[1, 64:] = np.sin(ang)
    return {"consts_odd": m.astype(np.float32), "rope": rope}


def host_consts():
    m = np.zeros((128, NCONST), np.float32)
    i = np.arange(128)
    m[:, 0:128] = np.eye(128)
    m[:, 128:256] = (i[:, None] <= i[None, :])
    m[:, 256:384] = (i[:, None] <= i[None, :]) & (i[:, None] // TS == i[None, :] // TS)
    m[:, 384:400] = (i[:, None] // TS == np.arange(NS)[None, :])
    m[:, 400:912] = (np.arange(512) % 128 != 0)[None, :]
    m[:, 912:1040] = (np.arange(128) % TS != 0)[None, :]
    m[:, 1040:1552] = (np.arange(512) % GCH != 0)[None, :]
    return {"consts": m}


def build(stages=ALL_STAGES):
    nc = bass.Bass("TRN2", target_bir_lowering=False)

    declared = []
    need = set(sg.rsplit("_", 1)[0] for sg in stages)
    import os
    kdbg = os.environ.get("KDBG", "")
    if "kv" in kdbg:
        need.add("xattn")

    def din(name, shape, dt=F32, when=None):
        if when is not None and when not in need:
            return None
        declared.append(name)
        return nc.dram_tensor(name, list(shape), dt, kind="ExternalInput").ap()

    def dout(name, shape):
        return nc.dram_tensor(name, list(shape), F32, kind="ExternalOutput").ap()

    x_prompt = din("x_prompt", [SEQ, D])
    x_sample = din("x_sample", [NS * TS, D])
    mem_prompt = din("mem_prompt", [NMEM, D])
    cache_k = din("cache_k", [2, NS, NMEM, D], when="xattn")
    cache_v = din("cache_v", [2, NS, NMEM, D], when="xattn")
    consts_d = din("consts", [128, NCONST])
    gains_d = din("gains", [NG, D])
    w_gu = [[din("ffn%d_w_gu_%d" % (f, l), [D, 2 * DFF], when="ffn%d" % f) for l in range(2)] for f in (1, 2)]
    w_dn = [[din("ffn%d_w_down_%d" % (f, l), [DFF, D], when="ffn%d" % f) for l in range(2)] for f in (1, 2)]
    w_q = [din("wq_%d" % l, [D, D], when="xattn") for l in range(2)]
    w_kv = [din("wkv_%d" % l, [D, 2 * D], when="xattn") for l in range(2)]
    w_o = [din("wo_%d" % l, [D, D], when="xattn") for l in range(2)]
    even_w_in = din("even_w_in", [D, 3584], when="mix")
    even_w_out = din("even_w_out", [D, D], when="mix")
    conv_w = din("conv_w", [3, 512], when="mix")
    hgrn_lb = din("hgrn_lb", [2, 512], when="mix")
    hgrn_gnorm = din("hgrn_gnorm", [512], when="mix")
    state_conv = din("state_conv", [NS, 2, 512], when="mix")
    state_hgrn = din("state_hgrn", [NS, 4, 128, 128], when="mix")
    odd_w_in = din("odd_w_in", [D, 3840], when="mix")
    odd_w_out = din("odd_w_out", [D, D], when="mix")
    consts_odd_d = din("consts_odd", [128, NCO], when="mix")
    rope_d = din("rope", [2, 128, NTOK], when="mix")
    ret_gnorm = din("ret_gnorm", [512], when="mix")
    rwkv_mu = din("rwkv_mu", [1792], when="mix")
    rwkv_w0 = din("rwkv_w0", [512], when="mix")
    rwkv_w2 = din("rwkv_w2", [64, 512], when="mix")
    rwkv_a0 = din("rwkv_a0", [512], when="mix")
    rwkv_a2 = din("rwkv_a2", [64, 512], when="mix")
    rwkv_g2 = din("rwkv_g2", [128, 512], when="mix")
    rwkv_k_k = din("rwkv_k_k", [512], when="mix")
    rwkv_k_a = din("rwkv_k_a", [512], when="mix")
    rwkv_r_k = din("rwkv_r_k", [512], when="mix")
    rwkv_lnx_g = din("rwkv_lnx_g", [512], when="mix")
    rwkv_lnx_b = din("rwkv_lnx_b", [512], when="mix")
    state_ret = din("state_ret", [NS, 4, 128, 128], when="mix")
    state_rwkv = din("state_rwkv", [NS, 8, 64, 64], when="mix")
    state_shift = din("state_shift", [NS, 1792], when="mix")
    ret_p = dout("ret_p", [4, 128, 128])
    rwkv_p = dout("rwkv_p", [8, 64, 64])
    shift_p = dout("shift_p", [1792])
    ret_s = dout("ret_s", [NS, 4, 128, 128])
    rwkv_s = dout("rwkv_s", [NS, 8, 64, 64])
    shift_s = dout("shift_s", [NS, 1792])
    conv_p = dout("conv_p", [2, 512])
    hgrn_p = dout("hgrn_p", [4, 128, 128])
    conv_s = dout("conv_s", [NS, 2, 512])
    hgrn_s = dout("hgrn_s", [NS, 4, 128, 128])
    y_prompt = dout("y_prompt", [SEQ, D])
    y_sample = dout("y_sample", [NS * TS, D])
    mem_k_p = dout("mem_k_p", [2, NMEM, D])
    mem_v_p = dout("mem_v_p", [2, NMEM, D])

    with ExitStack() as st:
        kb = KB(nc, st)

        _uid = [0]

        def sb(stk, name, shape, dt=F32):
            _uid[0] += 1
            return Buf(stk.enter_context(nc.sbuf_tensor("%s_%d" % (name, _uid[0]), list(shape), dt)))

        xT = st.enter_context(nc.sbuf_tensor("xT", [128, KC, NTOK], F32))
        xk = [[Trk() for _ in TILES] for _ in range(KC)]
        cst = sb(st, "consts_sb", [128, NCONST])

        class _V:
            pass
        ident = _V()
        ident.t = cst.t[:, 0:128]
        ident.k = cst.k
        ones_bf = sb(st, "ones_bf", [128, 128], BF16)
        one1_bf = sb(st, "one1_bf", [128, 128], BF16)
        gains = sb(st, "gains_sb", [128, NG, KC])
        epsc = sb(st, "epsc", [128, 4])
        wpool = Ring([sb(st, "wb%d" % i, [128, WBUF], BF16) for i in range(NWB)])
        psum = Ring([Buf(st.enter_context(nc.psum_tensor("ps%d" % i, [128, 512], F32)), excl=True) for i in range(6)], kb=kb)
        psacc = Ring([Buf(st.enter_context(nc.psum_tensor("psa%d" % i, [128, 512], F32)), excl=True) for i in range(2)])
        sqr = Ring([sb(st, "sq%d" % i, [128, 512], BF16) for i in range(2)])
        rstd = sb(st, "rstd", [128, 512])
        kT_p = [None, None]
        v_p = [None, None]
        ones128_bf = sb(st, "ones128_bf", [128, 128], BF16)

        kb.dma("sp", out=cst.t[:], in_=consts_d[:, :], writes=[cst.k])
        for g_ in range(NG):
            kb.dma("sp", out=gains.t[:, g_, :], in_=gains_d[g_].rearrange("(k p) -> p k", p=128), writes=[gains.k],
                   allow_slow_non_contiguous=True)
        kb.op("dve", lambda e: e.memset(ones_bf.t[:], 1.0 / D), writes=[ones_bf.k])
        kb.op("dve", lambda e: e.memset(one1_bf.t[:], 1.0), writes=[one1_bf.k])
        kb.op("dve", lambda e: e.memset(ones128_bf.t[:], 1.0 / 128), writes=[ones128_bf.k])
        kb.op("dve", lambda e: e.memset(epsc.t[:, 0:1], EPS), writes=[epsc.k])

        def tile_of(col):
            for i, (a, b) in enumerate(TILES):
                if a <= col < b:
                    return i
            raise ValueError

        evac_i = [0]

        def evac_copy(out, in_, reads, writes):
            evac_i[0] ^= 1
            if evac_i[0]:
                kb.op("act", lambda e: e.copy(out=out, in_=in_), reads=reads, writes=writes)
            else:
                kb.op("dve", lambda e: e.tensor_copy(out=out, in_=in_), reads=reads, writes=writes)

        def wview(t, k, n):
            return t[:, 0:k * n].rearrange("p (k n) -> p k n", k=k)

        def rmsnorm_g(src, gi, out, out_trk, n):
            ps = psum.next()
            for kc in range(KC):
                a, t = src(kc)
                q = sqr.next()
                kb.op("act", lambda e: e.activation(out=q.t[:, :n], in_=a, func=AF.Square), reads=[t], writes=[q.k])
                kb.op("pe", lambda e: e.matmul(ps.t[:, :n], ones_bf.t[:], q.t[:, :n], start=(kc == 0), stop=(kc == KC - 1)),
                      reads=[q.k, ones_bf.k], writes=[ps.k], inc=True)
            kb.op("act", lambda e: e.activation(out=rstd.t[:, :n], in_=ps.t[:, :n], func=AF.Ln, bias=epsc.t[:, 0:1], scale=1.0),
                  reads=[ps.k, epsc.k], writes=[rstd.k])
            kb.op("act", lambda e: e.activation(out=rstd.t[:, :n], in_=rstd.t[:, :n], func=AF.Exp, scale=-0.5), reads=[rstd.k], writes=[rstd.k])
            for kc in range(KC):
                a, t = src(kc)
                kb.op("dve", lambda e: e.scalar_tensor_tensor(out=out(kc), in0=a, scalar=gains.t[:, gi, kc:kc + 1], in1=rstd.t[:, :n],
                                                              op0=ALU.mult, op1=ALU.mult),
                      reads=[t, gains.k, rstd.k], writes=[out_trk])

        def rmsnorm_x(ti, gi, out_buf):
            c0, c1 = TILES[ti]
            n = c1 - c0
            rmsnorm_g(lambda kc: (xT[:, kc, c0:c1], xk[kc][ti]), gi, lambda kc: out_buf.t[:, kc, :n], out_buf.k, n)

        def proj_fm(s, li, nblk, bw, rhs, rhs_trk, n, consume, kch=KC):
            for b in range(nblk):
                w = s.get(li + b)
                wv = wview(w.t, kch, bw)
                for jj in range(bw // 128):
                    oc = b * (bw // 128) + jj
                    ps = psum.next()
                    for kc in range(kch):
                        kb.op("pe", lambda e: e.matmul(ps.t[:, :n], wv[:, kc, jj * 128:(jj + 1) * 128], rhs(kc),
                                                       start=(kc == 0), stop=(kc == kch - 1)),
                              reads=[w.k, rhs_trk], writes=[ps.k], inc=(kc == kch - 1))
                    consume(oc, ps)
            return li + nblk

        def resid_add(ti, scale):
            c0, c1 = TILES[ti]
            n = c1 - c0

            def f(oc, ps):
                kb.op("dve", lambda e: e.scalar_tensor_tensor(out=xT[:, oc, c0:c1], in0=ps.t[:, :n], scalar=scale,
                                                              in1=xT[:, oc, c0:c1], op0=ALU.mult, op1=ALU.add),
                      reads=[ps.k, xk[oc][ti]], writes=[xk[oc][ti]])
            return f

        def phase_load():
            with ExitStack() as ph:
                xin = Ring([sb(ph, "xin%d" % i, [128, D]) for i in range(2)])
                for blk in range(NTOK // 128):
                    xi = xin.next()
                    src = x_prompt[blk * 128:(blk + 1) * 128, :] if blk < 16 else x_sample[:, :]
                    kb.dma("sp", out=xi.t[:], in_=src, writes=[xi.k])
                    ti = tile_of(blk * 128)
                    for g in range(2):
                        ps = psum.next()
                        for j in range(4):
                            kc = 4 * g + j
                            kb.op("pe", lambda e: e.transpose(ps.t[:, j * 128:(j + 1) * 128], xi.t[:, kc * 128:(kc + 1) * 128], ident.t),
                                  reads=[xi.k, ident.k], writes=[ps.k], inc=(j == 3))
                        evac_copy(xT[:, 4 * g:4 * g + 4, blk * 128:(blk + 1) * 128],
                                  ps.t[:].rearrange("p (j c) -> p j c", j=4), [ps.k], [xk[kc][ti] for kc in range(4 * g, 4 * g + 4)])
                kb.barrier()

        def phase_ffn(f, l):
            gi = GI["ffn1" if f == 0 else "ffn2"] + l
            SUPER = [[0, 1], [2, 3, 4]]
            WMAX = 1152
            with ExitStack() as ph:
                hTs = sb(ph, "hTs", [128, KC, WMAX], BF16)
                actT = sb(ph, "actT", [128, JC, WMAX], BF16)
                stmp = Ring([sb(ph, "stmp%d" % i, [128, 512]) for i in range(2)])
                hk = [Trk() for _ in TILES]
                ak = [Trk() for _ in TILES]
                gu_v = w_gu[f][l].rearrange("(k p) n -> p k n", p=128)
                dn_v = w_dn[f][l].rearrange("(j p) n -> p j n", p=128)
                loads = []
                for st_ in SUPER:
                    for b in range(JC // 2):
                        loads.append((lambda t: wview(t, KC, 256), gu_v[:, :, b * 256:(b + 1) * 256]))
                        loads.append((lambda t: wview(t, KC, 256), gu_v[:, :, DFF + b * 256:DFF + (b + 1) * 256]))
                    for nch in range(KC):
                        loads.append((lambda t: wview(t, JC, 128), dn_v[:, :, nch * 128:(nch + 1) * 128]))
                s = Stream(kb, wpool, loads)
                li = 0
                for st_ in SUPER:
                    base = TILES[st_[0]][0]
                    for ti in st_:
                        c0, c1 = TILES[ti]
                        o0 = c0 - base
                        rmsnorm_g(lambda kc: (xT[:, kc, c0:c1], xk[kc][ti]), gi, lambda kc: hTs.t[:, kc, o0:o0 + (c1 - c0)], hk[ti], c1 - c0)
                    for b in range(JC // 2):
                        wg = s.get(li)
                        wu = s.get(li + 1)
                        li += 2
                        wgv = wview(wg.t, KC, 256)
                        wuv = wview(wu.t, KC, 256)
                        for jj in range(2):
                            j = 2 * b + jj
                            for ti in st_:
                                c0, c1 = TILES[ti]
                                n = c1 - c0
                                o0 = c0 - base
                                pg = psum.next()
                                pu = psum.next()
                                for kc in range(KC):
                                    kb.op("pe", lambda e: e.matmul(pg.t[:, :n], wgv[:, kc, jj * 128:(jj + 1) * 128], hTs.t[:, kc, o0:o0 + n],
                                                                   start=(kc == 0), stop=(kc == KC - 1)),
                                          reads=[wg.k, hk[ti]], writes=[pg.k], inc=(kc == KC - 1))
                                for kc in range(KC):
                                    kb.op("pe", lambda e: e.matmul(pu.t[:, :n], wuv[:, kc, jj * 128:(jj + 1) * 128], hTs.t[:, kc, o0:o0 + n],
                                                                   start=(kc == 0), stop=(kc == KC - 1)),
                                          reads=[wu.k, hk[ti]], writes=[pu.k], inc=(kc == KC - 1))
                                sm = stmp.next()
                                kb.op("act", lambda e: e.activation(out=sm.t[:, :n], in_=pg.t[:, :n], func=AF.Silu),
                                      reads=[pg.k], writes=[sm.k])
                                kb.op("dve", lambda e: e.tensor_tensor(out=actT.t[:, j, o0:o0 + n], in0=sm.t[:, :n], in1=pu.t[:, :n], op=ALU.mult),
                                      reads=[sm.k, pu.k], writes=[ak[ti]])
                    for nch in range(KC):
                        wd = s.get(li)
                        li += 1
                        wdv = wview(wd.t, JC, 128)
                        for ti in st_:
                            c0, c1 = TILES[ti]
                            n = c1 - c0
                            o0 = c0 - base
                            po = psum.next()
                            for j in range(JC):
                                kb.op("pe", lambda e: e.matmul(po.t[:, :n], wdv[:, j, :], actT.t[:, j, o0:o0 + n], start=(j == 0), stop=(j == JC - 1)),
                                      reads=[wd.k, ak[ti]], writes=[po.k], inc=(j == JC - 1))
                            resid_add(ti, 0.5)(nch, po)
                kb.barrier()

        def phase_kvprep(layers=(0, 1)):
            with ExitStack() as ph:
                memin = sb(ph, "memin", [128, 2, D])
                memT = sb(ph, "memT", [128, KC, NMEM])
                memh = sb(ph, "memh", [128, KC, NMEM], BF16)
                kvst = [sb(ph, "kvst%d" % i, [128, 2 * D]) for i in range(2)]
                kb.dma("sp", out=memin.t[:], in_=mem_prompt.rearrange("(c p) d -> p c d", p=128), writes=[memin.k])
                for mc in range(2):
                    for g in range(2):
                        ps = psum.next()
                        for j in range(4):
                            kc = 4 * g + j
                            kb.op("pe", lambda e: e.transpose(ps.t[:, j * 128:(j + 1) * 128], memin.t[:, mc, kc * 128:(kc + 1) * 128], ident.t),
                                  reads=[memin.k, ident.k], writes=[ps.k], inc=(j == 3))
                        evac_copy(memT.t[:, 4 * g:4 * g + 4, mc * 128:(mc + 1) * 128], ps.t[:].rearrange("p (j c) -> p j c", j=4),
                                  [ps.k], [memT.k])
                for l in layers:
                    if "kv1" in kdbg:
                        break
                    rmsnorm_g(lambda kc: (memT.t[:, kc, :], memT.k), GI["mem"] + l, lambda kc: memh.t[:, kc, :], memh.k, NMEM)
                    if "kv2" in kdbg:
                        continue
                    kv_v = w_kv[l].rearrange("(k p) n -> p k n", p=128)
                    loads = [(lambda t: wview(t, KC, 256), kv_v[:, :, b * 256:(b + 1) * 256]) for b in range(8)]
                    s = Stream(kb, wpool, loads)
                    for b in range(8):
                        w = s.get(b)
                        wv = wview(w.t, KC, 256)
                        for mc in range(2):
                            ps = psum.next()
                            for kc in range(KC):
                                kb.op("pe", lambda e: e.matmul(ps.t[:, :256], memh.t[:, kc, mc * 128:(mc + 1) * 128], wv[:, kc, :],
                                                               start=(kc == 0), stop=(kc == KC - 1)),
                                      reads=[w.k, memh.k], writes=[ps.k], inc=(kc == KC - 1))
                            kb.op("act", lambda e: e.copy(out=kvst[mc].t[:, b * 256:(b + 1) * 256], in_=ps.t[:, :256]),
                                  reads=[ps.k], writes=[kvst[mc].k])
                            if b >= 4 and "kvA" not in kdbg:
                                kb.op("dve", lambda e: e.tensor_copy(out=v_p[l].t[:, mc, (b - 4) * 256:(b - 3) * 256],
                                                                     in_=kvst[mc].t[:, b * 256:(b + 1) * 256]),
                                      reads=[kvst[mc].k], writes=[v_p[l].k])
                        if b < 4 and "kvB" not in kdbg:
                            for jj in range(2):
                                nch = 2 * b + jj
                                ps = psum.next()
                                for kc in range(KC):
                                    kb.op("pe", lambda e: e.matmul(ps.t[:, :256], wv[:, kc, jj * 128:(jj + 1) * 128], memh.t[:, kc, :],
                                                                   start=(kc == 0), stop=(kc == KC - 1)),
                                          reads=[w.k, memh.k], writes=[ps.k], inc=(kc == KC - 1))
                                kb.op("dve", lambda e: e.tensor_copy(out=kT_p[l].t[:, nch, :], in_=ps.t[:, :256]),
                                      reads=[ps.k], writes=[kT_p[l].k])
                    if "kv3" in kdbg:
                        continue
                    for mc in range(2):
                        kb.dma("sp", out=mem_k_p[l, mc * 128:(mc + 1) * 128, :], in_=kvst[mc].t[:, 0:D], reads=[kvst[mc].k])
                        kb.dma("sp", out=mem_v_p[l, mc * 128:(mc + 1) * 128, :], in_=kvst[mc].t[:, D:2 * D], reads=[kvst[mc].k])
                kb.barrier()

        def phase_xattn(l):
            gi = GI["xattn"] + l
            with ExitStack() as ph:
                hT = sb(ph, "hT", [128, KC, 512], BF16)
                kT_p[l] = sb(ph, "kT_p", [128, KC, NMEM], BF16)
                v_p[l] = sb(ph, "v_p", [128, 2, D], BF16)
                phase_kvprep((l,))
                qT = sb(ph, "qT", [128, KC, 512], BF16)
                oT = sb(ph, "oT", [128, KC, 512], BF16)
                pT = Ring([sb(ph, "pT%d" % i, [128, 2, 512], BF16) for i in range(2)])
                rinv = Ring([sb(ph, "rinv%d" % i, [128, 512]) for i in range(2)])
                kin = Ring([sb(ph, "kin%d" % i, [128, 2, D]) for i in range(2)])
                vs = Ring([sb(ph, "vs%d" % i, [128, 2, D], BF16) for i in range(2)])
                kTs = Ring([sb(ph, "kTs%d" % i, [128, KC, NMEM], BF16) for i in range(2)])
                pTs = Ring([sb(ph, "pTs%d" % i, [128, 64], BF16) for i in range(2)])
                rinvs = Ring([sb(ph, "rinvs%d" % i, [128, 32]) for i in range(2)])
                q_v = w_q[l].rearrange("(k p) n -> p k n", p=128)
                o_v = w_o[l].rearrange("(k p) n -> p k n", p=128)
                loads = []
                for ti in range(len(TILES)):
                    for b in range(4):
                        loads.append((lambda t: wview(t, KC, 256), q_v[:, :, b * 256:(b + 1) * 256]))
                    for b in range(4):
                        loads.append((lambda t: wview(t, KC, 256), o_v[:, :, b * 256:(b + 1) * 256]))
                s = Stream(kb, wpool, loads)
                li = 0
                sc = float(256 ** -0.5)
                for ti, (c0, c1) in enumerate(TILES):
                    n = c1 - c0
                    rmsnorm_x(ti, gi, hT)

                    def q_evac(oc, ps):
                        evac_copy(qT.t[:, oc, :n], ps.t[:, :n], [ps.k], [qT.k])
                    li = proj_fm(s, li, 4, 256, lambda kc: hT.t[:, kc, :n], hT.k, n, q_evac)
                    import os
                    if c0 >= SEQ and os.environ.get('XSKIP'):
                        kb.op('dve', lambda e: e.memset(oT.t[:, :, :n], 0.0), writes=[oT.k])
                    elif c0 < SEQ:
                        for hd in range(4):
                            p = pT.next()
                            for mc in range(2):
                                ps = psum.next()
                                for dc in range(2):
                                    kb.op("pe", lambda e: e.matmul(ps.t[:, :n], kT_p[l].t[:, 2 * hd + dc, mc * 128:(mc + 1) * 128],
                                                                   qT.t[:, 2 * hd + dc, :n], start=(dc == 0), stop=(dc == 1)),
                                          reads=[kT_p[l].k, qT.k], writes=[ps.k], inc=(dc == 1))
                                kb.op("act", lambda e: e.activation(out=p.t[:, mc, :n], in_=ps.t[:, :n], func=AF.Exp, scale=sc),
                                      reads=[ps.k], writes=[p.k])
                            pss = psum.next()
                            for mc in range(2):
                                kb.op("pe", lambda e: e.matmul(pss.t[:, :n], one1_bf.t[:], p.t[:, mc, :n], start=(mc == 0), stop=(mc == 1)),
                                      reads=[one1_bf.k, p.k], writes=[pss.k], inc=(mc == 1))
                            ri = rinv.next()
                            kb.op("act", lambda e: e.activation(out=ri.t[:, :n], in_=pss.t[:, :n], func=AF.Ln), reads=[pss.k], writes=[ri.k])
                            kb.op("act", lambda e: e.activation(out=ri.t[:, :n], in_=ri.t[:, :n], func=AF.Exp, scale=-1.0), reads=[ri.k], writes=[ri.k])
                            for dc in range(2):
                                pso = psum.next()
                                for mc in range(2):
                                    kb.op("pe", lambda e: e.matmul(pso.t[:, :n], v_p[l].t[:, mc, hd * 256 + dc * 128:hd * 256 + (dc + 1) * 128],
                                                                   p.t[:, mc, :n], start=(mc == 0), stop=(mc == 1)),
                                          reads=[v_p[l].k, p.k], writes=[pso.k], inc=(mc == 1))
                                kb.op("dve", lambda e: e.tensor_tensor(out=oT.t[:, 2 * hd + dc, :n], in0=pso.t[:, :n], in1=ri.t[:, :n], op=ALU.mult),
                                      reads=[pso.k, ri.k], writes=[oT.k])
                    else:
                        for sq_ in range(NS):
                            ki = kin.next()
                            vv = vs.next()
                            kt = kTs.next()
                            kb.dma("sp", out=ki.t[:], in_=cache_k[l, sq_].rearrange("(c p) d -> p c d", p=128), writes=[ki.k])
                            kb.dma("pool", out=vv.t[:], in_=cache_v[l, sq_].rearrange("(c p) d -> p c d", p=128), writes=[vv.k])
                            for mc in range(2):
                                for g in range(2):
                                    ps = psum.next()
                                    for j in range(4):
                                        kc = 4 * g + j
                                        kb.op("pe", lambda e: e.transpose(ps.t[:, j * 128:(j + 1) * 128], ki.t[:, mc, kc * 128:(kc + 1) * 128], ident.t),
                                              reads=[ki.k, ident.k], writes=[ps.k], inc=(j == 3))
                                    evac_copy(kt.t[:, 4 * g:4 * g + 4, mc * 128:(mc + 1) * 128], ps.t[:].rearrange("p (j c) -> p j c", j=4),
                                              [ps.k], [kt.k])
                            cs = slice(sq_ * TS, (sq_ + 1) * TS)
                            ps = psum.next()
                            for hd in range(4):
                                for mc in range(2):
                                    col = (hd * 2 + mc) * TS
                                    for dc in range(2):
                                        last = (hd == 3 and mc == 1 and dc == 1)
                                        kb.op("pe", lambda e: e.matmul(ps.t[:, col:col + TS], kt.t[:, 2 * hd + dc, mc * 128:(mc + 1) * 128],
                                                                       qT.t[:, 2 * hd + dc, cs], start=(dc == 0), stop=(dc == 1)),
                                              reads=[kt.k, qT.k], writes=[ps.k], inc=last)
                            p = pTs.next()
                            kb.op("act", lambda e: e.activation(out=p.t[:, :], in_=ps.t[:, 0:64], func=AF.Exp, scale=sc),
                                  reads=[ps.k], writes=[p.k])
                            pv4 = p.t[:, :].rearrange("p (h m t) -> p h m t", h=4, m=2)
                            pss = psum.next()
                            for mc in range(2):
                                kb.op("pe", lambda e: e.matmul(pss.t[:, 0:32].rearrange("p (h t) -> p h t", h=4), one1_bf.t[:], pv4[:, :, mc, :],
                                                               start=(mc == 0), stop=(mc == 1)),
                                      reads=[one1_bf.k, p.k], writes=[pss.k], inc=(mc == 1))
                            ri = rinvs.next()
                            kb.op("dve", lambda e: e.reciprocal(out=ri.t[:, :], in_=pss.t[:, 0:32]), reads=[pss.k], writes=[ri.k])
                            pso = psum.next()
                            for hd in range(4):
                                for dc in range(2):
                                    col = (hd * 2 + dc) * TS
                                    for mc in range(2):
                                        last = (hd == 3 and dc == 1 and mc == 1)
                                        kb.op("pe", lambda e: e.matmul(pso.t[:, col:col + TS], vv.t[:, mc, hd * 256 + dc * 128:hd * 256 + (dc + 1) * 128],
                                                                       pv4[:, hd, mc, :], start=(mc == 0), stop=(mc == 1)),
                                              reads=[vv.k, p.k], writes=[pso.k], inc=last)
                            for hd in range(4):
                                kb.op("dve", lambda e: e.tensor_tensor(out=oT.t[:, 2 * hd:2 * hd + 2, cs],
                                                                       in0=pso.t[:, hd * 16:(hd + 1) * 16].rearrange("p (c t) -> p c t", c=2),
                                                                       in1=ri.t[:, hd * TS:(hd + 1) * TS].unsqueeze(1).to_broadcast([128, 2, TS]),
                                                                       op=ALU.mult),
                                      reads=[pso.k, ri.k], writes=[oT.k])
                    li = proj_fm(s, li, 4, 256, lambda kc: oT.t[:, kc, :n], oT.k, n, resid_add(ti, 1.0))
                kb.barrier()

        def phase_final(do_norm):
            with ExitStack() as ph:
                yo = Ring([sb(ph, "yo%d" % i, [128, D]) for i in range(2)])
                hF = sb(ph, "hF", [128, KC, 512])
                for ti, (c0, c1) in enumerate(TILES):
                    n = c1 - c0
                    if do_norm:
                        rmsnorm_x(ti, GI["final"], hF)
                    for sbk in range(n // 128):
                        yb = yo.next()
                        for g in range(2):
                            ps = psum.next()
                            for j in range(4):
                                kc = 4 * g + j
                                if do_norm:
                                    src_ap, rd = hF.t[:, kc, sbk * 128:(sbk + 1) * 128], [hF.k, ident.k]
                                else:
                                    src_ap, rd = xT[:, kc, c0 + sbk * 128:c0 + (sbk + 1) * 128], [xk[kc][ti], ident.k]
                                kb.op("pe", lambda e: e.transpose(ps.t[:, j * 128:(j + 1) * 128], src_ap, ident.t),
                                      reads=rd, writes=[ps.k], inc=(j == 3))
                            evac_copy(yb.t[:, g * 512:(g + 1) * 512], ps.t[:], [ps.k], [yb.k])
                        col = c0 + sbk * 128
                        dst = y_prompt[col:col + 128, :] if col < SEQ else y_sample[:, :]
                        kb.dma("sp", out=dst, in_=yb.t[:], reads=[yb.k])
                kb.barrier()

        class HSet:
            def __init__(self, stk, tag, w=512):
                self.qs = sb(stk, "qs" + tag, [128, w])
                self.fk = sb(stk, "fk" + tag, [128, w])
                self.lf = sb(stk, "lf" + tag, [128, w])
                self.vb = sb(stk, "vb" + tag, [128, w])
                self.sg = sb(stk, "sg" + tag, [128, w])

        def head_norm_out(o_ap, o_trk, from_psum, osb, tmpb, gcol, gcol_trk, sgb, y_ap, y_trk, n, ones_t=None):
            ones_t = ones_t or ones128_bf
            q = sqr.next()
            kb.op("act", lambda e: e.activation(out=q.t[:, :n], in_=o_ap, func=AF.Square), reads=[o_trk], writes=[q.k])
            if from_psum:
                kb.op("act", lambda e: e.copy(out=osb.t[:, :n], in_=o_ap), reads=[o_trk], writes=[osb.k])
            psn = psum.next()
            kb.op("pe", lambda e: e.matmul(psn.t[:, :n], ones_t.t[:], q.t[:, :n], start=True, stop=True),
                  reads=[q.k, ones_t.k], writes=[psn.k])
            kb.op("act", lambda e: e.activation(out=rstd.t[:, :n], in_=psn.t[:, :n], func=AF.Ln, bias=epsc.t[:, 0:1], scale=1.0),
                  reads=[psn.k, epsc.k], writes=[rstd.k])
            kb.op("act", lambda e: e.activation(out=rstd.t[:, :n], in_=rstd.t[:, :n], func=AF.Exp, scale=-0.5), reads=[rstd.k], writes=[rstd.k])
            kb.op("dve", lambda e: e.tensor_tensor(out=tmpb.t[:, :n], in0=osb.t[:, :n], in1=rstd.t[:, :n], op=ALU.mult),
                  reads=[osb.k, rstd.k], writes=[tmpb.k])
            kb.op("dve", lambda e: e.scalar_tensor_tensor(out=y_ap, in0=tmpb.t[:, :n], scalar=gcol, in1=sgb.t[:, :n],
                                                          op0=ALU.mult, op1=ALU.mult),
                  reads=[tmpb.k, gcol_trk, sgb.k], writes=[y_trk])

        def phase_mix_even():
            gi = GI["mix"] + 0
            with ExitStack() as ph:
                hT = sb(ph, "hT", [128, KC, 512], BF16)
                cw = sb(ph, "cw", [128, 3, 4])
                lbr = sb(ph, "lbr", [128, 2, 4])
                gn = sb(ph, "gn", [128, 4])
                lb = sb(ph, "lb", [128, 4])
                oml = sb(ph, "oml", [128, 4])
                ubuf = sb(ph, "ubuf", [128, 4, 514])
                ubs = sb(ph, "ubs", [128, 4, NS, TS + 2])
                yT = sb(ph, "yT", [128, KC, 512], BF16)
                sets = [HSet(ph, "0"), HSet(ph, "1")]
                for S_ in sets:
                    S_.qb = sb(ph, "qb", [128, 512], BF16)
                    S_.kb = sb(ph, "kb", [128, 512], BF16)
                    S_.vh = sb(ph, "vh", [128, 512], BF16)
                identb = sb(ph, "identb", [128, 128], BF16)
                kb.op("dve", lambda e: e.tensor_copy(out=identb.t[:], in_=ident.t), reads=[ident.k], writes=[identb.k])
                bb = sb(ph, "bb", [128, 512])
                e1 = sb(ph, "e1", [128, 512])
                e2 = sb(ph, "e2", [128, 512])
                osb = sb(ph, "osb", [128, 512])
                tmpb = sb(ph, "tmpb", [128, 512])
                bm = sb(ph, "bm", [128, 16])
                ebm = sb(ph, "ebm", [128, 16])
                kvtm = Ring([sb(ph, "kvtm%d" % i, [128, 256]) for i in range(2)])
                kvb = Ring([sb(ph, "kvb%d" % i, [128, 256], BF16) for i in range(3)])
                Am = Ring([sb(ph, "Am%d" % i, [128, 128]) for i in range(2)])
                Am32 = Ring([sb(ph, "Am32_%d" % i, [GCH, GCH], BF16) for i in range(4)])
                ebmU = [sb(ph, "ebmU%d" % i, [128, 16]) for i in range(2)]
                elastU = [sb(ph, "elastU%d" % i, [128, 16]) for i in range(2)]
                for _b in Am32.bufs:
                    kb.op("dve", lambda e: e.memset(_b.t[:], 0.0), writes=[_b.k])
                Hh = Ring([sb(ph, "Hh%d" % i, [128, 128], BF16) for i in range(4)])
                Hs = Ring([sb(ph, "Hs%d" % i, [128, 128]) for i in range(2)])
                H = [sb(ph, "Hst%d" % i, [128, 128]) for i in range(4)]
                h0 = sb(ph, "h0", [128, NS, 128])
                vexp = sb(ph, "vexp", [128, NS, 128])
                hn = Ring([sb(ph, "hn%d" % i, [128, 4, 128]) for i in range(2)])
                scin = sb(ph, "scin", [32, 512])
                cst32 = sb(ph, "cst32", [128, 4, 32])
                scout = sb(ph, "scout", [32, 512])

                kb.dma("sp", out=cw.t[:], in_=conv_w.rearrange("j (c p) -> p j c", p=128), writes=[cw.k], allow_slow_non_contiguous=True)
                kb.dma("sp", out=lbr.t[:], in_=hgrn_lb.rearrange("r (c p) -> p r c", p=128), writes=[lbr.k], allow_slow_non_contiguous=True)
                kb.dma("sp", out=gn.t[:], in_=hgrn_gnorm.rearrange("(c p) -> p c", p=128), writes=[gn.k], allow_slow_non_contiguous=True)
                kb.op("dve", lambda e: e.tensor_tensor(out=lb.t[:], in0=lbr.t[:, 0, :], in1=lbr.t[:, 1, :], op=ALU.subtract),
                      reads=[lbr.k], writes=[lb.k])
                kb.op("act", lambda e: e.activation(out=lb.t[:], in_=lb.t[:], func=AF.Sigmoid), reads=[lb.k], writes=[lb.k])
                kb.op("dve", lambda e: e.tensor_scalar(out=oml.t[:], in0=lb.t[:], scalar1=-1.0, scalar2=1.0, op0=ALU.mult, op1=ALU.add),
                      reads=[lb.k], writes=[oml.k])
                kb.op("dve", lambda e: e.memset(ubuf.t[:, :, 0:2], 0.0), writes=[ubuf.k])
                for hd in range(4):
                    kb.op("dve", lambda e: e.memset(H[hd].t[:], 0.0), writes=[H[hd].k])
                kb.dma("sp", out=scin.t[:], in_=state_conv.rearrange("s r c -> (s r) c"), writes=[scin.k])
                for c in range(4):
                    ps = psum.next()
                    kb.op("pe", lambda e: e.transpose(ps.t[:, 0:32], scin.t[0:32, c * 128:(c + 1) * 128], ident.t[0:32, 0:32]),
                          reads=[scin.k, ident.k], writes=[ps.k])
                    kb.op("dve", lambda e: e.tensor_copy(out=ubs.t[:, c, :, 0:2], in_=ps.t[:, 0:32].rearrange("p (s r) -> p s r", r=2)),
                          reads=[ps.k], writes=[ubs.k])

                win_v = even_w_in.rearrange("(k p) n -> p k n", p=128)
                wout_v = even_w_out.rearrange("(k p) n -> p k n", p=128)
                order = [0, 4, 2, 1, 5, 3, 6, 8, 10, 12, 7, 9, 11, 13]
                loads = []
                for ti in range(len(TILES)):
                    for b in order:
                        loads.append((lambda t: wview(t, KC, 256), win_v[:, :, b * 256:(b + 1) * 256]))
                    for b in range(4):
                        loads.append((lambda t: wview(t, KC, 256), wout_v[:, :, b * 256:(b + 1) * 256]))
                s = Stream(kb, wpool, loads)
                li = 0

                def gla_common(hd, S, n, rm_off, mask_off, use_mid):
                    kb.op("dve", lambda e: e.tensor_tensor_scan(out=bb.t[:, :n], data0=cst.t[:, rm_off:rm_off + n], data1=S.lf.t[:, :n],
                                                                initial=0.0, op0=ALU.mult, op1=ALU.add),
                          reads=[cst.k, S.lf.k], writes=[bb.k])
                    if use_mid:
                        nch = n // GCH
                        b3 = bb.t[:, :n].rearrange("p (c t) -> p c t", t=GCH)
                        kb.op("dve", lambda e: e.tensor_copy(out=bm.t[:, 0:nch], in_=b3[:, :, GCH // 2 - 1]), reads=[bb.k], writes=[bm.k])
                        kb.op("dve", lambda e: e.tensor_tensor(out=b3, in0=b3, in1=bm.t[:, 0:nch].unsqueeze(2).to_broadcast([128, nch, GCH]),
                                                               op=ALU.subtract), reads=[bb.k, bm.k], writes=[bb.k])
                        kb.op("act", lambda e: e.activation(out=ebm.t[:, 0:nch], in_=bm.t[:, 0:nch], func=AF.Exp), reads=[bm.k], writes=[ebm.k])
                    kb.op("act", lambda e: e.activation(out=e1.t[:, :n], in_=bb.t[:, :n], func=AF.Exp), reads=[bb.k], writes=[e1.k])
                    kb.op("act", lambda e: e.activation(out=e2.t[:, :n], in_=bb.t[:, :n], func=AF.Exp, scale=-1.0), reads=[bb.k], writes=[e2.k])
                    qo, ko = (S.qb, S.kb) if use_mid else (S.qs, S.fk)
                    kb.op("dve", lambda e: e.tensor_tensor(out=qo.t[:, :n], in0=S.qs.t[:, :n], in1=e1.t[:, :n], op=ALU.mult),
                          reads=[S.qs.k, e1.k], writes=[qo.k])
                    kb.op("dve", lambda e: e.tensor_tensor(out=ko.t[:, :n], in0=S.fk.t[:, :n], in1=e2.t[:, :n], op=ALU.mult),
                          reads=[S.fk.k, e2.k], writes=[ko.k])

                def chunk_tm(S, a):
                    pst = psum.next()
                    kb.op("pe", lambda e: e.transpose(pst.t[:, 0:128], S.fk.t[:, a:a + 128], ident.t), reads=[S.fk.k, ident.k], writes=[pst.k], inc=False)
                    kb.op("pe", lambda e: e.transpose(pst.t[:, 128:256], S.vb.t[:, a:a + 128], ident.t), reads=[S.vb.k, ident.k], writes=[pst.k])
                    kv = kvtm.next()
                    evac_copy(kv.t[:, :], pst.t[:, 0:256], [pst.k], [kv.k])
                    return kv

                def chunk_att(S, a, kv, mask_off):
                    psA = psum.next()
                    kb.op("pe", lambda e: e.matmul(psA.t[:, 0:128], S.fk.t[:, a:a + 128], S.qs.t[:, a:a + 128], start=True, stop=True),
                          reads=[S.fk.k, S.qs.k], writes=[psA.k])
                    am = Am.next()
                    kb.op("dve", lambda e: e.tensor_tensor(out=am.t[:], in0=psA.t[:, 0:128], in1=cst.t[:, mask_off:mask_off + 128], op=ALU.mult),
                          reads=[psA.k, cst.k], writes=[am.k])
                    return am

                def gla_prompt_pair(hds, Ss, n):
                    nch = n // GCH
                    psOs = []
                    for u in range(2):
                        gla_common(hds[u], Ss[u], n, CO["rmask_p32"], CO["causalT"], True)
                        kb.op("dve", lambda e: e.tensor_copy(out=ebmU[u].t[:, 0:nch], in_=ebm.t[:, 0:nch]), reads=[ebm.k], writes=[ebmU[u].k])
                        kb.op("dve", lambda e: e.tensor_copy(out=elastU[u].t[:, 0:nch], in_=e1.t[:, :n].rearrange("p (c t) -> p c t", t=GCH)[:, :, GCH - 1]),
                              reads=[e1.k], writes=[elastU[u].k])
                        psOs.append(psacc.next())
                    cmask = cst.t[0:GCH, CO["causalT"]:CO["causalT"] + GCH].bitcast(mybir.dt.uint32)
                    items = [(ch, u) for ch in range(nch) for u in range(2)]
                    st = {}

                    def stage_a(it):
                        ch, u = it
                        S = Ss[u]
                        a = ch * GCH
                        pst = psum.next()
                        pstb = pst.t[:, :].bitcast(BF16)
                        kb.op("pe", lambda e: e.transpose(pstb[0:GCH, 0:128], S.kb.t[:, a:a + GCH], identb.t[:]), reads=[S.kb.k, identb.k], writes=[pst.k], inc=False)
                        kb.op("pe", lambda e: e.transpose(pstb[0:GCH, 128:256], S.vh.t[:, a:a + GCH], identb.t[:]), reads=[S.vh.k, identb.k], writes=[pst.k])
                        psA = psum.next()
                        kb.op("pe", lambda e: e.matmul(psA.t[0:GCH, 0:GCH], S.kb.t[:, a:a + GCH], S.qb.t[:, a:a + GCH], start=True, stop=True),
                              reads=[S.kb.k, S.qb.k], writes=[psA.k])
                        st[it] = [pst, psA]

                    def stage_b(it):
                        ch, u = it
                        pst, psA = st[it]
                        kv = kvb.next()
                        kb.op("act", lambda e: e.copy(out=kv.t[0:GCH, :], in_=pst.t[:, :].bitcast(BF16)[0:GCH, 0:256]), reads=[pst.k], writes=[kv.k])
                        am = Am32.next()
                        kb.op("dve", lambda e: e.copy_predicated(out=am.t[:, :], mask=cmask, data=psA.t[0:GCH, 0:GCH]),
                              reads=[psA.k, cst.k], writes=[am.k])
                        hh = Hh.next()
                        kb.op("dve", lambda e: e.tensor_scalar(out=hh.t[:], in0=H[hds[u]].t[:], scalar1=ebmU[u].t[:, ch:ch + 1], scalar2=None, op0=ALU.mult),
                              reads=[H[hds[u]].k, ebmU[u].k], writes=[hh.k])
                        st[it] = [kv, am, hh]

                    def stage_c(it):
                        ch, u = it
                        S = Ss[u]
                        a = ch * GCH
                        kv, am, hh = st[it]
                        psO = psOs[u]
                        kb.op("pe", lambda e: e.matmul(psO.t[:, a:a + GCH], hh.t[:], S.qb.t[:, a:a + GCH], start=True, stop=False),
                              reads=[hh.k, S.qb.k], writes=[psO.k], inc=False)
                        kb.op("pe", lambda e: e.matmul(psO.t[:, a:a + GCH], kv.t[0:GCH, 128:256], am.t[:, :], start=False, stop=True),
                              reads=[kv.k, am.k], writes=[psO.k])
                        psH = psum.next()
                        kb.op("pe", lambda e: e.matmul(psH.t[:, 0:128], kv.t[0:GCH, 0:128], kv.t[0:GCH, 128:256], start=True, stop=True),
                              reads=[kv.k], writes=[psH.k])
                        hs = Hs.next()
                        kb.op("dve", lambda e: e.scalar_tensor_tensor(out=hs.t[:], in0=H[hds[u]].t[:], scalar=ebmU[u].t[:, ch:ch + 1], in1=psH.t[:, 0:128],
                                                                      op0=ALU.mult, op1=ALU.add),
                              reads=[H[hds[u]].k, ebmU[u].k, psH.k], writes=[hs.k])
                        kb.op("dve", lambda e: e.tensor_scalar(out=H[hds[u]].t[:], in0=hs.t[:], scalar1=elastU[u].t[:, ch:ch + 1], scalar2=None, op0=ALU.mult),
                              reads=[hs.k, elastU[u].k], writes=[H[hds[u]].k])
                        del st[it]

                    stage_a(items[0])
                    for i, it in enumerate(items):
                        stage_b(it)
                        if i + 1 < len(items):
                            stage_a(items[i + 1])
                        stage_c(it)
                    for u in range(2):
                        head_norm_out(psOs[u].t[:, :n], psOs[u].k, True, osb, tmpb, gn.t[:, hds[u]:hds[u] + 1], gn.k, Ss[u].sg,
                                      yT.t[:, 4 + hds[u], :n], yT.k, n)

                def gla_sample(hd, S):
                    n = 128
                    gla_common(hd, S, n, CO["rmask_s"], CO["blkmask"], False)
                    kv = chunk_tm(S, 0)
                    am = chunk_att(S, 0, kv, CO["blkmask"])
                    psO = psacc.next()
                    kb.op("pe", lambda e: e.matmul(psO.t[:, 0:128], kv.t[:, 128:256], am.t[:], start=True, stop=True),
                          reads=[kv.k, am.k], writes=[psO.k])
                    kb.dma("sp", out=h0.t[:], in_=state_hgrn[:, hd].rearrange("s k v -> k s v"), writes=[h0.k])
                    psI = psum.next()
                    for sq_ in range(NS):
                        kb.op("pe", lambda e: e.matmul(psI.t[:, sq_ * TS:(sq_ + 1) * TS], h0.t[:, sq_, :], S.qs.t[:, sq_ * TS:(sq_ + 1) * TS],
                                                       start=True, stop=True),
                              reads=[h0.k, S.qs.k], writes=[psI.k], inc=(sq_ == NS - 1))
                    kb.op("act", lambda e: e.copy(out=tmpb.t[:, :n], in_=psI.t[:, :n]), reads=[psI.k], writes=[tmpb.k])
                    kb.op("dve", lambda e: e.tensor_tensor(out=osb.t[:, :n], in0=psO.t[:, :n], in1=tmpb.t[:, :n], op=ALU.add),
                          reads=[psO.k, tmpb.k], writes=[osb.k])
                    head_norm_out(osb.t[:, :n], osb.k, False, osb, tmpb, gn.t[:, hd:hd + 1], gn.k, S.sg, yT.t[:, 4 + hd, :n], yT.k, n)
                    kb.op("dve", lambda e: e.tensor_tensor(out=vexp.t[:], in0=kv.t[:, 128:256].unsqueeze(1).to_broadcast([128, NS, 128]),
                                                           in1=cst.t[:, CO["seqmask"]:CO["seqmask"] + NS].unsqueeze(2).to_broadcast([128, NS, 128]),
                                                           op=ALU.mult), reads=[kv.k, cst.k], writes=[vexp.k])
                    e1l = e1.t[:, :n].rearrange("p (s t) -> p s t", t=TS)
                    for j in range(4):
                        psD = psum.next()
                        kb.op("pe", lambda e: e.matmul(psD.t[:, :], kv.t[:, 0:128], vexp.t[:, 4 * j:4 * j + 4, :].rearrange("p s v -> p (s v)"),
                                                       start=True, stop=True), reads=[kv.k, vexp.k], writes=[psD.k])
                        hb = hn.next()
                        kb.op("dve", lambda e: e.tensor_tensor(out=hb.t[:], in0=psD.t[:, :].rearrange("p (s v) -> p s v", s=4),
                                                               in1=h0.t[:, 4 * j:4 * j + 4, :], op=ALU.add), reads=[psD.k, h0.k], writes=[hb.k])
                        kb.op("dve", lambda e: e.tensor_tensor(out=hb.t[:], in0=hb.t[:],
                                                               in1=e1l[:, 4 * j:4 * j + 4, TS - 1].unsqueeze(2).to_broadcast([128, 4, 128]),
                                                               op=ALU.mult), reads=[hb.k, e1.k], writes=[hb.k])
                        kb.dma("sp", out=hgrn_s[4 * j:4 * j + 4, hd].rearrange("s k v -> k s v"), in_=hb.t[:], reads=[hb.k])

                for ti, (c0, c1) in enumerate(TILES):
                    n = c1 - c0
                    is_s = c0 >= SEQ
                    rmsnorm_x(ti, gi, hT)
                    for oi, b in enumerate(order):
                        w = s.get(li)
                        li += 1
                        wv = wview(w.t, KC, 256)
                        role = b // 2
                        for jj in range(2):
                            c = (2 * b + jj) % 4
                            S = sets[jj]
                            ps = psum.next()
                            for kc in range(KC):
                                kb.op("pe", lambda e: e.matmul(ps.t[:, :n], wv[:, kc, jj * 128:(jj + 1) * 128], hT.t[:, kc, :n],
                                                               start=(kc == 0), stop=(kc == KC - 1)),
                                      reads=[w.k, hT.k], writes=[ps.k], inc=(kc == KC - 1))
                            if is_s:
                                u_now = ubs.t[:, c, :, 2:TS + 2]
                                u_m1 = ubs.t[:, c, :, 1:TS + 1]
                                u_m2 = ubs.t[:, c, :, 0:TS]
                                utrk = ubs.k
                                v3 = lambda ap: ap.rearrange("p (s t) -> p s t", t=TS)
                            else:
                                u_now = ubuf.t[:, c, 2:2 + n]
                                u_m1 = ubuf.t[:, c, 1:1 + n]
                                u_m2 = ubuf.t[:, c, 0:n]
                                utrk = ubuf.k
                                v3 = lambda ap: ap
                            if role == 0:
                                kb.op("act", lambda e: e.copy(out=S.qs.t[:, :n], in_=ps.t[:, :n]), reads=[ps.k], writes=[S.qs.k])
                            elif role == 2:
                                kb.op("dve", lambda e: e.tensor_tensor(out=u_now, in0=v3(S.qs.t[:, :n]), in1=v3(ps.t[:, :n]), op=ALU.mult),
                                      reads=[S.qs.k, ps.k], writes=[utrk])
                            elif role == 1:
                                ct = S.sg
                                kb.op("dve", lambda e: e.tensor_scalar(out=v3(ct.t[:, :n]), in0=u_now, scalar1=cw.t[:, 2, c:c + 1], scalar2=None, op0=ALU.mult),
                                      reads=[utrk, cw.k], writes=[ct.k])
                                kb.op("dve", lambda e: e.scalar_tensor_tensor(out=v3(ct.t[:, :n]), in0=u_m1, scalar=cw.t[:, 1, c:c + 1], in1=v3(ct.t[:, :n]),
                                                                              op0=ALU.mult, op1=ALU.add), reads=[utrk, cw.k, ct.k], writes=[ct.k])
                                kb.op("dve", lambda e: e.scalar_tensor_tensor(out=v3(ct.t[:, :n]), in0=u_m2, scalar=cw.t[:, 0, c:c + 1], in1=v3(ct.t[:, :n]),
                                                                              op0=ALU.mult, op1=ALU.add), reads=[utrk, cw.k, ct.k], writes=[ct.k])
                                kb.op("dve", lambda e: e.tensor_tensor(out=yT.t[:, c, :n], in0=ct.t[:, :n], in1=ps.t[:, :n], op=ALU.mult),
                                      reads=[ct.k, ps.k], writes=[yT.k])
                            elif role == 3:
                                kb.op("act", lambda e: e.activation(out=S.qs.t[:, :n], in_=ps.t[:, :n], func=AF.Silu), reads=[ps.k], writes=[S.qs.k])
                            elif role == 4:
                                kb.op("act", lambda e: e.activation(out=S.fk.t[:, :n], in_=ps.t[:, :n], func=AF.Sigmoid), reads=[ps.k], writes=[S.fk.k])
                                kb.op("dve", lambda e: e.tensor_scalar(out=S.fk.t[:, :n], in0=S.fk.t[:, :n], scalar1=oml.t[:, c:c + 1], scalar2=lb.t[:, c:c + 1],
                                                                       op0=ALU.mult, op1=ALU.add), reads=[S.fk.k, oml.k, lb.k], writes=[S.fk.k])
                                kb.op("act", lambda e: e.activation(out=S.lf.t[:, :n], in_=S.fk.t[:, :n], func=AF.Ln), reads=[S.fk.k], writes=[S.lf.k])
                                kb.op("dve", lambda e: e.tensor_scalar(out=S.fk.t[:, :n], in0=S.fk.t[:, :n], scalar1=-1.0, scalar2=1.0,
                                                                       op0=ALU.mult, op1=ALU.add), reads=[S.fk.k], writes=[S.fk.k])
                            elif role == 5:
                                vo = S.vb if is_s else S.vh
                                kb.op("act", lambda e: e.copy(out=vo.t[:, :n], in_=ps.t[:, :n]), reads=[ps.k], writes=[vo.k])
                            elif role == 6:
                                kb.op("act", lambda e: e.activation(out=S.sg.t[:, :n], in_=ps.t[:, :n], func=AF.Silu), reads=[ps.k], writes=[S.sg.k])
                        if role == 6:
                            if is_s:
                                for jj in range(2):
                                    gla_sample((2 * b + jj) % 4, sets[jj])
                            else:
                                gla_prompt_pair([(2 * b) % 4, (2 * b + 1) % 4], sets, n)
                    if not is_s:
                        if c1 == SEQ:
                            for c in range(4):
                                kb.dma("sp", out=conv_p[:, c * 128:(c + 1) * 128].rearrange("r p -> p r"), in_=ubuf.t[:, c, n:n + 2], reads=[ubuf.k],
                                       allow_slow_non_contiguous=True)
                            for hd in range(4):
                                kb.dma("sp", out=hgrn_p[hd], in_=H[hd].t[:], reads=[H[hd].k])
                        else:
                            kb.op("dve", lambda e: e.tensor_copy(out=ubuf.t[:, :, 0:2], in_=ubuf.t[:, :, n:n + 2]), reads=[ubuf.k], writes=[ubuf.k])
                    else:
                        kb.op("dve", lambda e: e.tensor_copy(out=cst32.t[:].rearrange("p c (s r) -> p c s r", r=2), in_=ubs.t[:, :, :, TS:TS + 2]),
                              reads=[ubs.k], writes=[cst32.k])
                        ps = psum.next()
                        for c in range(4):
                            kb.op("pe", lambda e: e.transpose(ps.t[0:32, c * 128:(c + 1) * 128], cst32.t[:, c, :], ident.t),
                                  reads=[cst32.k, ident.k], writes=[ps.k], inc=(c == 3))
                        kb.op("act", lambda e: e.copy(out=scout.t[:], in_=ps.t[0:32, :]), reads=[ps.k], writes=[scout.k])
                        kb.dma("sp", out=conv_s.rearrange("s r c -> (s r) c"), in_=scout.t[:], reads=[scout.k])
                    li = proj_fm(s, li, 4, 256, lambda kc: yT.t[:, kc, :n], yT.k, n, resid_add(ti, 1.0))
                kb.barrier()

        def phase_mix_odd():
            gi = GI["mix"] + 1
            WDEC = 0.6065306597126334
            with ExitStack() as ph:
                hT = sb(ph, "hT", [128, KC, 256], BF16)
                co = sb(ph, "co_sb", [128, NCO])
                kb.dma("sp", out=co.t[:], in_=consts_odd_d[:, :], writes=[co.k])
                prm = sb(ph, "oprm", [128, 64])
                PRM = {}
                pcol = [0]

                def pload(name, src_ap, ncol):
                    PRM[name] = pcol[0]
                    kb.dma("sp", out=prm.t[:, pcol[0]:pcol[0] + ncol], in_=src_ap, writes=[prm.k], allow_slow_non_contiguous=True)
                    pcol[0] += ncol
                pload("rgn", ret_gnorm.rearrange("(c p) -> p c", p=128), 4)
                pload("mu", rwkv_mu.rearrange("(c p) -> p c", p=128), 14)
                pload("w0", rwkv_w0.rearrange("(c p) -> p c", p=128), 4)
                pload("a0", rwkv_a0.rearrange("(c p) -> p c", p=128), 4)
                pload("kk", rwkv_k_k.rearrange("(c p) -> p c", p=128), 4)
                pload("ka", rwkv_k_a.rearrange("(c p) -> p c", p=128), 4)
                pload("rk", rwkv_r_k.rearrange("(c p) -> p c", p=128), 4)
                pload("lg", rwkv_lnx_g.rearrange("(c p) -> p c", p=128), 4)
                pload("lbb", rwkv_lnx_b.rearrange("(c p) -> p c", p=128), 4)
                omu = sb(ph, "omu", [128, 14])
                kb.op("dve", lambda e: e.tensor_scalar(out=omu.t[:], in0=prm.t[:, PRM["mu"]:PRM["mu"] + 14], scalar1=-1.0, scalar2=1.0,
                                                       op0=ALU.mult, op1=ALU.add), reads=[prm.k], writes=[omu.k])
                gneps = sb(ph, "gneps", [128, 1])
                kb.op("dve", lambda e: e.memset(gneps.t[:], 64e-5), writes=[gneps.k])
                w2a2 = sb(ph, "w2a2", [128, 512])
                g2sb = sb(ph, "g2sb", [128, 512])
                kb.dma("sp", out=w2a2.t[0:64, :], in_=rwkv_w2[:, :], writes=[w2a2.k])
                kb.dma("sp", out=w2a2.t[64:128, :], in_=rwkv_a2[:, :], writes=[w2a2.k])
                kb.dma("sp", out=g2sb.t[:], in_=rwkv_g2[:, :], writes=[g2sb.k])

                W = 256
                yT = sb(ph, "yT", [128, KC, W], BF16)
                sets = [HSet(ph, "0", W), HSet(ph, "1", W)]
                osb = sb(ph, "osb", [128, W])
                tmpb = sb(ph, "tmpb", [128, W])
                osbR = sb(ph, "osbR", [128, W])
                tmpbR = sb(ph, "tmpbR", [128, W])
                cosb = sb(ph, "cosb", [128, W])
                sinb = sb(ph, "sinb", [128, W])
                kvtm = Ring([sb(ph, "kvtm%d" % i, [128, 256]) for i in range(2)])
                Am = Ring([sb(ph, "Am%d" % i, [128, 128]) for i in range(2)])
                Hret = [sb(ph, "Hret%d" % i, [128, 128]) for i in range(4)]
                h0r = Ring([sb(ph, "h0r%d" % i, [128, 4, 128]) for i in range(1)])
                h0Tr = Ring([sb(ph, "h0Tr%d" % i, [128, 4, 128]) for i in range(4)])
                hn = Ring([sb(ph, "hn%d" % i, [128, 4, 128]) for i in range(1)])
                vexp = Ring([sb(ph, "vexp%d" % i, [128, 4, 128]) for i in range(2)])
                pdraw = sb(ph, "pdraw", [128, W + 4])
                carry = sb(ph, "carry", [128, 14])
                shs = sb(ph, "shs", [128, 14, NS])
                p28 = sb(ph, "p28", [128, W])
                p29 = sb(ph, "p29", [128, W])
                R = {nm: sb(ph, "rw_" + nm, [128, W]) for nm in ("r", "k", "v", "lw", "a", "kk", "bt", "at", "atA", "atB", "rA", "rB", "bon", "e1")}
                Hblk = [sb(ph, "Hblk%d" % i, [128, 128]) for i in range(4)]
                Hh = Ring([sb(ph, "Hh%d" % i, [128, 128]) for i in range(2)])
                Hs = Ring([sb(ph, "Hs%d" % i, [128, 128]) for i in range(2)])
                bm = sb(ph, "bm", [128, 4])
                ebm = sb(ph, "ebm", [128, 4])
                class RWUnit:
                    def __init__(self, u):
                        self.ext = {nm: sb(ph, "ext%d_%s" % (u, nm), [128, 128]) for nm in ("KA", "KB", "BA", "BB", "VA", "VB")}
                        self.M = Ring([sb(ph, "u%d_M%d" % (u, i), [128, 256], BF16) for i in range(2)])
                        self.MT = Ring([sb(ph, "u%d_MT%d" % (u, i), [128, 256], BF16) for i in range(2)])
                        self.TTb = Ring([sb(ph, "u%d_TTb%d" % (u, i), [128, 256], BF16) for i in range(2)])
                        self.TT = sb(ph, "u%d_TT" % u, [128, 256])
                        self.LakT = sb(ph, "u%d_LakT" % u, [128, 256])
                        self.ArbT = sb(ph, "u%d_ArbT" % u, [128, 256])
                        self.ArkT = sb(ph, "u%d_ArkT" % u, [128, 256])
                UX = [RWUnit(0), RWUnit(1)]
                extU = {nm: sb(ph, "ext_" + nm, [128, 128]) for nm in ("UA", "UB")}
                identb = sb(ph, "identb", [128, 128], BF16)
                kb.op("dve", lambda e: e.tensor_copy(out=identb.t[:], in_=ident.t), reads=[ident.k], writes=[identb.k])
                Rsb = sb(ph, "Rsb", [128, 128])
                xTs = sb(ph, "xTs", [128, 128])

                for X_ in UX:
                    for nm in X_.ext:
                        kb.op("dve", lambda e: e.memset(X_.ext[nm].t[:], 0.0), writes=[X_.ext[nm].k])
                for nm in extU:
                    kb.op("dve", lambda e: e.memset(extU[nm].t[:], 0.0), writes=[extU[nm].k])
                for i in range(4):
                    kb.op("dve", lambda e: e.memset(Hret[i].t[:], 0.0), writes=[Hret[i].k])
                    kb.op("dve", lambda e: e.memset(Hblk[i].t[:], 0.0), writes=[Hblk[i].k])
                kb.op("dve", lambda e: e.memset(carry.t[:], 0.0), writes=[carry.k])
                for c in range(14):
                    kb.dma("sp", out=shs.t[:, c, :], in_=state_shift[:, c * 128:(c + 1) * 128].rearrange("s p -> p s"), writes=[shs.k],
                           allow_slow_non_contiguous=True)

                win_v = odd_w_in.rearrange("(k p) n -> p k n", p=128)
                wout_v = odd_w_out.rearrange("(k p) n -> p k n", p=128)
                def pair_blocks(pr_):
                    return [((16 + pr_) * 128, 128), ((20 + pr_) * 128, 128), ((24 + pr_) * 128, 128)]
                blocks = ([(b * 256, 256) for b in (0, 2, 4, 6)] + [(28 * 128, 256)] + pair_blocks(0) + pair_blocks(1)
                          + [(b * 256, 256) for b in (1, 3, 5, 7)] + pair_blocks(2) + pair_blocks(3))
                NSUB = SEQ // 256 + 1
                loads = []
                for ti in range(NSUB):
                    for (c0_, wd_) in blocks:
                        loads.append(((lambda wd__: (lambda t: wview(t, KC, wd__)))(wd_), win_v[:, :, c0_:c0_ + wd_]))
                    for b in range(4):
                        loads.append((lambda t: wview(t, KC, 256), wout_v[:, :, b * 256:(b + 1) * 256]))
                s = Stream(kb, wpool, loads)
                li = 0
                LG = [float(np.log1p(-2.0 ** (-5.0 - h))) for h in range(4)]

                def ret_chunk_tm(S, a, hd, is_s):
                    pst = psum.next()
                    kb.op("pe", lambda e: e.transpose(pst.t[:, 0:128], S.fk.t[:, a:a + 128], ident.t), reads=[S.fk.k, ident.k], writes=[pst.k], inc=False)
                    kb.op("pe", lambda e: e.transpose(pst.t[:, 128:256], S.vb.t[:, a:a + 128], ident.t), reads=[S.vb.k, ident.k], writes=[pst.k])
                    kv = kvtm.next()
                    kdc = CO2["ret_kd"] + (4 if is_s else 0) + hd
                    kb.op("dve", lambda e: e.tensor_scalar(out=kv.t[:, 0:128], in0=pst.t[:, 0:128], scalar1=co.t[:, kdc:kdc + 1], scalar2=None, op0=ALU.mult),
                          reads=[pst.k, co.k], writes=[kv.k])
                    kb.op("dve", lambda e: e.tensor_copy(out=kv.t[:, 128:256], in_=pst.t[:, 128:256]), reads=[pst.k], writes=[kv.k])
                    return kv

                def ret_att(S, a, hd, is_s):
                    psA = psum.next()
                    kb.op("pe", lambda e: e.matmul(psA.t[:, 0:128], S.fk.t[:, a:a + 128], S.qs.t[:, a:a + 128], start=True, stop=True),
                          reads=[S.fk.k, S.qs.k], writes=[psA.k])
                    am = Am.next()
                    mo = CO2["ret_dm_s" if is_s else "ret_dm_p"] + hd * 128
                    kb.op("dve", lambda e: e.tensor_tensor(out=am.t[:], in0=psA.t[:, 0:128], in1=co.t[:, mo:mo + 128], op=ALU.mult),
                          reads=[psA.k, co.k], writes=[am.k])
                    return am

                def ret_head(hd, S, n, is_s):
                    qo = CO2["ret_qd_s" if is_s else "ret_qd_p"] + hd * 128
                    nch = n // 128
                    kb.op("dve", lambda e: e.tensor_tensor(out=S.lf.t[:, :n].rearrange("p (c t) -> p c t", t=128),
                                                           in0=S.qs.t[:, :n].rearrange("p (c t) -> p c t", t=128),
                                                           in1=co.t[:, qo:qo + 128].unsqueeze(1).to_broadcast([128, nch, 128]), op=ALU.mult),
                          reads=[S.qs.k, co.k], writes=[S.lf.k])
                    psO = psacc.next()
                    if not is_s:
                        g128 = float(np.exp(LG[hd] * 128))
                        for ch in range(nch):
                            a = ch * 128
                            kv = ret_chunk_tm(S, a, hd, False)
                            am = ret_att(S, a, hd, False)
                            kb.op("pe", lambda e: e.matmul(psO.t[:, a:a + 128], Hret[hd].t[:], S.lf.t[:, a:a + 128], start=True, stop=False),
                                  reads=[Hret[hd].k, S.lf.k], writes=[psO.k], inc=False)
                            kb.op("pe", lambda e: e.matmul(psO.t[:, a:a + 128], kv.t[:, 128:256], am.t[:], start=False, stop=True),
                                  reads=[kv.k, am.k], writes=[psO.k])
                            psH = psum.next()
                            kb.op("pe", lambda e: e.matmul(psH.t[:, 0:128], kv.t[:, 0:128], kv.t[:, 128:256], start=True, stop=True),
                                  reads=[kv.k], writes=[psH.k])
                            kb.op("dve", lambda e: e.scalar_tensor_tensor(out=Hret[hd].t[:], in0=Hret[hd].t[:], scalar=g128, in1=psH.t[:, 0:128],
                                                                          op0=ALU.mult, op1=ALU.add), reads=[Hret[hd].k, psH.k], writes=[Hret[hd].k])
                        head_norm_out(psO.t[:, :n], psO.k, True, osbR, tmpbR, prm.t[:, PRM["rgn"] + hd:PRM["rgn"] + hd + 1], prm.k, S.sg,
                                      yT.t[:, hd, :n], yT.k, n)
                    else:
                        g8 = float(np.exp(LG[hd] * TS))
                        kv = ret_chunk_tm(S, 0, hd, True)
                        am = ret_att(S, 0, hd, True)
                        kb.op("pe", lambda e: e.matmul(psO.t[:, 0:128], kv.t[:, 128:256], am.t[:], start=True, stop=True),
                              reads=[kv.k, am.k], writes=[psO.k])
                        psI = psum.next()
                        for j in range(4):
                            hg_ = h0Tr.next()
                            kb.dma("sp", out=hg_.t[:], in_=state_ret[4 * j:4 * j + 4, hd].rearrange("s k v -> k s v"), writes=[hg_.k])
                            for q in range(4):
                                sq_ = 4 * j + q
                                kb.op("pe", lambda e: e.matmul(psI.t[:, sq_ * TS:(sq_ + 1) * TS], hg_.t[:, q, :], S.lf.t[:, sq_ * TS:(sq_ + 1) * TS],
                                                               start=True, stop=True), reads=[hg_.k, S.lf.k], writes=[psI.k], inc=(q == 3))
                            vx = vexp.next()
                            kb.op("dve", lambda e: e.tensor_tensor(out=vx.t[:], in0=kv.t[:, 128:256].unsqueeze(1).to_broadcast([128, 4, 128]),
                                                                   in1=cst.t[:, CO["seqmask"] + 4 * j:CO["seqmask"] + 4 * j + 4].unsqueeze(2).to_broadcast([128, 4, 128]),
                                                                   op=ALU.mult), reads=[kv.k, cst.k], writes=[vx.k])
                            psD = psum.next()
                            kb.op("pe", lambda e: e.matmul(psD.t[:, :], kv.t[:, 0:128], vx.t[:].rearrange("p s v -> p (s v)"), start=True, stop=True),
                                  reads=[kv.k, vx.k], writes=[psD.k])
                            hb = hn.next()
                            kb.op("dve", lambda e: e.scalar_tensor_tensor(out=hb.t[:], in0=hg_.t[:], scalar=g8,
                                                                          in1=psD.t[:, :].rearrange("p (s v) -> p s v", s=4), op0=ALU.mult, op1=ALU.add),
                                  reads=[hg_.k, psD.k], writes=[hb.k])
                            kb.dma("sp", out=ret_s[4 * j:4 * j + 4, hd].rearrange("s k v -> k s v"), in_=hb.t[:], reads=[hb.k])
                        kb.op("act", lambda e: e.copy(out=tmpb.t[:, :n], in_=psI.t[:, :n]), reads=[psI.k], writes=[tmpb.k])
                        kb.op("dve", lambda e: e.tensor_tensor(out=osb.t[:, :n], in0=psO.t[:, :n], in1=tmpb.t[:, :n], op=ALU.add),
                              reads=[psO.k, tmpb.k], writes=[osb.k])
                        head_norm_out(osb.t[:, :n], osb.k, False, osb, tmpb, prm.t[:, PRM["rgn"] + hd:PRM["rgn"] + hd + 1], prm.k, S.sg,
                                      yT.t[:, hd, :n], yT.k, n)

                def rope(S_buf, ps, n, scale):
                    kb.op("act", lambda e: e.activation(out=tmpb.t[:, :n], in_=ps.t[:, :n], func=AF.Copy, scale=scale), reads=[ps.k], writes=[tmpb.k])
                    pr_ = psum.next()
                    kb.op("pe", lambda e: e.matmul(pr_.t[:, :n], co.t[:, CO2["perm"]:CO2["perm"] + 128], tmpb.t[:, :n], start=True, stop=True),
                          reads=[co.k, tmpb.k], writes=[pr_.k])
                    kb.op("dve", lambda e: e.tensor_tensor(out=S_buf.t[:, :n], in0=pr_.t[:, :n], in1=sinb.t[:, :n], op=ALU.mult),
                          reads=[pr_.k, sinb.k], writes=[S_buf.k])
                    kb.op("dve", lambda e: e.tensor_tensor(out=tmpb.t[:, :n], in0=tmpb.t[:, :n], in1=cosb.t[:, :n], op=ALU.mult),
                          reads=[tmpb.k, cosb.k], writes=[tmpb.k])
                    kb.op("dve", lambda e: e.tensor_tensor(out=S_buf.t[:, :n], in0=S_buf.t[:, :n], in1=tmpb.t[:, :n], op=ALU.add),
                          reads=[S_buf.k, tmpb.k], writes=[S_buf.k])

                def m2(ap):
                    return ap.rearrange("p (h t) -> p h t", h=2)

                def bc2(ap):
                    return ap.unsqueeze(1).to_broadcast([128, 2, 128])

                def rw_gram(dst, lhs_full, rhsA, rhsB, a, mask_ap, swap=False):
                    ps = psum.next()
                    for h, rh in enumerate((rhsA, rhsB)):
                        if swap:
                            l_, r_ = rh, lhs_full
                        else:
                            l_, r_ = lhs_full, rh
                        kb.op("pe", lambda e: e.matmul(ps.t[:, h * 128:(h + 1) * 128], l_.t[:, a:a + 128], r_.t[:, a:a + 128], start=True, stop=True),
                              reads=[l_.k, r_.k], writes=[ps.k], inc=(h == 1))
                    kb.op("dve", lambda e: e.tensor_tensor(out=m2(dst.t[:, :]), in0=m2(ps.t[:, 0:256]), in1=bc2(mask_ap), op=ALU.mult),
                          reads=[ps.k, co.k, cst.k], writes=[dst.k])

                def rw_tm(a, X):
                    ext = X.ext
                    pst = psum.next()
                    kb.op("pe", lambda e: e.transpose(pst.t[:, 0:128], R["k"].t[:, a:a + 128], ident.t), reads=[R["k"].k, ident.k], writes=[pst.k], inc=False)
                    kb.op("pe", lambda e: e.transpose(pst.t[:, 128:256], R["bt"].t[:, a:a + 128], ident.t), reads=[R["bt"].k, ident.k], writes=[pst.k], inc=False)
                    kb.op("pe", lambda e: e.transpose(pst.t[:, 256:384], R["v"].t[:, a:a + 128], ident.t), reads=[R["v"].k, ident.k], writes=[pst.k])
                    for i, nm in enumerate(("K", "B", "V")):
                        kb.op("act", lambda e: e.copy(out=ext[nm + "A"].t[:, 0:64], in_=pst.t[:, i * 128:i * 128 + 64]), reads=[pst.k], writes=[ext[nm + "A"].k])
                        kb.op("act", lambda e: e.copy(out=ext[nm + "B"].t[:, 64:128], in_=pst.t[:, i * 128 + 64:i * 128 + 128]), reads=[pst.k], writes=[ext[nm + "B"].k])

                def rw_pre(As, is_s):
                    mT = co.t[:, CO2["bstrictT" if is_s else "strictT"]:][:, 0:128]
                    mL = co.t[:, CO2["bstrictL" if is_s else "strictL"]:][:, 0:128]
                    mC = cst.t[:, CO["blkmask" if is_s else "causalT"]:][:, 0:128]
                    cur = []
                    for u, a in enumerate(As):
                        X = UX[u]
                        rw_tm(a, X)
                        MT = X.MT.next()
                        M = X.M.next()
                        rw_gram(MT, R["bt"], R["atA"], R["atB"], a, mT)
                        rw_gram(M, R["bt"], R["atA"], R["atB"], a, mL, swap=True)
                        rw_gram(X.LakT, R["k"], R["atA"], R["atB"], a, mT)
                        rw_gram(X.ArbT, R["bt"], R["rA"], R["rB"], a, mC)
                        rw_gram(X.ArkT, R["k"], R["rA"], R["rB"], a, mC)
                        cur.append([M, MT, None])
                    nlev = 2 if is_s else 6
                    for u in range(len(As)):
                        X = UX[u]
                        M, MT, _ = cur[u]
                        TT = X.TTb.next() if nlev > 0 else X.TT
                        kb.op("dve", lambda e: e.tensor_tensor(out=m2(TT.t[:, :]), in0=m2(MT.t[:, :]), in1=bc2(identb.t[:]), op=ALU.add),
                              reads=[MT.k, identb.k], writes=[TT.k])
                        cur[u][2] = TT
                    for lev in range(nlev):
                        pp = []
                        for u in range(len(As)):
                            M, MT, TT = cur[u]
                            psa = psum.next()
                            psb = psum.next()
                            for h in range(2):
                                hs_ = slice(h * 128, (h + 1) * 128)
                                kb.op("pe", lambda e: e.matmul(psa.t[:, hs_], MT.t[:, hs_], M.t[:, hs_], start=True, stop=True),
                                      reads=[MT.k, M.k], writes=[psa.k], inc=(h == 1))
                            for h in range(2):
                                hs_ = slice(h * 128, (h + 1) * 128)
                                kb.op("pe", lambda e: e.matmul(psb.t[:, hs_], M.t[:, hs_], MT.t[:, hs_], start=True, stop=True),
                                      reads=[MT.k, M.k], writes=[psb.k], inc=(h == 1))
                            pp.append((psa, psb))
                        for u in range(len(As)):
                            X = UX[u]
                            psa, psb = pp[u]
                            M2 = X.M.next()
                            MT2 = X.MT.next()
                            kb.op("act", lambda e: e.copy(out=M2.t[:, :], in_=psa.t[:, 0:256]), reads=[psa.k], writes=[M2.k])
                            kb.op("dve", lambda e: e.tensor_copy(out=MT2.t[:, :], in_=psb.t[:, 0:256]), reads=[psb.k], writes=[MT2.k])
                            cur[u][0], cur[u][1] = M2, MT2
                        pq = []
                        for u in range(len(As)):
                            M, MT, TT = cur[u]
                            pst_ = psum.next()
                            for h in range(2):
                                hs_ = slice(h * 128, (h + 1) * 128)
                                kb.op("pe", lambda e: e.matmul(pst_.t[:, hs_], M.t[:, hs_], TT.t[:, hs_], start=True, stop=True),
                                      reads=[M.k, TT.k], writes=[pst_.k], inc=(h == 1))
                            pq.append(pst_)
                        for u in range(len(As)):
                            X = UX[u]
                            M, MT, TT = cur[u]
                            TT2 = X.TT if lev == nlev - 1 else X.TTb.next()
                            kb.op("dve", lambda e: e.tensor_tensor(out=TT2.t[:, :], in0=TT.t[:, :], in1=pq[u].t[:, 0:256], op=ALU.add),
                                  reads=[TT.k, pq[u].k], writes=[TT2.k])
                            cur[u][2] = TT2
                    return [UX[u] for u in range(len(As))]

                def rw_solve_u(X, psR):
                    kb.op("act", lambda e: e.copy(out=Rsb.t[:], in_=psR.t[:, 0:128]), reads=[psR.k], writes=[Rsb.k])
                    psU = psum.next()
                    for h in range(2):
                        kb.op("pe", lambda e: e.matmul(psU.t[:, h * 64:(h + 1) * 64], X.TT.t[:, h * 128:(h + 1) * 128], Rsb.t[:, h * 64:(h + 1) * 64],
                                                       start=True, stop=True), reads=[X.TT.k, Rsb.k], writes=[psU.k], inc=(h == 1))
                    kb.op("dve", lambda e: e.tensor_copy(out=extU["UA"].t[:, 0:64], in_=psU.t[:, 0:64]), reads=[psU.k], writes=[extU["UA"].k])
                    kb.op("dve", lambda e: e.tensor_copy(out=extU["UB"].t[:, 64:128], in_=psU.t[:, 64:128]), reads=[psU.k], writes=[extU["UB"].k])

                def rw_o_intra(psO, a, first, X):
                    seq_ = [(extU["UA"], X.ArbT, 0), (extU["UB"], X.ArbT, 1), (X.ext["VA"], X.ArkT, 0), (X.ext["VB"], X.ArkT, 1)]
                    for i, (eb, mat, h) in enumerate(seq_):
                        kb.op("pe", lambda e: e.matmul(psO.t[:, a:a + 128], eb.t[:], mat.t[:, h * 128:(h + 1) * 128],
                                                       start=(first and i == 0), stop=(i == 3)),
                              reads=[eb.k, mat.k], writes=[psO.k], inc=(i == 3))

                def rw_prompt(pr, n, psO):
                    nch = n // 128
                    units = rw_pre([ch * 128 for ch in range(nch)], False)
                    for ch in range(nch):
                        a = ch * 128
                        X = units[ch]
                        ext = X.ext
                        hh = Hh.next()
                        kb.op("dve", lambda e: e.tensor_scalar(out=hh.t[:], in0=Hblk[pr].t[:], scalar1=ebm.t[:, ch:ch + 1], scalar2=None, op0=ALU.mult),
                              reads=[Hblk[pr].k, ebm.k], writes=[hh.k])
                        psR = psum.next()
                        for h, at_ in enumerate((R["atA"], R["atB"])):
                            kb.op("pe", lambda e: e.matmul(psR.t[:, h * 64:(h + 1) * 64], at_.t[:, a:a + 128], hh.t[:, h * 64:(h + 1) * 64], start=True, stop=False),
                                  reads=[at_.k, hh.k], writes=[psR.k], inc=False)
                            vx = ext["VA" if h == 0 else "VB"]
                            kb.op("pe", lambda e: e.matmul(psR.t[:, h * 64:(h + 1) * 64], X.LakT.t[:, h * 128:(h + 1) * 128], vx.t[:, h * 64:(h + 1) * 64],
                                                           start=False, stop=True), reads=[X.LakT.k, vx.k], writes=[psR.k], inc=(h == 1))
                        rw_solve_u(X, psR)
                        kb.op("pe", lambda e: e.matmul(psO.t[:, a:a + 128], hh.t[:], R["r"].t[:, a:a + 128], start=True, stop=False),
                              reads=[hh.k, R["r"].k], writes=[psO.k], inc=False)
                        rw_o_intra(psO, a, False, X)
                        psH = psum.next()
                        seq_ = [(ext["BA"], extU["UA"]), (ext["BB"], extU["UB"]), (ext["KA"], ext["VA"]), (ext["KB"], ext["VB"])]
                        for i, (l_, r_) in enumerate(seq_):
                            kb.op("pe", lambda e: e.matmul(psH.t[:, 0:128], l_.t[:], r_.t[:], start=(i == 0), stop=(i == 3)),
                                  reads=[l_.k, r_.k], writes=[psH.k], inc=(i == 3))
                        hs = Hs.next()
                        kb.op("dve", lambda e: e.tensor_tensor(out=hs.t[:], in0=psH.t[:, 0:128], in1=hh.t[:], op=ALU.add),
                              reads=[psH.k, hh.k], writes=[hs.k])
                        kb.op("dve", lambda e: e.tensor_scalar(out=Hblk[pr].t[:], in0=hs.t[:], scalar1=R["e1"].t[:, a + 127:a + 128], scalar2=None, op0=ALU.mult),
                              reads=[hs.k, R["e1"].k], writes=[Hblk[pr].k])

                def rw_sample(pr, psO):
                    n = 128
                    X = rw_pre([0], True)[0]
                    ext = dict(X.ext)
                    ext.update(extU)
                    LakT = X.LakT
                    psX = psum.next()
                    psI = psum.next()
                    h0gs = []
                    for j in range(4):
                        sg_ = h0r.next()
                        hg_ = h0Tr.next()
                        kb.op("dve", lambda e: e.memset(sg_.t[:], 0.0), writes=[sg_.k])
                        for h in range(2):
                            kb.dma("sp", out=sg_.t[h * 64:(h + 1) * 64, :, h * 64:(h + 1) * 64],
                                   in_=state_rwkv[4 * j:4 * j + 4, 2 * pr + h].rearrange("s v k -> v s k"), writes=[sg_.k])
                        pst = psum.next()
                        for q in range(4):
                            kb.op("pe", lambda e: e.transpose(pst.t[:, q * 128:(q + 1) * 128], sg_.t[:, q, :], ident.t), reads=[sg_.k, ident.k], writes=[pst.k], inc=(q == 3))
                        evac_copy(hg_.t[:].rearrange("p s v -> p (s v)"), pst.t[:, :], [pst.k], [hg_.k])
                        h0gs.append(hg_)
                        for q in range(4):
                            sq_ = 4 * j + q
                            cs = slice(sq_ * TS, (sq_ + 1) * TS)
                            kb.op("pe", lambda e: e.matmul(psX.t[:, cs], hg_.t[:, q, :], R["at"].t[:, cs], start=True, stop=True),
                                  reads=[hg_.k, R["at"].k], writes=[psX.k], inc=False)
                            kb.op("pe", lambda e: e.matmul(psI.t[:, cs], hg_.t[:, q, :], R["r"].t[:, cs], start=True, stop=True),
                                  reads=[hg_.k, R["r"].k], writes=[psI.k], inc=(q == 3))
                    kb.op("act", lambda e: e.copy(out=xTs.t[:], in_=psX.t[:, 0:128]), reads=[psX.k], writes=[xTs.k])
                    kb.op("act", lambda e: e.copy(out=tmpb.t[:, :n], in_=psI.t[:, :n]), reads=[psI.k], writes=[tmpb.k])
                    psR = psum.next()
                    kb.op("pe", lambda e: e.matmul(psR.t[:, 0:128], xTs.t[:], ident.t, start=True, stop=False), reads=[xTs.k, ident.k], writes=[psR.k], inc=False)
                    for h in range(2):
                        vx = ext["VA" if h == 0 else "VB"]
                        kb.op("pe", lambda e: e.matmul(psR.t[:, h * 64:(h + 1) * 64], LakT.t[:, h * 128:(h + 1) * 128], vx.t[:, h * 64:(h + 1) * 64],
                                                       start=False, stop=(h == 1)), reads=[LakT.k, vx.k], writes=[psR.k], inc=(h == 1))
                    rw_solve_u(X, psR)
                    rw_o_intra(psO, 0, True, X)
                    e1l = R["e1"].t[:, :n].rearrange("p (s t) -> p s t", t=TS)
                    for j in range(4):
                        sm = cst.t[:, CO["seqmask"] + 4 * j:CO["seqmask"] + 4 * j + 4].unsqueeze(2).to_broadcast([128, 4, 128])
                        psD = psum.next()
                        seq_ = [("BA", "UA"), ("BB", "UB"), ("KA", "VA"), ("KB", "VB")]
                        for i, (l_, r_) in enumerate(seq_):
                            vx = vexp.next()
                            kb.op("dve", lambda e: e.tensor_tensor(out=vx.t[:], in0=ext[r_].t[:].unsqueeze(1).to_broadcast([128, 4, 128]), in1=sm, op=ALU.mult),
                                  reads=[ext[r_].k, cst.k], writes=[vx.k])
                            kb.op("pe", lambda e: e.matmul(psD.t[:, :], ext[l_].t[:], vx.t[:].rearrange("p s v -> p (s v)"), start=(i == 0), stop=(i == 3)),
                                  reads=[ext[l_].k, vx.k], writes=[psD.k], inc=True)
                        hb = hn.next()
                        kb.op("dve", lambda e: e.tensor_tensor(out=hb.t[:], in0=psD.t[:, :].rearrange("p (s v) -> p s v", s=4), in1=h0gs[j].t[:], op=ALU.add),
                              reads=[psD.k, h0gs[j].k], writes=[hb.k])
                        kb.op("dve", lambda e: e.tensor_tensor(out=hb.t[:], in0=hb.t[:], in1=e1l[:, 4 * j:4 * j + 4, TS - 1].unsqueeze(2).to_broadcast([128, 4, 128]),
                                                               op=ALU.mult), reads=[hb.k, R["e1"].k], writes=[hb.k])
                        pst = psum.next()
                        for q in range(4):
                            kb.op("pe", lambda e: e.transpose(pst.t[:, q * 128:(q + 1) * 128], hb.t[:, q, :], ident.t), reads=[hb.k, ident.k], writes=[pst.k], inc=(q == 3))
                        so = h0r.next()
                        evac_copy(so.t[:].rearrange("p s v -> p (s v)"), pst.t[:, :], [pst.k], [so.k])
                        for h in range(2):
                            kb.dma("sp", out=rwkv_s[4 * j:4 * j + 4, 2 * pr + h].rearrange("s v k -> v s k"),
                                   in_=so.t[h * 64:(h + 1) * 64, :, h * 64:(h + 1) * 64], reads=[so.k])

                SUB = [(c0, c0 + 256) for c0 in range(0, SEQ, 256)] + [(SEQ, NTOK)]
                for c0, c1 in SUB:
                    n = c1 - c0
                    is_s = c0 >= SEQ
                    ti = tile_of(c0)
                    rmsnorm_g(lambda kc: (xT[:, kc, c0:c1], xk[kc][ti]), gi, lambda kc: hT.t[:, kc, :n], hT.k, n)
                    kb.dma("sp", out=cosb.t[:, :n], in_=rope_d[0, :, c0:c1], writes=[cosb.k])
                    kb.dma("sp", out=sinb.t[:, :n], in_=rope_d[1, :, c0:c1], writes=[sinb.k])
                    psO_rw = None
                    for (bc0, wd) in blocks:
                        w = s.get(li)
                        li += 1
                        wv = wview(w.t, KC, wd)
                        for jj in range(wd // 128):
                            chunk = bc0 // 128 + jj
                            ps = psum.next()
                            for kc in range(KC):
                                kb.op("pe", lambda e: e.matmul(ps.t[:, :n], wv[:, kc, jj * 128:(jj + 1) * 128], hT.t[:, kc, :n],
                                                               start=(kc == 0), stop=(kc == KC - 1)),
                                      reads=[w.k, hT.k], writes=[ps.k], inc=(kc == KC - 1))
                            if chunk < 16:
                                S = sets[jj]
                                role = chunk // 4
                                if role == 0:
                                    rope(S.qs, ps, n, 1.0)
                                elif role == 1:
                                    rope(S.fk, ps, n, float(128 ** -0.5))
                                elif role == 2:
                                    kb.op("act", lambda e: e.copy(out=S.vb.t[:, :n], in_=ps.t[:, :n]), reads=[ps.k], writes=[S.vb.k])
                                else:
                                    kb.op("act", lambda e: e.activation(out=S.sg.t[:, :n], in_=ps.t[:, :n], func=AF.Silu), reads=[ps.k], writes=[S.sg.k])
                                continue
                            pc = chunk - 16
                            if is_s:
                                raw = pdraw.t[:, 0:NS * (TS + 1)].rearrange("p (s t) -> p s t", t=TS + 1)
                                kb.op("dve", lambda e: e.tensor_copy(out=raw[:, :, 0], in_=shs.t[:, pc, :]), reads=[shs.k], writes=[pdraw.k])
                                kb.op("act", lambda e: e.copy(out=raw[:, :, 1:TS + 1], in_=ps.t[:, :n].rearrange("p (s t) -> p s t", t=TS)),
                                      reads=[ps.k], writes=[pdraw.k])
                                cur, prv = raw[:, :, 1:TS + 1], raw[:, :, 0:TS]
                                v3 = lambda ap: ap.rearrange("p (s t) -> p s t", t=TS)
                                kb.op("dve", lambda e: e.tensor_copy(out=shs.t[:, pc, :], in_=raw[:, :, TS]), reads=[pdraw.k], writes=[shs.k])
                            else:
                                kb.op("dve", lambda e: e.tensor_copy(out=pdraw.t[:, 0:1], in_=carry.t[:, pc:pc + 1]), reads=[carry.k], writes=[pdraw.k])
                                kb.op("act", lambda e: e.copy(out=pdraw.t[:, 1:n + 1], in_=ps.t[:, :n]), reads=[ps.k], writes=[pdraw.k])
                                cur, prv = pdraw.t[:, 1:n + 1], pdraw.t[:, 0:n]
                                v3 = lambda ap: ap
                                kb.op("dve", lambda e: e.tensor_copy(out=carry.t[:, pc:pc + 1], in_=pdraw.t[:, n:n + 1]), reads=[pdraw.k], writes=[carry.k])
                            mu_c = prm.t[:, PRM["mu"] + pc:PRM["mu"] + pc + 1]
                            dstb = {0: "r", 1: "k", 2: "v"}.get(pc // 4) if pc < 12 else None
                            dst = R[dstb] if dstb else (p28 if pc == 12 else p29)
                            kb.op("dve", lambda e: e.tensor_scalar(out=v3(dst.t[:, :n]), in0=cur, scalar1=omu.t[:, pc:pc + 1], scalar2=None, op0=ALU.mult),
                                  reads=[pdraw.k, omu.k], writes=[dst.k])
                            kb.op("dve", lambda e: e.scalar_tensor_tensor(out=v3(dst.t[:, :n]), in0=prv, scalar=mu_c, in1=v3(dst.t[:, :n]), op0=ALU.mult, op1=ALU.add),
                                  reads=[pdraw.k, prm.k, dst.k], writes=[dst.k])
                            if pc == 12:
                                kb.op("act", lambda e: e.activation(out=p28.t[0:64, :n], in_=p28.t[0:64, :n], func=AF.Tanh), reads=[p28.k], writes=[p28.k])
                            elif pc == 13:
                                kb.op("act", lambda e: e.activation(out=p29.t[:, :n], in_=p29.t[:, :n], func=AF.Sigmoid), reads=[p29.k], writes=[p29.k])
                            if pc >= 12 or pc // 4 != 2:
                                continue
                            pr = pc % 4

                            def do_pair(pr=pr, n=n, is_s=is_s):
                                csl = slice(pr * 128, (pr + 1) * 128)
                                psw = psum.next()
                                kb.op("pe", lambda e: e.matmul(psw.t[:, :n], w2a2.t[0:64, csl], p28.t[0:64, :n], start=True, stop=True),
                                      reads=[w2a2.k, p28.k], writes=[psw.k])
                                kb.op("act", lambda e: e.activation(out=R["lw"].t[:, :n], in_=psw.t[:, :n], func=AF.Sigmoid,
                                                                    bias=prm.t[:, PRM["w0"] + pr:PRM["w0"] + pr + 1], scale=1.0),
                                      reads=[psw.k, prm.k], writes=[R["lw"].k])
                                psa_ = psum.next()
                                kb.op("pe", lambda e: e.matmul(psa_.t[:, :n], w2a2.t[64:128, csl], p28.t[64:128, :n], start=True, stop=True),
                                      reads=[w2a2.k, p28.k], writes=[psa_.k])
                                kb.op("act", lambda e: e.activation(out=R["a"].t[:, :n], in_=psa_.t[:, :n], func=AF.Sigmoid,
                                                                    bias=prm.t[:, PRM["a0"] + pr:PRM["a0"] + pr + 1], scale=1.0),
                                      reads=[psa_.k, prm.k], writes=[R["a"].k])
                                kb.op("dve", lambda e: e.tensor_scalar(out=R["kk"].t[:, :n], in0=R["k"].t[:, :n], scalar1=prm.t[:, PRM["kk"] + pr:PRM["kk"] + pr + 1],
                                                                       scalar2=None, op0=ALU.mult), reads=[R["k"].k, prm.k], writes=[R["kk"].k])
                                kb.op("act", lambda e: e.activation(out=tmpb.t[:, :n], in_=R["kk"].t[:, :n], func=AF.Square), reads=[R["kk"].k], writes=[tmpb.k])
                                psn_ = psum.next()
                                kb.op("pe", lambda e: e.matmul(psn_.t[:, :n], co.t[:, CO2["blk1"]:CO2["blk1"] + 128], tmpb.t[:, :n], start=True, stop=True),
                                      reads=[co.k, tmpb.k], writes=[psn_.k])
                                kb.op("act", lambda e: e.activation(out=tmpb.t[:, :n], in_=psn_.t[:, :n], func=AF.Sqrt), reads=[psn_.k], writes=[tmpb.k])
                                kb.op("dve", lambda e: e.tensor_scalar(out=tmpb.t[:, :n], in0=tmpb.t[:, :n], scalar1=1e-12, scalar2=None, op0=ALU.max),
                                      reads=[tmpb.k], writes=[tmpb.k])
                                kb.op("dve", lambda e: e.reciprocal(out=tmpb.t[:, :n], in_=tmpb.t[:, :n]), reads=[tmpb.k], writes=[tmpb.k])
                                kb.op("dve", lambda e: e.tensor_tensor(out=R["kk"].t[:, :n], in0=R["kk"].t[:, :n], in1=tmpb.t[:, :n], op=ALU.mult),
                                      reads=[R["kk"].k, tmpb.k], writes=[R["kk"].k])
                                kb.op("dve", lambda e: e.tensor_scalar(out=tmpb.t[:, :n], in0=R["a"].t[:, :n], scalar1=-1.0, scalar2=prm.t[:, PRM["ka"] + pr:PRM["ka"] + pr + 1],
                                                                       op0=ALU.add, op1=ALU.mult), reads=[R["a"].k, prm.k], writes=[tmpb.k])
                                kb.op("dve", lambda e: e.scalar_tensor_tensor(out=R["k"].t[:, :n], in0=tmpb.t[:, :n], scalar=1.0, in1=R["k"].t[:, :n], op0=ALU.add, op1=ALU.mult),
                                      reads=[tmpb.k, R["k"].k], writes=[R["k"].k])
                                kb.op("dve", lambda e: e.scalar_tensor_tensor(out=tmpb.t[:, :n], in0=R["r"].t[:, :n], scalar=prm.t[:, PRM["rk"] + pr:PRM["rk"] + pr + 1],
                                                                              in1=R["k"].t[:, :n], op0=ALU.mult, op1=ALU.mult), reads=[R["r"].k, R["k"].k, prm.k], writes=[tmpb.k])
                                psb_ = psum.next()
                                kb.op("pe", lambda e: e.matmul(psb_.t[:, :n], co.t[:, CO2["blk1"]:CO2["blk1"] + 128], tmpb.t[:, :n], start=True, stop=True),
                                      reads=[co.k, tmpb.k], writes=[psb_.k])
                                kb.op("dve", lambda e: e.tensor_tensor(out=R["bon"].t[:, :n], in0=psb_.t[:, :n], in1=R["v"].t[:, :n], op=ALU.mult),
                                      reads=[psb_.k, R["v"].k], writes=[R["bon"].k])
                                kb.op("dve", lambda e: e.tensor_scalar(out=R["lw"].t[:, :n], in0=R["lw"].t[:, :n], scalar1=-WDEC, scalar2=None, op0=ALU.mult),
                                      reads=[R["lw"].k], writes=[R["lw"].k])
                                rm_off = CO["rmask_s"] if is_s else CO["rmask_p"]
                                bbuf = R["e1"]
                                kb.op("dve", lambda e: e.tensor_tensor_scan(out=osb.t[:, :n], data0=cst.t[:, rm_off:rm_off + n], data1=R["lw"].t[:, :n],
                                                                            initial=0.0, op0=ALU.mult, op1=ALU.add), reads=[cst.k, R["lw"].k], writes=[osb.k])
                                if not is_s:
                                    nch = n // 128
                                    b3 = osb.t[:, :n].rearrange("p (c t) -> p c t", t=128)
                                    kb.op("dve", lambda e: e.tensor_copy(out=bm.t[:, 0:nch], in_=b3[:, :, 63]), reads=[osb.k], writes=[bm.k])
                                    kb.op("dve", lambda e: e.tensor_tensor(out=b3, in0=b3, in1=bm.t[:, 0:nch].unsqueeze(2).to_broadcast([128, nch, 128]), op=ALU.subtract),
                                          reads=[osb.k, bm.k], writes=[osb.k])
                                    kb.op("act", lambda e: e.activation(out=ebm.t[:, 0:nch], in_=bm.t[:, 0:nch], func=AF.Exp), reads=[bm.k], writes=[ebm.k])
                                kb.op("act", lambda e: e.activation(out=R["e1"].t[:, :n], in_=osb.t[:, :n], func=AF.Exp), reads=[osb.k], writes=[R["e1"].k])
                                kb.op("act", lambda e: e.activation(out=tmpb.t[:, :n], in_=osb.t[:, :n], func=AF.Exp, scale=-1.0), reads=[osb.k], writes=[tmpb.k])
                                kb.op("dve", lambda e: e.tensor_tensor(out=R["lw"].t[:, :n], in0=osb.t[:, :n], in1=R["lw"].t[:, :n], op=ALU.subtract),
                                      reads=[osb.k, R["lw"].k], writes=[R["lw"].k])
                                kb.op("act", lambda e: e.activation(out=R["lw"].t[:, :n], in_=R["lw"].t[:, :n], func=AF.Exp), reads=[R["lw"].k], writes=[R["lw"].k])
                                kb.op("dve", lambda e: e.tensor_tensor(out=R["r"].t[:, :n], in0=R["r"].t[:, :n], in1=R["e1"].t[:, :n], op=ALU.mult),
                                      reads=[R["r"].k, R["e1"].k], writes=[R["r"].k])
                                kb.op("dve", lambda e: e.tensor_tensor(out=R["k"].t[:, :n], in0=R["k"].t[:, :n], in1=tmpb.t[:, :n], op=ALU.mult),
                                      reads=[R["k"].k, tmpb.k], writes=[R["k"].k])
                                kb.op("dve", lambda e: e.tensor_tensor(out=R["bt"].t[:, :n], in0=R["kk"].t[:, :n], in1=R["a"].t[:, :n], op=ALU.mult),
                                      reads=[R["kk"].k, R["a"].k], writes=[R["bt"].k])
                                kb.op("dve", lambda e: e.tensor_tensor(out=R["bt"].t[:, :n], in0=R["bt"].t[:, :n], in1=tmpb.t[:, :n], op=ALU.mult),
                                      reads=[R["bt"].k, tmpb.k], writes=[R["bt"].k])
                                kb.op("dve", lambda e: e.scalar_tensor_tensor(out=R["at"].t[:, :n], in0=R["kk"].t[:, :n], scalar=-1.0, in1=R["lw"].t[:, :n], op0=ALU.mult, op1=ALU.mult),
                                      reads=[R["kk"].k, R["lw"].k], writes=[R["at"].k])
                                hmo = CO2["hmask"]
                                for nm, src, col in (("atA", "at", 0), ("atB", "at", 1), ("rA", "r", 0), ("rB", "r", 1)):
                                    kb.op("dve", lambda e: e.tensor_scalar(out=R[nm].t[:, :n], in0=R[src].t[:, :n], scalar1=co.t[:, hmo + col:hmo + col + 1], scalar2=None, op0=ALU.mult),
                                          reads=[R[src].k, co.k], writes=[R[nm].k])
                                psO = psacc.next()
                                if is_s:
                                    rw_sample(pr, psO)
                                    kb.op("dve", lambda e: e.tensor_tensor(out=osb.t[:, :n], in0=psO.t[:, :n], in1=tmpb.t[:, :n], op=ALU.add),
                                          reads=[psO.k, tmpb.k], writes=[osb.k])
                                else:
                                    rw_prompt(pr, n, psO)
                                    kb.op("act", lambda e: e.copy(out=osb.t[:, :n], in_=psO.t[:, :n]), reads=[psO.k], writes=[osb.k])
                                blk = co.t[:, CO2["blk1"]:CO2["blk1"] + 128]
                                psm = psum.next()
                                kb.op("pe", lambda e: e.matmul(psm.t[:, :n], blk, osb.t[:, :n], start=True, stop=True), reads=[co.k, osb.k], writes=[psm.k])
                                kb.op("dve", lambda e: e.scalar_tensor_tensor(out=osb.t[:, :n], in0=psm.t[:, :n], scalar=-1.0 / 64, in1=osb.t[:, :n], op0=ALU.mult, op1=ALU.add),
                                      reads=[psm.k, osb.k], writes=[osb.k])
                                kb.op("act", lambda e: e.activation(out=tmpb.t[:, :n], in_=osb.t[:, :n], func=AF.Square), reads=[osb.k], writes=[tmpb.k])
                                psv = psum.next()
                                kb.op("pe", lambda e: e.matmul(psv.t[:, :n], blk, tmpb.t[:, :n], start=True, stop=True), reads=[co.k, tmpb.k], writes=[psv.k])
                                kb.op("act", lambda e: e.activation(out=tmpb.t[:, :n], in_=psv.t[:, :n], func=AF.Ln, bias=gneps.t[:, 0:1], scale=1.0 / 64),
                                      reads=[psv.k, gneps.k], writes=[tmpb.k])
                                kb.op("act", lambda e: e.activation(out=tmpb.t[:, :n], in_=tmpb.t[:, :n], func=AF.Exp, scale=-0.5), reads=[tmpb.k], writes=[tmpb.k])
                                kb.op("dve", lambda e: e.tensor_tensor(out=osb.t[:, :n], in0=osb.t[:, :n], in1=tmpb.t[:, :n], op=ALU.mult),
                                      reads=[osb.k, tmpb.k], writes=[osb.k])
                                kb.op("dve", lambda e: e.tensor_scalar(out=osb.t[:, :n], in0=osb.t[:, :n], scalar1=prm.t[:, PRM["lg"] + pr:PRM["lg"] + pr + 1],
                                                                       scalar2=prm.t[:, PRM["lbb"] + pr:PRM["lbb"] + pr + 1], op0=ALU.mult, op1=ALU.add),
                                      reads=[osb.k, prm.k], writes=[osb.k])
                                kb.op("dve", lambda e: e.tensor_tensor(out=osb.t[:, :n], in0=osb.t[:, :n], in1=R["bon"].t[:, :n], op=ALU.add),
                                      reads=[osb.k, R["bon"].k], writes=[osb.k])
                                psg = psum.next()
                                kb.op("pe", lambda e: e.matmul(psg.t[:, :n], g2sb.t[:, csl], p29.t[:, :n], start=True, stop=True), reads=[g2sb.k, p29.k], writes=[psg.k])
                                kb.op("dve", lambda e: e.tensor_tensor(out=yT.t[:, 4 + pr, :n], in0=osb.t[:, :n], in1=psg.t[:, :n], op=ALU.mult),
                                      reads=[osb.k, psg.k], writes=[yT.k])

                            if is_s:
                                ret_head(pr, sets[pr % 2], n, True)
                                do_pair()
                            else:
                                kb.interleave([(lambda hd_=pr: ret_head(hd_, sets[hd_ % 2], n, False)), do_pair], quantum=3)
                    def radd(oc, ps):
                        kb.op("dve", lambda e: e.scalar_tensor_tensor(out=xT[:, oc, c0:c1], in0=ps.t[:, :n], scalar=1.0, in1=xT[:, oc, c0:c1], op0=ALU.mult, op1=ALU.add),
                              reads=[ps.k, xk[oc][ti]], writes=[xk[oc][ti]])
                    li = proj_fm(s, li, 4, 256, lambda kc: yT.t[:, kc, :n], yT.k, n, radd)
                    if c1 == SEQ:
                        for hd in range(4):
                            kb.dma("sp", out=ret_p[hd], in_=Hret[hd].t[:], reads=[Hret[hd].k])
                        for pr in range(4):
                            pst = psum.next()
                            kb.op("pe", lambda e: e.transpose(pst.t[:, 0:128], Hblk[pr].t[:], ident.t), reads=[Hblk[pr].k, ident.k], writes=[pst.k])
                            so = Hs.next()
                            kb.op("act", lambda e: e.copy(out=so.t[:], in_=pst.t[:, 0:128]), reads=[pst.k], writes=[so.k])
                            for h in range(2):
                                kb.dma("sp", out=rwkv_p[2 * pr + h], in_=so.t[h * 64:(h + 1) * 64, h * 64:(h + 1) * 64], reads=[so.k])
                        kb.dma("sp", out=shift_p.rearrange("(c p) -> p c", p=128), in_=carry.t[:, :], reads=[carry.k], allow_slow_non_contiguous=True)
                for c in range(14):
                    kb.dma("sp", out=shift_s[:, c * 128:(c + 1) * 128].rearrange("s p -> p s"), in_=shs.t[:, c, :], reads=[shs.k],
                           allow_slow_non_contiguous=True)
                kb.barrier()

        def phase_mix(l):
            if l == 0:
                phase_mix_even()
            else:
                phase_mix_odd()

        phase_load()
        for sg in stages:
            if sg == "final":
                continue
            name, l = sg.rsplit("_", 1)
            l = int(l)
            if name == "ffn1":
                phase_ffn(0, l)
            elif name == "ffn2":
                phase_ffn(1, l)
            elif name == "xattn":
                phase_xattn(l)
            elif name == "mix":
                phase_mix(l)
        phase_final("final" in stages)
        kb.finish()
        print("instructions:", kb.nins)
    return nc, declared


_CACHE = {}


def _get_nc(stages):
    if stages not in _CACHE:
        _CACHE[stages] = build(stages)
    return _CACHE[stages]


def make_in_maps(inp):
    hc = host_consts()
    f32 = lambda a: np.ascontiguousarray(a, dtype=np.float32)
    gains = f32(np.concatenate([inp["ffn1_norm"], inp["mix_norm"], inp["xattn_norm"], inp["ffn2_norm"],
                                inp["final_norm"][None, :], inp["mem_norm"]], axis=0))
    shared = {"consts": hc["consts"], "gains": gains}
    for f in (1, 2):
        for l in range(2):
            shared["ffn%d_w_gu_%d" % (f, l)] = f32(inp["ffn%d_w_gu" % f][l])
            shared["ffn%d_w_down_%d" % (f, l)] = f32(inp["ffn%d_w_down" % f][l])
    for l in range(2):
        shared["wq_%d" % l] = f32(inp["xattn_wq"][l])
        shared["wkv_%d" % l] = f32(inp["xattn_wkv"][l])
        shared["wo_%d" % l] = f32(inp["xattn_wo"][l])
    shared.update(host_consts_odd())
    shared["odd_w_in"] = f32(inp["odd_w_in"][0])
    shared["odd_w_out"] = f32(inp["odd_w_out"][0])
    for nm in ("ret_gnorm", "rwkv_mu", "rwkv_w0", "rwkv_w2", "rwkv_a0", "rwkv_a2", "rwkv_g2", "rwkv_k_k", "rwkv_k_a", "rwkv_lnx_g", "rwkv_lnx_b"):
        shared[nm] = f32(inp[nm][0])
    shared["rwkv_r_k"] = f32(inp["rwkv_r_k"][0].reshape(512))
    shared["even_w_in"] = f32(inp["even_w_in"][0])
    shared["even_w_out"] = f32(inp["even_w_out"][0])
    shared["conv_w"] = f32(inp["conv_w"][0])
    shared["hgrn_lb"] = f32(inp["hgrn_lb"])
    shared["hgrn_gnorm"] = f32(inp["hgrn_gnorm"][0])
    maps = []
    for c in range(NCORES):
        sl = slice(NS * c, NS * (c + 1))
        m = dict(shared)
        m["state_ret"] = f32(inp["state_ret"][0, sl])
        m["state_rwkv"] = f32(inp["state_rwkv"][0, sl])
        m["state_shift"] = f32(inp["state_shift"][0, sl])
        m["state_conv"] = f32(inp["state_conv"][0, sl])
        m["state_hgrn"] = f32(inp["state_hgrn"][0, sl])
        m["x_prompt"] = f32(inp["x_prompt"][c])
        m["x_sample"] = f32(inp["x_sample"][sl].reshape(NS * TS, D))
        m["mem_prompt"] = f32(inp["mem_prompt"][c])
        m["cache_k"] = f32(inp["cache_mem_k"][:, sl].reshape(2, NS, NMEM, D))
        m["cache_v"] = f32(inp["cache_mem_v"][:, sl].reshape(2, NS, NMEM, D))
        maps.append(m)
    return maps


def run(inp, stages=ALL_STAGES):
    nc, declared = _get_nc(tuple(stages))
    maps = [{k: m[k] for k in declared} for m in make_in_maps(inp)]
    res = run_bass_kernel_spmd(nc, maps, core_ids=list(range(NCORES)))
    return res.results


def kernel(**inp):
    inp = {k: np.asarray(v) for k, v in inp.items()}
    r = run(inp)
    cat = lambda name, shp: np.concatenate([np.asarray(r[c][name], np.float32).reshape(shp) for c in range(NCORES)], axis=0)
    y_prompt = cat("y_prompt", (1, SEQ, D))
    y_sample = cat("y_sample", (NS, TS, D))
    conv_p = cat("conv_p", (1, 2, 512))[None]
    hgrn_p = cat("hgrn_p", (1, 4, 128, 128))[None]
    ret_p = cat("ret_p", (1, 4, 128, 128))[None]
    rwkv_p = cat("rwkv_p", (1, 8, 64, 64))[None]
    shift_p = cat("shift_p", (1, 1792))[None]
    mem_k_p = np.stack([np.asarray(r[c]["mem_k_p"], np.float32).reshape(2, NMEM, 4, 256) for c in range(NCORES)], axis=1)
    mem_v_p = np.stack([np.asarray(r[c]["mem_v_p"], np.float32).reshape(2, NMEM, 4, 256) for c in range(NCORES)], axis=1)
    conv_s = cat("conv_s", (NS, 2, 512))[None]
    hgrn_s = cat("hgrn_s", (NS, 4, 128, 128))[None]
    ret_s = cat("ret_s", (NS, 4, 128, 128))[None]
    rwkv_s = cat("rwkv_s", (NS, 8, 64, 64))[None]
    shift_s = cat("shift_s", (NS, 1792))[None]
    return (y_prompt, y_sample, conv_p, hgrn_p, ret_p, rwkv_p, shift_p, mem_k_p, mem_v_p,
            conv_s, hgrn_s, ret_s, rwkv_s, shift_s)
```
